# Optimizing a Trainium2 kernel written in Bass

```python
import jax, jax.numpy as jnp
from jax import lax
import numpy as np

D_MODEL = 1024
BATCH = 8
SEQ = 2048
DEPTH = 2

HEAD_DIM = 64
ROT_DIM = HEAD_DIM // 4
ROPE_THETA = 500000.0
DIL_GROUPS = ((128, 1), (512, 4), (2048, 16))
N_GROUPS = len(DIL_GROUPS)
HEADS_PER_GROUP = 8
GROUP_WIDTH = HEADS_PER_GROUP * HEAD_DIM
BLOCK = 128
N_MEM = 256
MEM_HEADS = 4
MEM_WIDTH = MEM_HEADS * HEAD_DIM
CONV_WIDTH = D_MODEL
CONV_K = 3
N_MIXERS = 2
N_ATTN_LAYERS = (DEPTH + 1) // 2
N_CONV_LAYERS = DEPTH // 2
BRANCH_A = GROUP_WIDTH + MEM_WIDTH
BRANCH_B = CONV_WIDTH + MEM_WIDTH
IN_A = 3 * N_GROUPS * GROUP_WIDTH + MEM_WIDTH + BRANCH_A
IN_B = 3 * CONV_WIDTH + MEM_WIDTH + BRANCH_B
EPS = 1e-6

kernel_name = "hybrid_dilated_attn_shortconv_memory"


def rms_norm(x, g):
    xf = x.astype(jnp.float32)
    y = xf * lax.rsqrt(jnp.mean(xf * xf, axis=-1, keepdims=True) + EPS)
    return (y * g.astype(jnp.float32)).astype(x.dtype)


def partial_rope(t, positions):
    half = ROT_DIM // 2
    inv_freq = ROPE_THETA ** (-jnp.arange(half, dtype=jnp.float32) * (2.0 / ROT_DIM))
    ang = positions.astype(jnp.float32)[:, :, None] * inv_freq
    cos = jnp.cos(ang)[:, :, None, :]
    sin = jnp.sin(ang)[:, :, None, :]
    tr = t[..., :ROT_DIM].astype(jnp.float32)
    t1, t2 = tr[..., :half], tr[..., half:]
    rot = jnp.concatenate([t1 * cos - t2 * sin, t2 * cos + t1 * sin], axis=-1)
    return jnp.concatenate([rot.astype(t.dtype), t[..., ROT_DIM:]], axis=-1)


def dilated_window_attention(q, k, v, window, dilation):
    b, s, h, dh = q.shape
    n_stream = s // dilation
    span = window // dilation
    nb = -(-n_stream // BLOCK)
    lp = nb * BLOCK

    def to_streams(t):
        t = t.reshape(b, n_stream, dilation, h, dh).transpose(0, 2, 1, 3, 4)
        return jnp.pad(t, ((0, 0), (0, 0), (0, lp - n_stream), (0, 0), (0, 0)))

    def banded(t):
        t = jnp.pad(t, ((0, 0), (0, 0), (BLOCK, 0), (0, 0), (0, 0)))
        t = t.reshape(b, dilation, nb + 1, BLOCK, h, dh)
        return jnp.concatenate([t[:, :, :-1], t[:, :, 1:]], axis=3)

    qb = to_streams(q).reshape(b, dilation, nb, BLOCK, h, dh)
    kb = banded(to_streams(k))
    vb = banded(to_streams(v))

    scores = jnp.einsum('brnqhd,brnkhd->brnhqk', qb, kb).astype(jnp.float32) * (dh ** -0.5)
    qi = jnp.arange(BLOCK)[:, None]
    kj = jnp.arange(2 * BLOCK)[None, :]
    blk = jnp.arange(nb)[:, None, None]
    dist = qi + BLOCK - kj
    kpos = blk * BLOCK + kj - BLOCK
    valid = (dist >= 0) & (dist <= span) & (kpos >= 0)
    scores = jnp.where(valid[None, None, :, None], scores, -jnp.inf)
    lse = jax.nn.logsumexp(scores, axis=-1)
    p = jnp.exp(scores - lse[..., None])
    out = jnp.einsum('brnhqk,brnkhd->brnqhd', p.astype(vb.dtype), vb).astype(jnp.float32)

    out = out.reshape(b, dilation, lp, h, dh)[:, :, :n_stream]
    out = out.transpose(0, 2, 1, 3, 4).reshape(b, s, h, dh)
    lse = lse.transpose(0, 1, 2, 4, 3).reshape(b, dilation, lp, h)[:, :, :n_stream]
    lse = lse.transpose(0, 2, 1, 3).reshape(b, s, h)
    return out, lse


def memory_cross_attention(qm, mem_n, w_mem_kv):
    b, s, _ = qm.shape
    kv = mem_n @ w_mem_kv
    km, vm = jnp.split(kv, 2, axis=-1)
    km = km.reshape(b, N_MEM, MEM_HEADS, HEAD_DIM)
    vm = vm.reshape(b, N_MEM, MEM_HEADS, HEAD_DIM)
    qh = qm.reshape(b, s, MEM_HEADS, HEAD_DIM)
    scores = jnp.einsum('bshd,bmhd->bhsm', qh, km).astype(jnp.float32) * (HEAD_DIM ** -0.5)
    p = jax.nn.softmax(scores, axis=-1)
    out = jnp.einsum('bhsm,bmhd->bshd', p.astype(vm.dtype), vm)
    return out.reshape(b, s, MEM_WIDTH)


def dilated_attention_layer(hn, positions, mem_n, w_in, w_mem_kv, w_out):
    b, s, _ = hn.shape
    gw = N_GROUPS * GROUP_WIDTH
    proj = hn @ w_in
    q, k, v, qm, z = jnp.split(proj, [gw, 2 * gw, 3 * gw, 3 * gw + MEM_WIDTH], axis=-1)
    n_heads = N_GROUPS * HEADS_PER_GROUP
    q = partial_rope(q.reshape(b, s, n_heads, HEAD_DIM), positions)
    k = partial_rope(k.reshape(b, s, n_heads, HEAD_DIM), positions)
    v = v.reshape(b, s, n_heads, HEAD_DIM)
    outs, lses = [], []
    for g, (window, dilation) in enumerate(DIL_GROUPS):
        sl = slice(g * HEADS_PER_GROUP, (g + 1) * HEADS_PER_GROUP)
        o, l = dilated_window_attention(q[:, :, sl], k[:, :, sl], v[:, :, sl], window, dilation)
        outs.append(o)
        lses.append(l)
    wts = jax.nn.softmax(jnp.stack(lses, axis=0), axis=0)
    mix = jnp.sum(wts[..., None] * jnp.stack(outs, axis=0), axis=0)
    mix = mix.reshape(b, s, GROUP_WIDTH).astype(hn.dtype)
    mem_out = memory_cross_attention(qm, mem_n, w_mem_kv)
    y = jnp.concatenate([mix, mem_out], axis=-1) * jax.nn.silu(z)
    return y @ w_out


def short_conv_layer(hn, mem_n, w_in, conv_w, w_mem_kv, w_out):
    c = CONV_WIDTH
    proj = hn @ w_in
    bg, cg, u, qm, z = jnp.split(proj, [c, 2 * c, 3 * c, 3 * c + MEM_WIDTH], axis=-1)
    conv = lax.conv_general_dilated(
        cg * u, conv_w[:, None, :].astype(u.dtype),
        window_strides=(1,), padding=((CONV_K - 1, 0),),
        dimension_numbers=('NWC', 'WIO', 'NWC'), feature_group_count=c)
    mix = bg * conv
    mem_out = memory_cross_attention(qm, mem_n, w_mem_kv)
    y = jnp.concatenate([mix, mem_out], axis=-1) * jax.nn.silu(z)
    return y @ w_out


def setup_inputs(seed: int = 0) -> dict:
    key = jax.random.key(seed)
    ks = jax.random.split(key, 14)
    f32 = jnp.float32

    def nrm(k, shape, fan_in):
        return jax.random.normal(k, shape, f32) * (fan_in ** -0.5)

    x = jax.random.normal(ks[0], (BATCH, SEQ, D_MODEL), f32)
    mem = jax.random.normal(ks[1], (BATCH, N_MEM, D_MODEL), f32)
    offset = jax.random.randint(ks[2], (BATCH, 1), 0, 1024, dtype=jnp.int32)
    positions = offset + jnp.arange(SEQ, dtype=jnp.int32)[None, :]
    norm_g = 1.0 + 0.05 * jax.random.normal(ks[3], (DEPTH, D_MODEL), f32)
    mem_norm_g = 1.0 + 0.05 * jax.random.normal(ks[4], (DEPTH, D_MODEL), f32)
    w_mem_kv = nrm(ks[5], (DEPTH, D_MODEL, 2 * MEM_WIDTH), D_MODEL)
    attn_w_in = nrm(ks[6], (N_ATTN_LAYERS, D_MODEL, IN_A), D_MODEL)
    attn_w_out = nrm(ks[7], (N_ATTN_LAYERS, BRANCH_A, D_MODEL), BRANCH_A)
    conv_w_in = nrm(ks[8], (N_CONV_LAYERS, D_MODEL, IN_B), D_MODEL)
    conv_w = nrm(ks[9], (N_CONV_LAYERS, CONV_K, CONV_WIDTH), CONV_K)
    conv_w_out = nrm(ks[10], (N_CONV_LAYERS, BRANCH_B, D_MODEL), BRANCH_B)
    final_g = 1.0 + 0.05 * jax.random.normal(ks[11], (D_MODEL,), f32)
    return {"x": x, "mem": mem, "positions": positions, "norm_g": norm_g,
            "mem_norm_g": mem_norm_g, "w_mem_kv": w_mem_kv,
            "attn_w_in": attn_w_in, "attn_w_out": attn_w_out,
            "conv_w_in": conv_w_in, "conv_w": conv_w, "conv_w_out": conv_w_out,
            "final_g": final_g}


def reference(x, mem, positions, norm_g, mem_norm_g, w_mem_kv, attn_w_in, attn_w_out,
              conv_w_in, conv_w, conv_w_out, final_g):
    h = x
    for i in range(DEPTH):
        j = i // N_MIXERS
        hn = rms_norm(h, norm_g[i])
        mem_n = rms_norm(mem, mem_norm_g[i])
        if i % N_MIXERS == 0:
            delta = dilated_attention_layer(hn, positions, mem_n, attn_w_in[j],
                                            w_mem_kv[i], attn_w_out[j])
        else:
            delta = short_conv_layer(hn, mem_n, conv_w_in[j], conv_w[j],
                                     w_mem_kv[i], conv_w_out[j])
        h = h + delta
    return rms_norm(h, final_g)
```

```python
import os
import numpy as np
from contextlib import ExitStack
import concourse.bass as bass
import concourse.mybir as mybir
from concourse.bass_utils import run_bass_kernel_spmd

F32 = mybir.dt.float32
BF16 = mybir.dt.bfloat16
I32 = mybir.dt.int32
AF = mybir.ActivationFunctionType
ALU = mybir.AluOpType
AX = mybir.AxisListType

ENGS = ("pe", "act", "dve", "pool", "sp")

S_TOK = 2048
DM = 1024
NT = 16
N_WARM = 120
DIL = (1, 4, 16)
EPS = 1e-6
ROPE_THETA = 500000.0
PI = float(np.pi)


class _Op:
    __slots__ = ("eng", "emit", "waits", "signal", "idx", "chan", "vc")


class Sched:
    SAME_WIN = {"pe": 0, "act": 2, "dve": 2, "pool": 1 << 30, "sp": 0}

    def __init__(self):
        self.streams = {e: [] for e in ENGS}
        self.last_w = {}
        self.readers = {}
        self.clock = {e: {} for e in ENGS}
        self.chan_cnt = {}
        self.label = ''
        self.labels = {e: [] for e in ENGS}

    def add(self, eng, emit, reads=(), writes=(), chan=None):
        op = _Op()
        op.eng, op.emit, op.chan, op.signal = eng, emit, chan, False
        op.idx = len(self.streams[eng])
        deps = []
        for r in reads:
            t = self.last_w.get(r)
            if t is not None:
                deps.append(t)
        for w in writes:
            t = self.last_w.get(w)
            if t is not None:
                deps.append(t)
            deps.extend(self.readers.get(w, ()))
        clk = self.clock[eng]
        waits = []
        for t in deps:
            if t[0] == "e":
                _, E, k, vc = t
                if E == eng:
                    if op.idx - k <= self.SAME_WIN[eng] and clk.get(("self", E), -1) < k:
                        waits.append(("e", E, k))
                        clk[("self", E)] = k
                        self.streams[E][k].signal = True
                    continue
                if clk.get(E, -1) >= k:
                    continue
                waits.append(("e", E, k))
                self.streams[E][k].signal = True
            else:
                _, E, k, vc = t
                if clk.get(E, -1) >= k:
                    continue
                waits.append(("d", E, k))
            for kk, vv in vc.items():
                if clk.get(kk, -1) < vv:
                    clk[kk] = vv
            clk[E] = max(clk.get(E, -1), k)
        best = {}
        for w in waits:
            key = (w[0], w[1])
            if key not in best or best[key][2] < w[2]:
                best[key] = w
        op.waits = list(best.values())
        vc = {k: v for k, v in clk.items() if not isinstance(k, tuple)}
        if chan is None:
            vc[eng] = op.idx
            tok = ("e", eng, op.idx, vc)
        else:
            n = self.chan_cnt.get(chan, 0) + 1
            self.chan_cnt[chan] = n
            tok = ("d", chan, n, vc)
        self.streams[eng].append(op)
        self.labels[eng].append(self.label)
        for r in reads:
            self.readers.setdefault(r, []).append(tok)
        for w in writes:
            self.last_w[w] = tok
            self.readers[w] = []
        return tok

    def emit_all(self, block, sems, chan_sems, final_waits):
        rank = {}
        for e in ENGS:
            c = 0
            rk = {}
            for op in self.streams[e]:
                if op.signal and op.chan is None:
                    c += 1
                    rk[op.idx] = c
            rank[e] = rk

        def run(e, engine):
            for op in self.streams[e]:
                for w in op.waits:
                    if w[0] == "e":
                        engine.wait_ge(sems[w[1]], rank[w[1]][w[2]])
                    else:
                        engine.wait_ge(chan_sems[w[1]], 16 * w[2])
                ins = op.emit(engine)
                if op.chan is not None:
                    ins.then_inc(chan_sems[op.chan], 16)
                elif op.signal:
                    ins.then_inc(sems[e], 1)
            for (C, n) in final_waits.get(e, ()):
                engine.wait_ge(chan_sems[C], 16 * n)

        names = {"pe": "tensor", "act": "scalar", "dve": "vector", "pool": "gpsimd", "sp": "sync"}
        for e in ENGS:
            if not self.streams[e] and e not in final_waits:
                continue
            getattr(block, names[e])(lambda engine, e=e: run(e, engine))


def sl(start, n, step=1):
    return slice(start, start + (n - 1) * step + 1, step)


def tok_start(g, b):
    d = DIL[g]
    nb = NT // d
    r, n = divmod(b, nb)
    return n * 128 * d + r


class _Stop(Exception):
    pass


def build(mode="full", stop=None):
    do_l0 = mode in ("full", "l0")
    do_l1 = mode in ("full", "l1")
    nc = bass.Bass("TRN2", target_bir_lowering=False)

    def din(name, shape, dt=F32):
        return nc.dram_tensor(name, list(shape), dt, kind="ExternalInput").ap()

    x_d = din("x", [S_TOK, DM])
    mem_d = din("mem", [256, DM])
    ident_d = din("ident", [128, 128])
    cpk_d = din("cpk", [128, 112])
    wkv_d = din("wkv", [2, DM, 512])
    if do_l0:
        maskA_d = din("maskA", [128, 512])
        maskD_d = din("maskD", [128, 512])
        w0a_d = din("w0a", [12, DM, 384])
        w0z_d = din("w0z", [2, DM, 384])
        w0qm_d = din("w0qm", [DM, 256])
        wout0_d = din("wout0", [768, DM])
    if do_l1:
        fg_d = din("fg", [128, DM])
        w1b_d = din("w1b", [8, DM, 512])
        w1z2_d = din("w1z2", [DM, 256])
        w1qm_d = din("w1qm", [DM, 256])
        wout1_d = din("wout1", [1280, DM])
    out_d = nc.dram_tensor("out", [S_TOK, DM], F32, kind="ExternalOutput").ap()

    S = Sched()
    frozen = {"f": False}

    def A(eng, emit, reads=(), writes=(), chan=None):
        if frozen["f"]:
            return None
        return S.add(eng, emit, reads, writes, chan)

    def stage(n):
        S.label = 'st%d' % n
        if stop is not None and n >= stop:
            frozen["f"] = True
    with ExitStack() as es:
        def sb(name, shape, dt):
            return es.enter_context(nc.sbuf_tensor(name, list(shape), dt))

        h = sb("h", [128, NT, DM], F32)
        hnT = sb("hnT", [128, 8, S_TOK], BF16)
        U = sb("U", [128, 26624], BF16)
        wr = [sb("wr%d" % i, [128, 8, 512], BF16) for i in range(2)]
        memnT = sb("memnT", [128, 8, 256], BF16)
        kmT = sb("kmT", [128, 2, 256], BF16)
        vm = sb("vm", [128, 2, 256], BF16)
        identb = sb("identb", [128, 128], BF16)
        ones64 = sb("ones64", [128, 64], BF16)
        cpk = sb("cpks", [128, 112], F32)
        ngs = cpk[:, 0:16].rearrange("p (l k) -> p l k", l=2)
        mgs = cpk[:, 16:32].rearrange("p (l k) -> p l k", l=2)
        invfs = cpk[:, 32:40]
        cws = cpk[:, 40:64].rearrange("p (j k) -> p j k", j=8)
        pos_i = cpk[:, 64:112].bitcast(I32)
        xn = [sb("xn%d" % i, [128, DM], BF16) for i in range(2)]
        ss = sb("ss", [128, 40], F32)
        rstd = sb("rstd", [128, 40], F32)
        Ptt = sb("Ptt", [128, 4, 512], BF16)
        Pt = [Ptt[:, i, :] for i in range(4)]
        sqjunk = Ptt[:, 0:2, :].rearrange("p a b -> p (a b)")
        szt = [sb("sz%d" % i, [128, 512], BF16) for i in range(2)]
        mtb = [sb("mtb%d" % i, [128, 512], BF16) for i in range(2)]
        X = sb("X", [128, 10240], BF16)
        if do_l0:
            qkvB = X[:, 0:6144]
            qkst = [X[:, 6144 + 1024 * i:6144 + 1024 * (i + 1)].rearrange("p (b c) -> p b c", b=4) for i in range(2)]
            rt = [X[:, 8192 + 256 * i:8192 + 256 * (i + 1)].bitcast(F32).rearrange("p (a b c) -> p a b c", a=4, b=4) for i in range(4)]
            maskAb = X[:, 9216:9728]
            maskDb = X[:, 9728:10240]
            posf = sb("posf", [128, 48], F32)
            ang = qkvB[:, 0:768].bitcast(F32).rearrange("p (a b) -> p a b", a=48)
            ang2 = qkvB[:, 768:1536].bitcast(F32).rearrange("p (a b) -> p a b", a=48)
            angi = qkvB[:, 1536:2304].bitcast(I32).rearrange("p (a b) -> p a b", a=48)
            halfpi = sb("halfpi", [128, 1], F32)
            cosT = sb("cosT", [128, 48, 8], F32)
            sinT = sb("sinT", [128, 48, 8], F32)
        banks = [es.enter_context(nc.psum_tensor("bank%d" % i, [128, 512], F32)) for i in range(8)]
        sems = {e: es.enter_context(nc.semaphore("s_" + e)) for e in ENGS}
        chan_names = ["x0", "x1", "x2", "x3", "c", "cp", "w0", "w1", "wo", "mem", "fg", "o0", "o1"]
        chans = {c: es.enter_context(nc.semaphore("c_" + c)) for c in chan_names}
        block = es.enter_context(nc.Block())

        def KB(i):
            return [("bank", i)]
        HK = [("h", t) for t in range(NT)]
        HN = [("hnT", t) for t in range(NT)]

        xv = x_d.rearrange("(t p) d -> p t d", p=128)

        def load_x(chunks=(0, 1, 2, 3)):
            for i in chunks:
                A("sp", lambda e, i=i: e.dma_start(out=h[:, 4 * i:4 * i + 4, :], in_=xv[:, 4 * i:4 * i + 4, :]),
                  writes=HK[4 * i:4 * i + 4], chan="x%d" % i)
        cl = [(cpk, cpk_d, "sp"), (identb, ident_d, "pool")]
        if do_l0:
            cl += [(maskAb, maskA_d, "pool"), (maskDb, maskD_d, "pool")]
        for (dst, src, q) in cl:
            A(q, lambda e, dst=dst, src=src: e.dma_start(out=dst[:], in_=src), writes=["constsP" if q == "pool" else "consts"],
              chan=("cp" if q == "pool" else "c"))
        load_x((0, 1))
        load_x((2, 3))
        A("pool", lambda e: e.memset(ones64[:], 1.0), writes=["ones"])
        A("dve", lambda e: e.memset(ss[:], 0.0), writes=["ss"])
        if do_l0:
            A("dve", lambda e: e.memset(halfpi[:], PI / 2), writes=["halfpi"])

        wplan = []
        if do_l0:
            wplan += [(w0a_d[0], 384), (w0a_d[1], 384), (wkv_d[0], 512)] + [(w0a_d[i], 384) for i in range(2, 12)] + [(w0qm_d, 256), (w0z_d[1], 384), (w0z_d[0], 384)]
        if do_l1:
            wplan += [(wkv_d[1], 512)] + [(w1b_d[j], 512) for j in range(8)] + [(w1qm_d, 256), (w1z2_d, 256)]
        wstate = {"cur": 0, "issued": 0}

        def _wissue():
            k = wstate["issued"]
            if k >= len(wplan):
                return
            src_ap, ncols = wplan[k]
            i = k % 2
            wstate["issued"] += 1
            A("pool", lambda e: e.dma_start(out=wr[i][:, :, 0:ncols], in_=src_ap.rearrange("(kc p) n -> p kc n", p=128)),
              writes=[("wr", i)], chan="w%d" % i)

        def wload(src_ap, ncols, prefetch=True):
            k = wstate["cur"]
            assert wplan[k][1] == ncols, (k, ncols, wplan[k][1])
            while wstate["issued"] <= min(k + (1 if prefetch else 0), len(wplan) - 1):
                _wissue()
            wstate["cur"] += 1
            return wr[k % 2], ("wr", k % 2)

        def rms_squares(tiles, s0):
            for i, (src_tile, src_keys) in enumerate(tiles):
                A("act", lambda e, src_tile=src_tile, i=i: e.activation(out=sqjunk, in_=src_tile, func=AF.Square, accum_out=ss[:, s0 + i:s0 + i + 1]),
                  reads=list(src_keys) + ["ss"], writes=[("ss", s0 + i), ("P", 0), ("P", 1)])

        def rms_rstd(n, s0):
            A("dve", lambda e: e.tensor_scalar(out=rstd[:, s0:s0 + n], in0=ss[:, s0:s0 + n], scalar1=1.0 / DM, scalar2=EPS,
                                               op0=ALU.mult, op1=ALU.add), reads=[("ss", s0 + i) for i in range(n)] + ["ss"], writes=[("rs0", s0)])
            A("act", lambda e: e.activation(out=rstd[:, s0:s0 + n], in_=rstd[:, s0:s0 + n], func=AF.Sqrt), reads=[("rs0", s0)], writes=[("rs1", s0)])
            A("dve", lambda e: e.reciprocal(out=rstd[:, s0:s0 + n], in_=rstd[:, s0:s0 + n]), reads=[("rs1", s0)], writes=[("rs", s0)])

        def rms_stats(tiles, s0):
            rms_squares(tiles, s0)
            rms_rstd(len(tiles), s0)

        def rmsnorm_T(src_tile, src_keys, gvec, dstT, dst_col0, dst_keys, sidx, s0, i):
            xb = xn[i % 2]
            xk = ("xn", i % 2)
            A("act", lambda e: e.activation(out=xb[:], in_=src_tile, func=AF.Copy, scale=rstd[:, sidx:sidx + 1]),
              reads=list(src_keys) + [("rs", s0)], writes=[xk])
            tb_ = (2, 4)[i % 2]
            pb = banks[tb_][:].bitcast(BF16)
            for kc in range(8):
                A("pe", lambda e, kc=kc: e.transpose(out=pb[:, kc * 128:(kc + 1) * 128], in_=xb[:, kc * 128:(kc + 1) * 128], identity=identb[:]),
                  reads=[xk, "constsP"], writes=KB(tb_))
            A("dve", lambda e: e.tensor_tensor(out=dstT[:, :, dst_col0:dst_col0 + 128],
                                               in0=pb.rearrange("p (k t) -> p k t", k=8),
                                               in1=gvec.unsqueeze(2).to_broadcast([128, 8, 128]), op=ALU.mult),
              reads=KB(tb_) + ["consts"], writes=list(dst_keys))

        def layer_norm_cb(l, defer_last=None):
            def apply(b):
                for t in range(4 * b, 4 * b + 4):
                    rmsnorm_T(h[:, t, :], [HK[t]], ngs[:, l, :], hnT, t * 128, [HN[t]], t, 4 * b, t)

            def cb(b):
                rms_squares([(h[:, t, :], [HK[t]]) for t in range(4 * b, 4 * b + 4)], 4 * b)
                if b >= 1:
                    rms_rstd(4, 4 * (b - 1))
                if b >= 2:
                    apply(b - 2)
                if b == 3:
                    apply(1)
                    rms_rstd(4, 12)
                    if defer_last is not None:
                        defer_last.append(lambda: (apply(2), apply(3)))
                    else:
                        apply(2)
                        apply(3)
            return cb

        def layer_norm_phase(l, units=(), after_stats1=None):
            units = list(units)
            tiles_done = 0

            def stats(b):
                rms_stats([(h[:, t, :], [HK[t]]) for t in range(4 * b, 4 * b + 4)], 4 * b)

            stats(0)
            for b in range(4):
                if b + 1 < 4:
                    stats(b + 1)
                if b == 0 and after_stats1 is not None:
                    after_stats1()
                ready = 4 * b
                while units and tiles_done < ready:
                    kind, u = units.pop(0)
                    u()
                    if kind == "t":
                        tiles_done += 1
                while units and units[0][0] != "t":
                    units.pop(0)[1]()
                for t in range(4 * b, 4 * b + 4):
                    rmsnorm_T(h[:, t, :], [HK[t]], ngs[:, l, :], hnT, t * 128, [HN[t]], t, 4 * b, t)
            for kind, u in units:
                u()

        def mem_stats_part(l, tmp_ap, tmp_keys, extra_reads):
            A("sp", lambda e: e.dma_start(out=tmp_ap, in_=mem_d.rearrange("(t p) d -> p t d", p=128)),
              reads=list(extra_reads), writes=list(tmp_keys), chan="mem")
            rms_stats([(tmp_ap[:, t, :], tmp_keys) for t in range(2)], 16 + 2 * l)

        def mem_apply_part(l, tmp_ap, tmp_keys):
            for t in range(2):
                rmsnorm_T(tmp_ap[:, t, :], tmp_keys, mgs[:, l, :], memnT, t * 128, ["memnT"], 16 + 2 * l + t, 16 + 2 * l, t)

        def mem_norm_part(l, tmp_ap, tmp_keys, extra_reads):
            mem_stats_part(l, tmp_ap, tmp_keys, extra_reads)
            mem_apply_part(l, tmp_ap, tmp_keys)

        def mem_kv_part(l):
            wt, wk = wload(wkv_d[l], 512)
            for mc in range(2):
                for kc in range(8):
                    A("pe", lambda e, mc=mc, kc=kc: e.matmul(banks[0][:, mc * 256:(mc + 1) * 256], lhsT=wt[:, kc, mc * 128:(mc + 1) * 128],
                                                             rhs=memnT[:, kc, :], start=(kc == 0), stop=(kc == 7)),
                      reads=[wk, "memnT"], writes=KB(0))
            A("act", lambda e: e.activation(out=kmT[:].rearrange("p a b -> p (a b)"), in_=banks[0][:, :], func=AF.Copy),
              reads=KB(0), writes=["kmT"])
            for mb in range(2):
                for kc in range(8):
                    A("pe", lambda e, mb=mb, kc=kc: e.matmul(banks[1][:, mb * 256:(mb + 1) * 256], lhsT=memnT[:, kc, mb * 128:(mb + 1) * 128],
                                                             rhs=wt[:, kc, 256:512], start=(kc == 0), stop=(kc == 7)),
                      reads=[wk, "memnT"], writes=KB(1))
            A("dve", lambda e: e.tensor_copy(out=vm[:].rearrange("p a b -> p (a b)"), in_=banks[1][:, :]),
              reads=KB(1), writes=["vm"])

        def make_memattn(qmT, qm_keys, yT, ychunk0, rd_ap, rd_key, rd_first=()):
            out = []
            sbank = {(0, 0): 3, (1, 0): 4, (0, 1): 5, (1, 1): 6}
            for it in range(8):
                mc, qd = divmod(it, 4)
                qs = slice(qd * 512, (qd + 1) * 512)

                def S_fn(mc=mc, qs=qs):
                    for mb in range(2):
                        for hh in range(2):
                            hp = slice(hh * 64, hh * 64 + 64)
                            bk = sbank[(hh, mb)]
                            A("pe", lambda e, mb=mb, hp=hp, bk=bk: e.matmul(
                                banks[bk][:, :], lhsT=kmT[hp, mc, mb * 128:(mb + 1) * 128], rhs=qmT[hp, mc, qs], start=True, stop=True),
                              reads=["kmT"] + list(qm_keys[mc]), writes=KB(bk))
                    for mb in range(2):
                        for hh in range(2):
                            bk = sbank[(hh, mb)]
                            pi = hh * 2 + mb
                            A("act", lambda e, bk=bk, pi=pi: e.activation(out=Pt[pi][:], in_=banks[bk][:, :], func=AF.Exp, scale=0.125),
                              reads=KB(bk), writes=[("P", pi)])

                def PV_fn(mc=mc, qd=qd, qs=qs, it=it):
                    for hh in range(2):
                        ph = slice(hh * 64, hh * 64 + 64)
                        tp = (0, 64) if hh else None
                        for mb in range(2):
                            pi = hh * 2 + mb
                            A("pe", lambda e, mb=mb, hh=hh, ph=ph, pi=pi, tp=tp: e.matmul(
                                banks[7][ph, :], lhsT=vm[:, mb, mc * 128 + hh * 64:mc * 128 + hh * 64 + 64], rhs=Pt[pi][:],
                                start=(mb == 0), stop=(mb == 1), tile_position=tp),
                              reads=["vm", ("P", pi)], writes=KB(7))
                        for mb in range(2):
                            pi = hh * 2 + mb
                            A("pe", lambda e, mb=mb, ph=ph, pi=pi, tp=tp: e.matmul(
                                banks[2][ph, :], lhsT=ones64[:], rhs=Pt[pi][:], start=(mb == 0), stop=(mb == 1), tile_position=tp),
                              reads=["ones", ("P", pi)], writes=KB(2))
                    rdv = rd_ap[:, (it % 2) * 512:(it % 2) * 512 + 512]
                    rk = (rd_key, it % 2)
                    tb = mtb[it % 2]
                    tk = ("mtb", it % 2)
                    A("act", lambda e: e.activation(out=tb[:], in_=banks[7][:, :], func=AF.Copy, scale=0.5), reads=KB(7), writes=[tk])
                    A("act", lambda e: e.activation(out=rdv, in_=banks[2][:, :], func=AF.Copy), reads=KB(2), writes=[rk] + (list(rd_first) if it < 2 else []))
                    A("dve", lambda e: e.reciprocal(out=rdv, in_=rdv), reads=[rk], writes=[rk])
                    A("dve", lambda e: e.tensor_tensor(out=tb[:], in0=tb[:], in1=rdv, op=ALU.mult), reads=[tk, rk], writes=[tk])
                    A("pool", lambda e: e.tensor_tensor(out=yT[:, ychunk0 + mc, qs], in0=yT[:, ychunk0 + mc, qs], in1=tb[:], op=ALU.mult),
                      reads=[tk, ("y", ychunk0 + mc, qd)], writes=[("y", ychunk0 + mc, qd)])
                out.append((S_fn, PV_fn))
            return out

        def proj_unit(wt, wk, col0, qd, bk, evac):
            for kc in range(8):
                A("pe", lambda e, kc=kc: e.matmul(banks[bk][:, :], lhsT=wt[:, kc, col0:col0 + 128],
                                                 rhs=hnT[:, kc, qd * 512:(qd + 1) * 512], start=(kc == 0), stop=(kc == 7)),
                  reads=[wk] + HN[4 * qd:4 * qd + 4], writes=KB(bk))
            evac(qd, banks[bk], KB(bk))

        def proj_fm(wt, wk, col0, evac, bank_ids):
            for qd in range(4):
                bk = bank_ids[qd % len(bank_ids)]
                for kc in range(8):
                    A("pe", lambda e, kc=kc, qd=qd, bk=bk: e.matmul(banks[bk][:, :], lhsT=wt[:, kc, col0:col0 + 128],
                                                                    rhs=hnT[:, kc, qd * 512:(qd + 1) * 512], start=(kc == 0), stop=(kc == 7)),
                      reads=[wk] + HN[4 * qd:4 * qd + 4], writes=KB(bk))
                evac(qd, banks[bk], KB(bk))

        def load_wout(wo, wout_d, wo_keys, extra_reads=()):
            A("pool", lambda e: e.dma_start(out=wo, in_=wout_d.rearrange("(c p) n -> p c n", p=128)),
              reads=list(extra_reads), writes=list(wo_keys), chan="wo")

        def out_proj(yT, nchunk, wo, wo_keys, after_batch=None, after_tile=None):
            for t in range(NT):
                for hf in range(2):
                    bk = (0, 1, 7, 3)[(t * 2 + hf) % 4]
                    for c in range(nchunk):
                        A("pe", lambda e, t=t, hf=hf, c=c, bk=bk: e.matmul(banks[bk][:, :], lhsT=yT[:, c, t * 128:(t + 1) * 128],
                                                                           rhs=wo[:, c, hf * 512:(hf + 1) * 512], start=(c == 0), stop=(c == nchunk - 1)),
                          reads=list(wo_keys) + [("y", c, t // 4)], writes=KB(bk))
                    A("dve", lambda e, t=t, hf=hf, bk=bk: e.tensor_tensor(out=h[:, t, hf * 512:(hf + 1) * 512], in0=h[:, t, hf * 512:(hf + 1) * 512],
                                                                          in1=banks[bk][:, :], op=ALU.add),
                      reads=KB(bk) + [HK[t]], writes=[HK[t]])
                if after_batch is not None and t % 4 == 3:
                    after_batch(t // 4)
                if after_tile is not None:
                    after_tile(t)

        if do_l0:
            yT0 = U[:, 0:12288].rearrange("p (c t) -> p c t", c=6)
            qT = U[:, 12288:14336]
            kT = U[:, 14336:16384]
            Vt = U[:, 16384:18432].rearrange("p (b c) -> p b c", b=16)
            accn = U[:, 18432:22528].bitcast(F32)
            accd = U[:, 22528:26624].bitcast(F32)
            qmT0 = U[:, 12288:16384].rearrange("p (c t) -> p c t", c=2)
            memtmp0 = U[:, 18432:22528].bitcast(F32).rearrange("p (t d) -> p t d", t=2)

            stage(1)
            memtmp0 = U[:, 8192:12288].bitcast(F32).rearrange("p (t d) -> p t d", t=2)
            MK0 = [("y", 4 + i, qd) for i in range(2) for qd in range(4)]

            QB = qkvB
            sets = [
                dict(qT=U[:, 12288:14336], kT=U[:, 14336:16384], V=U[:, 16384:18432].rearrange("p (b c) -> p b c", b=16), kq="qT", kk="kT", kv="V", ix=0),
                dict(qT=QB[:, 0:2048], kT=QB[:, 2048:4096], V=QB[:, 4096:6144].rearrange("p (b c) -> p b c", b=16), kq="qT1", kk="kT1", kv="V1", ix=1),
            ]
            def GK(st0, which, b4):
                return (st0[which], b4)

            def GKALL(st0, which):
                return [(st0[which], b4) for b4 in range(4)]
            tile_ctr = {"n": 0}
            TB = (0, 1, 7)

            def make_inproj(k):
                c, g = divmod(k, 3)
                d = DIL[g]
                st_ = sets[k % 2]
                wt, wk = wload(w0a_d[k], 384)
                units = []
                for b4 in range(4):
                    stg = qkst[b4 % 2]
                    stk = ("qkst", b4 % 2)
                    for bi in range(4):
                        def tile_unit(b4=b4, bi=bi, stg=stg, stk=stk):
                            b = b4 * 4 + bi
                            t0 = tok_start(g, b)
                            if g == 0:
                                hn_keys = [HN[b]]
                            elif g == 1:
                                hn_keys = HN[4 * (b % 4):4 * (b % 4) + 4]
                            else:
                                hn_keys = HN
                            bk = TB[tile_ctr["n"] % 3]
                            tile_ctr["n"] += 1
                            for kc in range(8):
                                A("pe", lambda e, kc=kc, t0=t0, bk=bk: e.matmul(banks[bk][:, 0:384], lhsT=hnT[:, kc, sl(t0, 128, d)],
                                                                               rhs=wt[:, kc, 0:384], start=(kc == 0), stop=(kc == 7)),
                                  reads=[wk] + hn_keys, writes=KB(bk))
                            Vt = st_["V"]
                            if b % 2 == 0:
                                A("act", lambda e: e.activation(out=stg[:, bi, :], in_=banks[bk][:, 0:256], func=AF.Copy), reads=KB(bk), writes=[stk])
                                A("act", lambda e: e.activation(out=Vt[:, b, :], in_=banks[bk][:, 256:384], func=AF.Copy), reads=KB(bk), writes=[GK(st_, "kv", b // 4)])
                            else:
                                A("dve", lambda e: e.tensor_copy(out=stg[:, bi, :], in_=banks[bk][:, 0:256]), reads=KB(bk), writes=[stk])
                                A("dve", lambda e: e.tensor_copy(out=Vt[:, b, :], in_=banks[bk][:, 256:384]), reads=KB(bk), writes=[GK(st_, "kv", b // 4)])
                            if bi == 3:
                                sv = stg[:].rearrange("p b (h d) -> p b h d", h=4)
                                t1 = sv[:, :, :, 0:8]
                                t2 = sv[:, :, :, 8:16]
                                col = g * 16 + b4 * 4
                                cb = cosT[:, col:col + 4, :].unsqueeze(2).to_broadcast([128, 4, 4, 8])
                                sbb = sinT[:, col:col + 4, :].unsqueeze(2).to_broadcast([128, 4, 4, 8])
                                A("dve", lambda e: e.tensor_tensor(out=rt[0][:], in0=t1, in1=cb, op=ALU.mult), reads=[stk, "cosT"], writes=["rt0"])
                                A("pool", lambda e: e.tensor_tensor(out=rt[2][:], in0=t2, in1=cb, op=ALU.mult), reads=[stk, "cosT"], writes=["rt2"])
                                A("dve", lambda e: e.tensor_tensor(out=rt[1][:], in0=t2, in1=sbb, op=ALU.mult), reads=[stk, "sinT"], writes=["rt1"])
                                A("pool", lambda e: e.tensor_tensor(out=rt[3][:], in0=t1, in1=sbb, op=ALU.mult), reads=[stk, "sinT"], writes=["rt3"])
                                A("dve", lambda e: e.tensor_tensor(out=t1, in0=rt[0][:], in1=rt[1][:], op=ALU.subtract), reads=["rt0", "rt1", "rt3"], writes=[stk])
                                A("pool", lambda e: e.tensor_tensor(out=t2, in0=rt[2][:], in1=rt[3][:], op=ALU.add), reads=["rt2", "rt3", "rt1"], writes=[stk])
                        units.append(("t", tile_unit))

                    def tr_unit(b4=b4, stg=stg, stk=stk):
                        pb = banks[2][:].bitcast(BF16)
                        for bi in range(4):
                            A("pe", lambda e, bi=bi: e.transpose(out=pb[:, bi * 128:(bi + 1) * 128], in_=stg[:, bi, 0:128], identity=identb[:]),
                              reads=[stk, "constsP"], writes=KB(2))
                            A("pe", lambda e, bi=bi: e.transpose(out=pb[:, 512 + bi * 128:512 + (bi + 1) * 128], in_=stg[:, bi, 128:256], identity=identb[:]),
                              reads=[stk, "constsP"], writes=KB(2))
                        qT_, kT_ = st_["qT"], st_["kT"]
                        if b4 % 2 == 0:
                            A("act", lambda e: e.activation(out=qT_[:, b4 * 512:(b4 + 1) * 512], in_=pb[:, 0:512], func=AF.Copy), reads=KB(2), writes=[GK(st_, "kq", b4)])
                            A("act", lambda e: e.activation(out=kT_[:, b4 * 512:(b4 + 1) * 512], in_=pb[:, 512:1024], func=AF.Copy), reads=KB(2), writes=[GK(st_, "kk", b4)])
                        else:
                            A("dve", lambda e: e.tensor_copy(out=qT_[:, b4 * 512:(b4 + 1) * 512], in_=pb[:, 0:512]), reads=KB(2), writes=[GK(st_, "kq", b4)])
                            A("dve", lambda e: e.tensor_copy(out=kT_[:, b4 * 512:(b4 + 1) * 512], in_=pb[:, 512:1024]), reads=KB(2), writes=[GK(st_, "kk", b4)])
                    units.append(("r", tr_unit))
                tiles = [u for u in units if u[0] == "t"]
                trs = [u for u in units if u[0] == "r"]
                order = tiles[0:8] + [trs[0]] + tiles[8:12] + [trs[1]] + tiles[12:16] + [trs[2]]
                return order, trs[3][1]

            def make_attn(k):
                c, g = divmod(k, 3)
                d = DIL[g]
                nb = NT // d
                st_ = sets[k % 2]
                qT, kT, Vt = st_["qT"], st_["kT"], st_["V"]
                if g < 2:
                    iters = []
                    for r in range(d):
                        for nh in range(nb // 2):
                            iters.append([(r * nb + 2 * nh + s, (2 * nh + s) > 0) for s in range(2)])
                    mask = maskAb
                else:
                    iters = [[(4 * i + s, False) for s in range(4)] for i in range(4)]
                    mask = maskDb
                out = []
                for iti, qbs in enumerate(iters):
                    lo = 512
                    offs = []
                    for s, (b, hp_) in enumerate(qbs):
                        if g < 2:
                            o_prev, o_diag = s * 256, s * 256 + 128
                        else:
                            o_prev, o_diag = None, s * 128
                        offs.append((o_prev, o_diag))
                        lo = min(lo, o_prev if hp_ else o_diag)
                    par = iti % 2

                    def S_fn(qbs=qbs, offs=offs, lo=lo, par=par):
                        for s, (b, hp_) in enumerate(qbs):
                            o_prev, o_diag = offs[s]
                            for part in ((0, 1) if hp_ else (1,)):
                                for hh in range(2):
                                    hp = slice(hh * 64, hh * 64 + 64)
                                    bk = 3 + hh
                                    kb = b - 1 if part == 0 else b
                                    o = o_prev if part == 0 else o_diag
                                    A("pe", lambda e, hp=hp, bk=bk, b=b, kb=kb, o=o, hh=hh: e.matmul(banks[bk][:, o:o + 128], lhsT=kT[hp, kb * 128:(kb + 1) * 128],
                                                                                             rhs=qT[hp, b * 128:(b + 1) * 128], start=True, stop=True,
                                                                                             tile_position=(64 * hh, 0)),
                                      reads=[GK(st_, "kq", b // 4), GK(st_, "kk", kb // 4)], writes=KB(bk))
                        for hh in range(2):
                            bk = 3 + hh
                            pi = par * 2 + hh
                            A("act", lambda e, bk=bk, pi=pi: e.activation(out=Pt[pi][:, lo:512], in_=banks[bk][:, lo:512], func=AF.Exp, scale=0.125),
                              reads=KB(bk), writes=[("P", pi)])
                            A("dve" if hh == 0 else "pool", lambda e, pi=pi: e.tensor_tensor(out=Pt[pi][:, lo:512], in0=Pt[pi][:, lo:512],
                                                                                          in1=mask[:, lo:512], op=ALU.mult),
                              reads=[("P", pi), "constsP"], writes=[("P", pi)])

                    def PV_fn(qbs=qbs, offs=offs, par=par, iti=iti):
                        onb, odb = 5, 6
                        for (obank, lv) in ((onb, True), (odb, False)):
                            for s, (b, hp_) in enumerate(qbs):
                                o_prev, o_diag = offs[s]
                                oc = s * 128
                                for part in ((0, 1) if hp_ else (1,)):
                                    for hh in range(2):
                                        ph = slice(hh * 64, hh * 64 + 64)
                                        tp = (0, 64 * hh)
                                        pi = par * 2 + hh
                                        kb = b - 1 if part == 0 else b
                                        o = o_prev if part == 0 else o_diag
                                        st_flag = (part == 0) or (not hp_)
                                        sp_flag = (part == 1)
                                        A("pe", lambda e, ph=ph, tp=tp, pi=pi, kb=kb, o=o, oc=oc, obank=obank, lv=lv, hh=hh, st_flag=st_flag, sp_flag=sp_flag: e.matmul(
                                            banks[obank][ph, oc:oc + 128], lhsT=(Vt[:, kb, hh * 64:hh * 64 + 64] if lv else ones64[:]),
                                            rhs=Pt[pi][:, o:o + 128], start=st_flag, stop=sp_flag, tile_position=tp),
                                          reads=[GK(st_, "kv", kb // 4), "ones", ("P", pi)], writes=KB(obank))
                        nq = len(qbs) * 128
                        if g == 0:
                            dn = accn[:, iti * 256:iti * 256 + 256]
                            dd = accd[:, iti * 256:iti * 256 + 256]
                        elif g == 1:
                            r, nh = divmod(iti, 2)
                            dn = accn[:, sl(1024 * nh + r, 256, 4)]
                            dd = accd[:, sl(1024 * nh + r, 256, 4)]
                        else:
                            dn = accn.rearrange("p (i r) -> p r i", r=16)[:, 4 * iti:4 * iti + 4, :]
                            dd = accd.rearrange("p (i r) -> p r i", r=16)[:, 4 * iti:4 * iti + 4, :]
                        srcn = banks[onb][:, 0:nq]
                        srcd = banks[odb][:, 0:nq]
                        if g == 2:
                            srcn = srcn.rearrange("p (r i) -> p r i", r=4)
                            srcd = srcd.rearrange("p (r i) -> p r i", r=4)
                        if g == 0:
                            A("act", lambda e: e.activation(out=dn, in_=srcn, func=AF.Copy), reads=KB(onb), writes=[("accn", iti // 2)])
                            A("dve", lambda e: e.tensor_copy(out=dd, in_=srcd), reads=KB(odb), writes=[("accd", iti // 2)])
                        else:
                            AN = [("accn", q_) for q_ in range(4)]
                            AD = [("accd", q_) for q_ in range(4)]
                            A("dve", lambda e: e.tensor_tensor(out=dn, in0=dn, in1=srcn, op=ALU.add), reads=KB(onb) + AN, writes=AN)
                            A("dve", lambda e: e.tensor_tensor(out=dd, in0=dd, in1=srcd, op=ALU.add), reads=KB(odb) + AD, writes=AD)
                    out.append((S_fn, PV_fn))
                return out

            NK = 12
            S.label = "norm0"
            def rope_tables():
                A("dve", lambda e: e.tensor_copy(out=posf[:], in_=pos_i[:]), reads=["consts"], writes=["posf"])
                A("dve", lambda e: e.tensor_tensor(out=ang[:], in0=posf[:].unsqueeze(2).to_broadcast([128, 48, 8]),
                                                   in1=invfs[:].unsqueeze(1).to_broadcast([128, 48, 8]), op=ALU.mult),
                  reads=["posf", "consts"], writes=["ang"])
                C1 = 6.28125
                C2 = 2.0 * np.pi - C1
                A("dve", lambda e: e.tensor_scalar(out=ang2[:], in0=ang[:], scalar1=1.0 / (2 * PI), scalar2=None, op0=ALU.mult),
                  reads=["ang"], writes=["ang2"])
                A("dve", lambda e: e.tensor_copy(out=angi[:], in_=ang2[:]), reads=["ang2"], writes=["angi"])
                A("dve", lambda e: e.tensor_copy(out=ang2[:], in_=angi[:]), reads=["angi", "ang2"], writes=["angf"])
                A("dve", lambda e: e.scalar_tensor_tensor(out=ang[:], in0=ang2[:], scalar=-C1, in1=ang[:], op0=ALU.mult, op1=ALU.add),
                  reads=["angf", "ang"], writes=["r1"])
                A("dve", lambda e: e.scalar_tensor_tensor(out=ang[:], in0=ang2[:], scalar=-C2, in1=ang[:], op0=ALU.mult, op1=ALU.add),
                  reads=["angf", "r1"], writes=["frac"])
                A("act", lambda e: e.activation(out=sinT[:], in_=ang[:], func=AF.Sin, scale=0.5), reads=["frac"], writes=["sh"])
                A("act", lambda e: e.activation(out=cosT[:], in_=ang[:], func=AF.Sin, scale=-0.5, bias=halfpi[:, 0:1]), reads=["frac", "halfpi"], writes=["ch"])
                A("dve", lambda e: e.tensor_tensor(out=ang2[:], in0=sinT[:], in1=sinT[:], op=ALU.mult), reads=["sh", "angf", "frac"], writes=["s2"])
                A("dve", lambda e: e.scalar_tensor_tensor(out=sinT[:], in0=sinT[:], scalar=2.0, in1=cosT[:], op0=ALU.mult, op1=ALU.mult),
                  reads=["sh", "ch", "s2"], writes=["sinT"])
                A("dve", lambda e: e.tensor_scalar(out=cosT[:], in0=ang2[:], scalar1=-2.0, scalar2=1.0, op0=ALU.mult, op1=ALU.add),
                  reads=["s2", "sinT"], writes=["cosT"])

            for _ in range(N_WARM):
                A("pe", lambda e: e.matmul(banks[6][:, :], lhsT=identb[:], rhs=maskAb[:, 0:512], start=True, stop=True),
                  reads=["constsP"], writes=KB(6))
            units0, carry = make_inproj(0)
            pending_norm = []
            layer_norm_phase(0, units=units0, after_stats1=rope_tables)
            mem_stats_part(0, memtmp0, MK0, [])
            for k in range(NK):
                c, g = divmod(k, 3)
                S.label = "attn%d" % k
                its = make_attn(k)
                nxt, nxt_carry = make_inproj(k + 1) if k + 1 < NK else ([], None)
                n_it = len(its)
                if k == NK - 1:
                    wq_t, wq_k = wload(w0qm_d, 256)
                    qn = 0
                    for mc in range(2):
                        for qd in range(4):
                            def qm_unit(mc=mc, qd=qd, bk=(0, 1)[qn % 2]):
                                def ev_qm(qd_, bank, bkey):
                                    A("act", lambda e: e.activation(out=qmT0[:, mc, qd_ * 512:(qd_ + 1) * 512], in_=bank[:, :], func=AF.Copy),
                                      reads=list(bkey), writes=GKALL(sets[0], ("kq", "kk")[mc]))
                                proj_unit(wq_t, wq_k, mc * 128, qd, bk, ev_qm)
                            nxt.append(("t", qm_unit))
                            qn += 1
                    qm_done = True
                per = [[] for _ in range(n_it)]
                tiles_per = max(1, sum(1 for kd, _ in nxt if kd == "t") // n_it)
                cnt = 0
                slot = 0
                for (kind, u) in nxt:
                    per[min(slot, n_it - 1)].append(u)
                    if kind == "t":
                        cnt += 1
                        if cnt % tiles_per == 0:
                            slot += 1
                def make_norm(c_):
                    fns = []
                    for q_ in range(4):
                        def fn(q_=q_):
                            cs = slice(q_ * 512, (q_ + 1) * 512)
                            A("act", lambda e: e.activation(out=accd[:, cs], in_=accd[:, cs], func=AF.Ln), reads=[("accd", q_)], writes=[("accd", q_)])
                            A("act", lambda e: e.activation(out=accd[:, cs], in_=accd[:, cs], func=AF.Exp, scale=-1.0), reads=[("accd", q_)], writes=[("accd", q_)])
                            A("dve", lambda e: e.scalar_tensor_tensor(out=yT0[:, c_, cs], in0=accn[:, cs], scalar=0.5, in1=accd[:, cs], op0=ALU.mult, op1=ALU.mult),
                              reads=[("accn", q_), ("accd", q_)], writes=[("y", c_, q_)])
                        fns.append(fn)
                    return fns

                its[0][0]()
                for i in range(n_it):
                    if i + 1 < n_it:
                        its[i + 1][0]()
                    if i == 0 and carry is not None:
                        carry()
                    for u in per[i]:
                        u()
                    if g == 0 and pending_norm and i % 2 == 0:
                        pending_norm.pop(0)()
                    its[i][1]()
                carry = nxt_carry
                if k == 0:
                    S.label = "mem0"
                    mem_apply_part(0, memtmp0, MK0)
                    mem_kv_part(0)
                if g == 2:
                    pending_norm = make_norm(c)
                    if c == 3:
                        for fn in pending_norm:
                            fn()
                        pending_norm = []
            S.label = "qm0"
            ACCK = [("accn", q_) for q_ in range(4)] + [("accd", q_) for q_ in range(4)]
            wo0 = U[:, 18432:24576].rearrange("p (c n) -> p c n", c=6)
            assert qm_done
            if do_l1:
                memtmpX = X[:, 0:4096].bitcast(F32).rearrange("p (t d) -> p t d", t=2)
                mem_stats_part(1, memtmpX, GKALL(sets[1], "kq") + GKALL(sets[1], "kk"), [])
            S.label = "z0"
            zstate = {"zi": 0, "n": 0}
            zunits = []
            for zb in (1, 0):
                def lazy_w(zb=zb, cache={}):
                    if "w" not in cache:
                        cache["w"] = wload(w0z_d[zb], 384)
                    return cache["w"]
                for zc3 in ((1, 2, 0) if zb == 1 else (0, 1, 2)):
                    zc = zb * 3 + zc3
                    for qd in range(4):
                        def zunit(zc=zc, zc3=zc3, qd=qd, lazy_w=lazy_w):
                            wt, wk = lazy_w()

                            def ev_z(qd, bank, bkey):
                                szb = szt[zstate["zi"] % 2]
                                szk = ("sz", zstate["zi"] % 2)
                                zstate["zi"] += 1
                                qs_ = slice(qd * 512, (qd + 1) * 512)
                                A("act", lambda e: e.activation(out=szb[:], in_=bank[:, :], func=AF.Tanh, scale=0.5), reads=list(bkey), writes=[szk])
                                if zc >= 4:
                                    A("dve", lambda e: e.scalar_tensor_tensor(out=yT0[:, zc, qs_], in0=szb[:], scalar=1.0, in1=bank[:, :], op0=ALU.add, op1=ALU.mult),
                                      reads=list(bkey) + [szk], writes=[("y", zc, qd)])
                                    return
                                A("dve", lambda e: e.scalar_tensor_tensor(out=szb[:], in0=szb[:], scalar=1.0, in1=bank[:, :], op0=ALU.add, op1=ALU.mult),
                                  reads=list(bkey) + [szk], writes=[szk])
                                A("pool", lambda e: e.tensor_tensor(out=yT0[:, zc, qs_], in0=yT0[:, zc, qs_], in1=szb[:], op=ALU.mult),
                                  reads=[szk, ("y", zc, qd)], writes=[("y", zc, qd)])
                            bk = (0, 1)[zstate["n"] % 2]
                            zstate["n"] += 1
                            proj_unit(wt, wk, zc3 * 128, qd, bk, ev_z)
                        zunits.append(zunit)
            for u in zunits[:8]:
                u()
            load_wout(wo0, wout0_d, ACCK)
            rest = zunits[8:]
            rd0 = U[:, 16384:18432].bitcast(F32)
            mits = make_memattn(qmT0, [GKALL(sets[0], "kq"), GKALL(sets[0], "kk")], yT0, 4, rd0, "rdA", rd_first=GKALL(sets[0], "kv"))
            S.label = "memattn0"
            for i, (S_fn, PV_fn) in enumerate(mits):
                S_fn()
                for u in rest[2 * i:2 * i + 2]:
                    u()
                PV_fn()
            S.label = "outproj0"
            l1_deferred = []
            out_proj(yT0, 6, wo0, ACCK, after_batch=(layer_norm_cb(1, defer_last=l1_deferred) if do_l1 else None))
            if do_l1:
                S.label = "mem1n"
                mem_apply_part(1, memtmpX, GKALL(sets[1], "kq") + GKALL(sets[1], "kk"))
                S.label = "mem1kv"
                mem_kv_part(1)

        if do_l1:
            yT1 = U[:, 0:20480].rearrange("p (c t) -> p c t", c=10)
            cgs = U[:, 20480:21504]
            a_sb = U[:, 21504:23560].bitcast(F32)
            cv = U[:, 23560:25608].bitcast(F32)
            memtmp1 = U[:, 20480:24576].bitcast(F32).rearrange("p (t d) -> p t d", t=2)
            qmT1 = U[:, 20480:24576].rearrange("p (c t) -> p c t", c=2)
            rd1 = U[:, 24576:26624].bitcast(F32)
            L1K = ["cg", "a", "cv"]

            S.label = 'L1mem'
            if not do_l0:
                mem_norm_part(1, memtmp1, L1K, [])
                mem_kv_part(1)
            S.label = 'L1norm'
            if not do_l0:
                layer_norm_phase(1)
            wo1 = X[:, :].rearrange("p (c n) -> p c n", c=10)
            load_wout(wo1, wout1_d, ["X"] + ((GKALL(sets[1], "kq") + GKALL(sets[1], "kk")) if do_l0 else []), HK)

            zi = 0
            for j in range(8):
                S.label = 'L1conv%d' % j
                wt, wk = wload(w1b_d[j], 512)
                for half in range(2):
                    tk0 = half * 1024
                    hnk = HN[8 * half:8 * half + 8]

                    def proj_half(col0, evac, bank_ids, wt=wt, wk=wk, tk0=tk0, hnk=hnk):
                        for q2 in range(2):
                            bk = bank_ids[q2]
                            for kc in range(8):
                                A("pe", lambda e, kc=kc, q2=q2, bk=bk, wt=wt, tk0=tk0, col0=col0: e.matmul(banks[bk][:, :], lhsT=wt[:, kc, col0:col0 + 128],
                                                                                rhs=hnT[:, kc, tk0 + q2 * 512:tk0 + (q2 + 1) * 512], start=(kc == 0), stop=(kc == 7)),
                                  reads=[wk] + hnk, writes=KB(bk))
                            evac(q2, banks[bk], KB(bk))

                    if half == 0:
                        A("dve", lambda e: e.memset(a_sb[:, 0:2], 0.0), writes=["a"])
                    else:
                        A("dve", lambda e: e.tensor_copy(out=a_sb[:, 0:2], in_=a_sb[:, 1024:1026]), reads=["a", "cv"], writes=["a"])
                    proj_half(128, lambda q2, bank, bkey: A("act", lambda e: e.activation(out=cgs[:, q2 * 512:(q2 + 1) * 512], in_=bank[:, :], func=AF.Copy),
                                                            reads=list(bkey), writes=["cg"]), (0, 1))
                    proj_half(256, lambda q2, bank, bkey: A("dve", lambda e: e.tensor_tensor(out=a_sb[:, 2 + q2 * 512:2 + (q2 + 1) * 512], in0=bank[:, :],
                                                                                             in1=cgs[:, q2 * 512:(q2 + 1) * 512], op=ALU.mult),
                                                            reads=list(bkey) + ["cg"], writes=["a"]), (3, 4))
                    A("act", lambda e, j=j: e.activation(out=cv[:, :], in_=a_sb[:, 2:1026], func=AF.Copy, scale=cws[:, j, 2:3]), reads=["a", "consts"], writes=["cv"])
                    A("dve", lambda e, j=j: e.scalar_tensor_tensor(out=cv[:, :], in0=a_sb[:, 1:1025], scalar=cws[:, j, 1:2], in1=cv[:, :], op0=ALU.mult, op1=ALU.add),
                      reads=["a", "cv", "consts"], writes=["cv"])
                    A("dve", lambda e, j=j: e.scalar_tensor_tensor(out=cv[:, :], in0=a_sb[:, 0:1024], scalar=cws[:, j, 0:1], in1=cv[:, :], op0=ALU.mult, op1=ALU.add),
                      reads=["a", "cv", "consts"], writes=["cv"])
                    proj_half(0, lambda q2, bank, bkey: A("dve", lambda e: e.tensor_tensor(out=cv[:, q2 * 512:(q2 + 1) * 512], in0=bank[:, :],
                                                                                           in1=cv[:, q2 * 512:(q2 + 1) * 512], op=ALU.mult),
                                                          reads=list(bkey) + ["cv"], writes=["cv"]), (5, 6))

                    def ev_z1(q2, bank, bkey, j=j, tk0=tk0, half=half):
                        nonlocal zi
                        szb = szt[zi % 2]
                        szk = ("sz", zi % 2)
                        zi += 1
                        A("act", lambda e: e.activation(out=szb[:], in_=bank[:, :], func=AF.Silu), reads=list(bkey), writes=[szk])
                        A("pool", lambda e: e.tensor_tensor(out=yT1[:, j, tk0 + q2 * 512:tk0 + (q2 + 1) * 512], in0=cv[:, q2 * 512:(q2 + 1) * 512],
                                                            in1=szb[:], op=ALU.mult),
                          reads=[szk, "cv"], writes=[("y", j, half * 2 + q2)])
                    proj_half(384, ev_z1, (7, 2))
                    if j == 0 and half == 0 and do_l0:
                        for fn in l1_deferred:
                            fn()

            S.label = 'L1qm'
            wt, wk = wload(w1qm_d, 256)
            for mc in range(2):
                def ev_qm1(qd, bank, bkey, mc=mc):
                    A("act", lambda e: e.activation(out=qmT1[:, mc, qd * 512:(qd + 1) * 512], in_=bank[:, :], func=AF.Copy),
                      reads=list(bkey), writes=L1K)
                proj_fm(wt, wk, mc * 128, ev_qm1, (0, 1))
            S.label = 'L1memattn'
            wz2 = wload(w1z2_d, 256)
            mits1 = make_memattn(qmT1, [[L1K[0]], [L1K[0]]], yT1, 8, rd1, "rdB", rd_first=["cv"])
            for i, (S_fn, PV_fn) in enumerate(mits1):
                S_fn()
                zc, qd = divmod(i, 4)

                def ev_z2(qd, bank, bkey, zc=zc, i=i):
                    szb = szt[i % 2]
                    szk = ("sz", i % 2)
                    A("act", lambda e: e.activation(out=szb[:], in_=bank[:, :], func=AF.Tanh, scale=0.5), reads=list(bkey), writes=[szk])
                    A("dve", lambda e: e.scalar_tensor_tensor(out=yT1[:, 8 + zc, qd * 512:(qd + 1) * 512], in0=szb[:], scalar=1.0, in1=bank[:, :],
                                                              op0=ALU.add, op1=ALU.mult),
                      reads=list(bkey) + [szk], writes=[("y", 8 + zc, qd)])
                proj_unit(wz2[0], wz2[1], zc * 128, qd, (0, 1)[i % 2], ev_z2)
                PV_fn()
            S.label = 'L1outproj'
            ov = out_d.rearrange("(t p) d -> p t d", p=128)
            fgs = wr[0][:].rearrange("p a b -> p (a b)")[:, 0:2048].bitcast(F32)
            A("sp", lambda e: e.dma_start(out=fgs, in_=fg_d), writes=[("wr", 0)], chan="fg")
            ost = [U[:, 20480:22528].bitcast(F32), U[:, 22528:24576].bitcast(F32)]
            ykeys = [[("ost", 0)], [("ost", 1)]]

            FG = [(0, 4), (4, 8), (8, 12), (12, 14), (14, 15), (15, 16)]

            def final_apply(gi):
                t0_, t1_ = FG[gi]
                for t in range(t0_, t1_):
                    sidx = 20 + t
                    o = ost[t % 2]
                    A("dve", lambda e, t=t, o=o, sidx=sidx: e.scalar_tensor_tensor(out=o, in0=h[:, t, :], scalar=rstd[:, sidx:sidx + 1], in1=fgs,
                                                                                   op0=ALU.mult, op1=ALU.mult),
                      reads=[HK[t], ("rs", 20 + t0_), ("wr", 0)], writes=ykeys[t % 2] + (L1K if t < 2 else []))
                    A("sp", lambda e, t=t, o=o: e.dma_start(out=ov[:, t, :], in_=o), reads=ykeys[t % 2], chan="o%d" % (t % 2))

            def final_tile_cb(t):
                for gi, (t0_, t1_) in enumerate(FG):
                    if t == t1_ - 1:
                        rms_stats([(h[:, tt, :], [HK[tt]]) for tt in range(t0_, t1_)], 20 + t0_)
                        if gi > 0:
                            final_apply(gi - 1)
                        if gi == len(FG) - 1:
                            final_apply(gi)
            out_proj(yT1, 10, wo1, ["X"], after_tile=final_tile_cb)

        frozen["f"] = False
        if do_l1:
            fw = {"sp": [("o0", 8), ("o1", 8)]}
        else:
            ov = out_d.rearrange("(t p) d -> p t d", p=128)
            for i in range(4):
                A("sp", lambda e, i=i: e.dma_start(out=ov[:, 4 * i:4 * i + 4, :], in_=h[:, 4 * i:4 * i + 4, :]), reads=HK[4 * i:4 * i + 4], chan="o%d" % (i % 2))
            fw = {"sp": [("o0", 2), ("o1", 2)]}
        S.emit_all(block, sems, chans, fw)
        if os.environ.get('KLABELS'):
            import json
            json.dump(S.labels, open(os.environ['KLABELS'], 'w'))
    return nc


def _consts():
    ident = np.eye(128, dtype=np.float32)
    k = np.arange(128)[:, None]
    q = np.arange(128)[None, :]
    diag = (q >= k).astype(np.float32)
    prev = (q <= k).astype(np.float32)
    maskA = np.concatenate([prev, diag, prev, diag], axis=1)
    maskD = np.concatenate([diag, diag, diag, diag], axis=1)
    half = 8
    invf = (np.float32(ROPE_THETA) ** (-np.arange(half, dtype=np.float32) * np.float32(2.0 / 16))).astype(np.float32)
    invf = np.broadcast_to(invf[None, :], (128, half)).copy()
    return ident, maskA, maskD, invf


def _vec_layout(v):
    L = v.shape[0]
    return np.ascontiguousarray(v.reshape(L, 8, 128).transpose(2, 0, 1))


def _prep_shared(norm_g, mem_norm_g, w_mem_kv, attn_w_in, attn_w_out, conv_w_in, conv_w, conv_w_out, final_g):
    ident, maskA, maskD, invf = _consts()
    d = {"ident": ident, "maskA": maskA, "maskD": maskD}
    ng = _vec_layout(np.asarray(norm_g, np.float32)).reshape(128, 16)
    mg = _vec_layout(np.asarray(mem_norm_g, np.float32)).reshape(128, 16)
    d["wkv"] = np.ascontiguousarray(w_mem_kv, dtype=np.float32)
    w0 = np.asarray(attn_w_in[0], np.float32)
    blocks = []
    for c in range(4):
        for g in range(3):
            o = g * 512 + c * 128
            blocks.append(np.concatenate([w0[:, o:o + 128], w0[:, 1536 + o:1536 + o + 128], w0[:, 3072 + o:3072 + o + 128]], axis=1))
    d["w0a"] = np.ascontiguousarray(np.stack(blocks))
    d["w0qm"] = np.ascontiguousarray(w0[:, 4608:4864])
    d["w0z"] = np.ascontiguousarray(np.stack([w0[:, 4864:4864 + 384], w0[:, 4864 + 384:5632]]))
    d["wout0"] = np.ascontiguousarray(attn_w_out[0], dtype=np.float32)
    w1 = np.asarray(conv_w_in[0], np.float32)
    b1 = []
    for j in range(8):
        o = j * 128
        b1.append(np.concatenate([w1[:, o:o + 128], w1[:, 1024 + o:1024 + o + 128], w1[:, 2048 + o:2048 + o + 128],
                                  w1[:, 3328 + o:3328 + o + 128]], axis=1))
    d["w1b"] = np.ascontiguousarray(np.stack(b1))
    d["w1qm"] = np.ascontiguousarray(w1[:, 3072:3328])
    d["w1z2"] = np.ascontiguousarray(w1[:, 3328 + 1024:3328 + 1280])
    d["wout1"] = np.ascontiguousarray(conv_w_out[0], dtype=np.float32)
    cw = np.asarray(conv_w[0], np.float32)
    cwl = np.ascontiguousarray(cw.reshape(3, 8, 128).transpose(2, 1, 0)).reshape(128, 24)
    d["_cpk_head"] = np.ascontiguousarray(np.concatenate([ng, mg, invf, cwl], axis=1))
    d["fg"] = np.ascontiguousarray(np.broadcast_to(np.asarray(final_g, np.float32)[None, :], (128, DM)))
    return d


def _pos_layout(pos_row):
    out = np.empty((128, 48), np.int32)
    p = np.arange(128)
    for g in range(3):
        for b in range(16):
            out[:, g * 16 + b] = pos_row[tok_start(g, b) + p * DIL[g]]
    return out


L0_KEYS = ["x", "mem", "ident", "cpk", "wkv", "maskA", "maskD", "w0a", "w0z", "w0qm", "wout0"]
L1_KEYS = ["x", "mem", "ident", "cpk", "wkv", "fg", "w1b", "w1z2", "w1qm", "wout1"]
FULL_KEYS = L0_KEYS + [k for k in L1_KEYS if k not in L0_KEYS]

_CACHE = {}


def _get_nc(mode):
    if mode not in _CACHE:
        st = os.environ.get("KSTOP")
        _CACHE[mode] = build(mode, stop=int(st) if st else None)
    return _CACHE[mode]


def kernel(x, mem, positions, norm_g, mem_norm_g, w_mem_kv, attn_w_in, attn_w_out,
           conv_w_in, conv_w, conv_w_out, final_g, _mode="full"):
    x = np.asarray(x, np.float32)
    mem = np.asarray(mem, np.float32)
    positions = np.asarray(positions, np.int32)
    B = x.shape[0]
    shared = _prep_shared(norm_g, mem_norm_g, w_mem_kv, attn_w_in, attn_w_out, conv_w_in, conv_w, conv_w_out, final_g)

    def run(mode, xs, keys):
        nc = _get_nc(mode)
        in_maps = []
        for b in range(B):
            m = dict(shared)
            m["x"] = np.ascontiguousarray(xs[b])
            m["mem"] = np.ascontiguousarray(mem[b])
            m["cpk"] = np.ascontiguousarray(np.concatenate([shared["_cpk_head"], _pos_layout(positions[b]).view(np.float32)], axis=1))
            in_maps.append({k: m[k] for k in keys})
        res = run_bass_kernel_spmd(nc, in_maps, core_ids=list(range(B)))
        return np.stack([r["out"] for r in res.results], axis=0)

    if _mode == "full":
        return run("full", x, FULL_KEYS)
    if _mode == "l0":
        return run("l0", x, L0_KEYS)
    if _mode == "unfused":
        h1 = run("l0", x, L0_KEYS)
        return run("l1", h1, L1_KEYS)
    raise ValueError(_mode)
```

```python
import os
import numpy as np
from contextlib import ExitStack
import concourse.bass as bass
import concourse.mybir as mybir
from concourse.bass_utils import run_bass_kernel_spmd

F32 = mybir.dt.float32
BF16 = mybir.dt.bfloat16
I32 = mybir.dt.int32
AF = mybir.ActivationFunctionType
ALU = mybir.AluOpType
AX = mybir.AxisListType

ENGS = ("pe", "act", "dve", "pool", "sp")

S_TOK = 2048
DM = 1024
NT = 16
DIL = (1, 4, 16)
EPS = 1e-6
ROPE_THETA = 500000.0
PI = float(np.pi)


class _Op:
    __slots__ = ("eng", "emit", "waits", "signal", "idx", "chan", "vc")


class Sched:
    SAME_WIN = {"pe": 0, "act": 2, "dve": 2, "pool": 1 << 30, "sp": 0}

    def __init__(self):
        self.streams = {e: [] for e in ENGS}
        self.last_w = {}
        self.readers = {}
        self.clock = {e: {} for e in ENGS}
        self.chan_cnt = {}
        self.label = ''
        self.labels = {e: [] for e in ENGS}

    def add(self, eng, emit, reads=(), writes=(), chan=None):
        op = _Op()
        op.eng, op.emit, op.chan, op.signal = eng, emit, chan, False
        op.idx = len(self.streams[eng])
        deps = []
        for r in reads:
            t = self.last_w.get(r)
            if t is not None:
                deps.append(t)
        for w in writes:
            t = self.last_w.get(w)
            if t is not None:
                deps.append(t)
            deps.extend(self.readers.get(w, ()))
        clk = self.clock[eng]
        waits = []
        for t in deps:
            if t[0] == "e":
                _, E, k, vc = t
                if E == eng:
                    if op.idx - k <= self.SAME_WIN[eng] and clk.get(("self", E), -1) < k:
                        waits.append(("e", E, k))
                        clk[("self", E)] = k
                        self.streams[E][k].signal = True
                    continue
                if clk.get(E, -1) >= k:
                    continue
                waits.append(("e", E, k))
                self.streams[E][k].signal = True
            else:
                _, E, k, vc = t
                if clk.get(E, -1) >= k:
                    continue
                waits.append(("d", E, k))
            for kk, vv in vc.items():
                if clk.get(kk, -1) < vv:
                    clk[kk] = vv
            clk[E] = max(clk.get(E, -1), k)
        best = {}
        for w in waits:
            key = (w[0], w[1])
            if key not in best or best[key][2] < w[2]:
                best[key] = w
        op.waits = list(best.values())
        vc = {k: v for k, v in clk.items() if not isinstance(k, tuple)}
        if chan is None:
            vc[eng] = op.idx
            tok = ("e", eng, op.idx, vc)
        else:
            n = self.chan_cnt.get(chan, 0) + 1
            self.chan_cnt[chan] = n
            tok = ("d", chan, n, vc)
        self.streams[eng].append(op)
        self.labels[eng].append(self.label)
        for r in reads:
            self.readers.setdefault(r, []).append(tok)
        for w in writes:
            self.last_w[w] = tok
            self.readers[w] = []
        return tok

    def emit_all(self, block, sems, chan_sems, final_waits):
        rank = {}
        for e in ENGS:
            c = 0
            rk = {}
            for op in self.streams[e]:
                if op.signal and op.chan is None:
                    c += 1
                    rk[op.idx] = c
            rank[e] = rk

        def run(e, engine):
            for op in self.streams[e]:
                for w in op.waits:
                    if w[0] == "e":
                        engine.wait_ge(sems[w[1]], rank[w[1]][w[2]])
                    else:
                        engine.wait_ge(chan_sems[w[1]], 16 * w[2])
                ins = op.emit(engine)
                if op.chan is not None:
                    ins.then_inc(chan_sems[op.chan], 16)
                elif op.signal:
                    ins.then_inc(sems[e], 1)
            for (C, n) in final_waits.get(e, ()):
                engine.wait_ge(chan_sems[C], 16 * n)

        names = {"pe": "tensor", "act": "scalar", "dve": "vector", "pool": "gpsimd", "sp": "sync"}
        for e in ENGS:
            if not self.streams[e] and e not in final_waits:
                continue
            getattr(block, names[e])(lambda engine, e=e: run(e, engine))


def sl(start, n, step=1):
    return slice(start, start + (n - 1) * step + 1, step)


def tok_start(g, b):
    d = DIL[g]
    nb = NT // d
    r, n = divmod(b, nb)
    return n * 128 * d + r


class _Stop(Exception):
    pass


def build(mode="full", stop=None):
    do_l0 = mode in ("full", "l0")
    do_l1 = mode in ("full", "l1")
    nc = bass.Bass("TRN2", target_bir_lowering=False)

    def din(name, shape, dt=F32):
        return nc.dram_tensor(name, list(shape), dt, kind="ExternalInput").ap()

    x_d = din("x", [S_TOK, DM])
    mem_d = din("mem", [256, DM])
    ident_d = din("ident", [128, 128])
    cpk_d = din("cpk", [128, 112])
    wkv_d = din("wkv", [2, DM, 512])
    if do_l0:
        maskA_d = din("maskA", [128, 512])
        maskD_d = din("maskD", [128, 512])
        w0a_d = din("w0a", [12, DM, 384])
        w0z_d = din("w0z", [2, DM, 384])
        w0qm_d = din("w0qm", [DM, 256])
        wout0_d = din("wout0", [768, DM])
    if do_l1:
        fg_d = din("fg", [128, DM])
        w1b_d = din("w1b", [8, DM, 512])
        w1z2_d = din("w1z2", [DM, 256])
        w1qm_d = din("w1qm", [DM, 256])
        wout1_d = din("wout1", [1280, DM])
    out_d = nc.dram_tensor("out", [S_TOK, DM], F32, kind="ExternalOutput").ap()

    S = Sched()
    frozen = {"f": False}

    def A(eng, emit, reads=(), writes=(), chan=None):
        if frozen["f"]:
            return None
        return S.add(eng, emit, reads, writes, chan)

    def stage(n):
        S.label = 'st%d' % n
        if stop is not None and n >= stop:
            frozen["f"] = True
    with ExitStack() as es:
        def sb(name, shape, dt):
            return es.enter_context(nc.sbuf_tensor(name, list(shape), dt))

        h = sb("h", [128, NT, DM], F32)
        hnT = sb("hnT", [128, 8, S_TOK], BF16)
        U = sb("U", [128, 26624], BF16)
        wr = [sb("wr%d" % i, [128, 8, 512], BF16) for i in range(2)]
        memnT = sb("memnT", [128, 8, 256], BF16)
        kmT = sb("kmT", [128, 2, 256], BF16)
        vm = sb("vm", [128, 2, 256], BF16)
        identb = sb("identb", [128, 128], BF16)
        ones64 = sb("ones64", [128, 64], BF16)
        cpk = sb("cpks", [128, 112], F32)
        ngs = cpk[:, 0:16].rearrange("p (l k) -> p l k", l=2)
        mgs = cpk[:, 16:32].rearrange("p (l k) -> p l k", l=2)
        invfs = cpk[:, 32:40]
        cws = cpk[:, 40:64].rearrange("p (j k) -> p j k", j=8)
        pos_i = cpk[:, 64:112].bitcast(I32)
        xn = [sb("xn%d" % i, [128, DM], BF16) for i in range(2)]
        ss = sb("ss", [128, 40], F32)
        rstd = sb("rstd", [128, 40], F32)
        Ptt = sb("Ptt", [128, 4, 512], BF16)
        Pt = [Ptt[:, i, :] for i in range(4)]
        sqjunk = Ptt[:, 0:2, :].rearrange("p a b -> p (a b)")
        szt = [sb("sz%d" % i, [128, 512], BF16) for i in range(2)]
        mtb = [sb("mtb%d" % i, [128, 512], BF16) for i in range(2)]
        X = sb("X", [128, 10240], BF16)
        if do_l0:
            qkvB = X[:, 0:6144]
            qkst = [X[:, 6144 + 1024 * i:6144 + 1024 * (i + 1)].rearrange("p (b c) -> p b c", b=4) for i in range(2)]
            rt = [X[:, 8192 + 256 * i:8192 + 256 * (i + 1)].bitcast(F32).rearrange("p (a b c) -> p a b c", a=4, b=4) for i in range(4)]
            maskAb = X[:, 9216:9728]
            maskDb = X[:, 9728:10240]
            posf = sb("posf", [128, 48], F32)
            ang = qkvB[:, 0:768].bitcast(F32).rearrange("p (a b) -> p a b", a=48)
            ang2 = qkvB[:, 768:1536].bitcast(F32).rearrange("p (a b) -> p a b", a=48)
            angi = qkvB[:, 1536:2304].bitcast(I32).rearrange("p (a b) -> p a b", a=48)
            halfpi = sb("halfpi", [128, 1], F32)
            cosT = sb("cosT", [128, 48, 8], F32)
            sinT = sb("sinT", [128, 48, 8], F32)
        banks = [es.enter_context(nc.psum_tensor("bank%d" % i, [128, 512], F32)) for i in range(8)]
        sems = {e: es.enter_context(nc.semaphore("s_" + e)) for e in ENGS}
        chan_names = ["x0", "x1", "x2", "x3", "c", "cp", "w0", "w1", "wo", "mem", "fg", "o0", "o1"]
        chans = {c: es.enter_context(nc.semaphore("c_" + c)) for c in chan_names}
        block = es.enter_context(nc.Block())

        def KB(i):
            return [("bank", i)]
        HK = [("h", t) for t in range(NT)]
        HN = [("hnT", t) for t in range(NT)]

        xv = x_d.rearrange("(t p) d -> p t d", p=128)

        def load_x(chunks=(0, 1, 2, 3)):
            for i in chunks:
                A("sp", lambda e, i=i: e.dma_start(out=h[:, 4 * i:4 * i + 4, :], in_=xv[:, 4 * i:4 * i + 4, :]),
                  writes=HK[4 * i:4 * i + 4], chan="x%d" % i)
        cl = [(cpk, cpk_d, "sp"), (identb, ident_d, "pool")]
        if do_l0:
            cl += [(maskAb, maskA_d, "pool"), (maskDb, maskD_d, "pool")]
        for (dst, src, q) in cl:
            A(q, lambda e, dst=dst, src=src: e.dma_start(out=dst[:], in_=src), writes=["constsP" if q == "pool" else "consts"],
              chan=("cp" if q == "pool" else "c"))
        load_x((0, 1))
        load_x((2, 3))
        A("pool", lambda e: e.memset(ones64[:], 1.0), writes=["ones"])
        A("dve", lambda e: e.memset(ss[:], 0.0), writes=["ss"])
        if do_l0:
            A("dve", lambda e: e.memset(halfpi[:], PI / 2), writes=["halfpi"])

        wplan = []
        if do_l0:
            wplan += [(w0a_d[0], 384), (w0a_d[1], 384), (wkv_d[0], 512)] + [(w0a_d[i], 384) for i in range(2, 12)] + [(w0qm_d, 256), (w0z_d[1], 384), (w0z_d[0], 384)]
        if do_l1:
            wplan += [(wkv_d[1], 512)] + [(w1b_d[j], 512) for j in range(8)] + [(w1qm_d, 256), (w1z2_d, 256)]
        wstate = {"cur": 0, "issued": 0}

        def _wissue():
            k = wstate["issued"]
            if k >= len(wplan):
                return
            src_ap, ncols = wplan[k]
            i = k % 2
            wstate["issued"] += 1
            A("pool", lambda e: e.dma_start(out=wr[i][:, :, 0:ncols], in_=src_ap.rearrange("(kc p) n -> p kc n", p=128)),
              writes=[("wr", i)], chan="w%d" % i)

        def wload(src_ap, ncols, prefetch=True):
            k = wstate["cur"]
            assert wplan[k][1] == ncols, (k, ncols, wplan[k][1])
            while wstate["issued"] <= min(k + (1 if prefetch else 0), len(wplan) - 1):
                _wissue()
            wstate["cur"] += 1
            return wr[k % 2], ("wr", k % 2)

        def rms_squares(tiles, s0):
            for i, (src_tile, src_keys) in enumerate(tiles):
                A("act", lambda e, src_tile=src_tile, i=i: e.activation(out=sqjunk, in_=src_tile, func=AF.Square, accum_out=ss[:, s0 + i:s0 + i + 1]),
                  reads=list(src_keys) + ["ss"], writes=[("ss", s0 + i), ("P", 0), ("P", 1)])

        def rms_rstd(n, s0):
            A("dve", lambda e: e.tensor_scalar(out=rstd[:, s0:s0 + n], in0=ss[:, s0:s0 + n], scalar1=1.0 / DM, scalar2=EPS,
                                               op0=ALU.mult, op1=ALU.add), reads=[("ss", s0 + i) for i in range(n)] + ["ss"], writes=[("rs0", s0)])
            A("act", lambda e: e.activation(out=rstd[:, s0:s0 + n], in_=rstd[:, s0:s0 + n], func=AF.Sqrt), reads=[("rs0", s0)], writes=[("rs1", s0)])
            A("dve", lambda e: e.reciprocal(out=rstd[:, s0:s0 + n], in_=rstd[:, s0:s0 + n]), reads=[("rs1", s0)], writes=[("rs", s0)])

        def rms_stats(tiles, s0):
            rms_squares(tiles, s0)
            rms_rstd(len(tiles), s0)

        def rmsnorm_T(src_tile, src_keys, gvec, dstT, dst_col0, dst_keys, sidx, s0, i):
            xb = xn[i % 2]
            xk = ("xn", i % 2)
            A("act", lambda e: e.activation(out=xb[:], in_=src_tile, func=AF.Copy, scale=rstd[:, sidx:sidx + 1]),
              reads=list(src_keys) + [("rs", s0)], writes=[xk])
            tb_ = (2, 4)[i % 2]
            pb = banks[tb_][:].bitcast(BF16)
            for kc in range(8):
                A("pe", lambda e, kc=kc: e.transpose(out=pb[:, kc * 128:(kc + 1) * 128], in_=xb[:, kc * 128:(kc + 1) * 128], identity=identb[:]),
                  reads=[xk, "constsP"], writes=KB(tb_))
            A("dve", lambda e: e.tensor_tensor(out=dstT[:, :, dst_col0:dst_col0 + 128],
                                               in0=pb.rearrange("p (k t) -> p k t", k=8),
                                               in1=gvec.unsqueeze(2).to_broadcast([128, 8, 128]), op=ALU.mult),
              reads=KB(tb_) + ["consts"], writes=list(dst_keys))

        def layer_norm_cb(l, defer_last=None):
            def apply(b):
                for t in range(4 * b, 4 * b + 4):
                    rmsnorm_T(h[:, t, :], [HK[t]], ngs[:, l, :], hnT, t * 128, [HN[t]], t, 4 * b, t)

            def cb(b):
                rms_squares([(h[:, t, :], [HK[t]]) for t in range(4 * b, 4 * b + 4)], 4 * b)
                if b >= 1:
                    rms_rstd(4, 4 * (b - 1))
                if b >= 2:
                    apply(b - 2)
                if b == 3:
                    apply(1)
                    rms_rstd(4, 12)
                    if defer_last is not None:
                        defer_last.append(lambda: (apply(2), apply(3)))
                    else:
                        apply(2)
                        apply(3)
            return cb

        def layer_norm_phase(l, units=(), after_stats1=None):
            units = list(units)
            tiles_done = 0

            def stats(b):
                rms_stats([(h[:, t, :], [HK[t]]) for t in range(4 * b, 4 * b + 4)], 4 * b)

            stats(0)
            for b in range(4):
                if b + 1 < 4:
                    stats(b + 1)
                if b == 0 and after_stats1 is not None:
                    after_stats1()
                ready = 4 * b
                while units and tiles_done < ready:
                    kind, u = units.pop(0)
                    u()
                    if kind == "t":
                        tiles_done += 1
                while units and units[0][0] != "t":
                    units.pop(0)[1]()
                for t in range(4 * b, 4 * b + 4):
                    rmsnorm_T(h[:, t, :], [HK[t]], ngs[:, l, :], hnT, t * 128, [HN[t]], t, 4 * b, t)
            for kind, u in units:
                u()

        def mem_stats_part(l, tmp_ap, tmp_keys, extra_reads):
            A("sp", lambda e: e.dma_start(out=tmp_ap, in_=mem_d.rearrange("(t p) d -> p t d", p=128)),
              reads=list(extra_reads), writes=list(tmp_keys), chan="mem")
            rms_stats([(tmp_ap[:, t, :], tmp_keys) for t in range(2)], 16 + 2 * l)

        def mem_apply_part(l, tmp_ap, tmp_keys):
            for t in range(2):
                rmsnorm_T(tmp_ap[:, t, :], tmp_keys, mgs[:, l, :], memnT, t * 128, ["memnT"], 16 + 2 * l + t, 16 + 2 * l, t)

        def mem_norm_part(l, tmp_ap, tmp_keys, extra_reads):
            mem_stats_part(l, tmp_ap, tmp_keys, extra_reads)
            mem_apply_part(l, tmp_ap, tmp_keys)

        def mem_kv_part(l):
            wt, wk = wload(wkv_d[l], 512)
            for mc in range(2):
                for kc in range(8):
                    A("pe", lambda e, mc=mc, kc=kc: e.matmul(banks[0][:, mc * 256:(mc + 1) * 256], lhsT=wt[:, kc, mc * 128:(mc + 1) * 128],
                                                             rhs=memnT[:, kc, :], start=(kc == 0), stop=(kc == 7)),
                      reads=[wk, "memnT"], writes=KB(0))
            A("act", lambda e: e.activation(out=kmT[:].rearrange("p a b -> p (a b)"), in_=banks[0][:, :], func=AF.Copy),
              reads=KB(0), writes=["kmT"])
            for mb in range(2):
                for kc in range(8):
                    A("pe", lambda e, mb=mb, kc=kc: e.matmul(banks[1][:, mb * 256:(mb + 1) * 256], lhsT=memnT[:, kc, mb * 128:(mb + 1) * 128],
                                                             rhs=wt[:, kc, 256:512], start=(kc == 0), stop=(kc == 7)),
                      reads=[wk, "memnT"], writes=KB(1))
            A("dve", lambda e: e.tensor_copy(out=vm[:].rearrange("p a b -> p (a b)"), in_=banks[1][:, :]),
              reads=KB(1), writes=["vm"])

        def make_memattn(qmT, qm_keys, yT, ychunk0, rd_ap, rd_key, rd_first=()):
            out = []
            sbank = {(0, 0): 3, (1, 0): 4, (0, 1): 5, (1, 1): 6}
            for it in range(8):
                mc, qd = divmod(it, 4)
                qs = slice(qd * 512, (qd + 1) * 512)

                def S_fn(mc=mc, qs=qs):
                    for mb in range(2):
                        for hh in range(2):
                            hp = slice(hh * 64, hh * 64 + 64)
                            bk = sbank[(hh, mb)]
                            A("pe", lambda e, mb=mb, hp=hp, bk=bk: e.matmul(
                                banks[bk][:, :], lhsT=kmT[hp, mc, mb * 128:(mb + 1) * 128], rhs=qmT[hp, mc, qs], start=True, stop=True),
                              reads=["kmT"] + list(qm_keys[mc]), writes=KB(bk))
                    for mb in range(2):
                        for hh in range(2):
                            bk = sbank[(hh, mb)]
                            pi = hh * 2 + mb
                            A("act", lambda e, bk=bk, pi=pi: e.activation(out=Pt[pi][:], in_=banks[bk][:, :], func=AF.Exp, scale=0.125),
                              reads=KB(bk), writes=[("P", pi)])

                def PV_fn(mc=mc, qd=qd, qs=qs, it=it):
                    for hh in range(2):
                        ph = slice(hh * 64, hh * 64 + 64)
                        tp = (0, 64) if hh else None
                        for mb in range(2):
                            pi = hh * 2 + mb
                            A("pe", lambda e, mb=mb, hh=hh, ph=ph, pi=pi, tp=tp: e.matmul(
                                banks[7][ph, :], lhsT=vm[:, mb, mc * 128 + hh * 64:mc * 128 + hh * 64 + 64], rhs=Pt[pi][:],
                                start=(mb == 0), stop=(mb == 1), tile_position=tp),
                              reads=["vm", ("P", pi)], writes=KB(7))
                        for mb in range(2):
                            pi = hh * 2 + mb
                            A("pe", lambda e, mb=mb, ph=ph, pi=pi, tp=tp: e.matmul(
                                banks[2][ph, :], lhsT=ones64[:], rhs=Pt[pi][:], start=(mb == 0), stop=(mb == 1), tile_position=tp),
                              reads=["ones", ("P", pi)], writes=KB(2))
                    rdv = rd_ap[:, (it % 2) * 512:(it % 2) * 512 + 512]
                    rk = (rd_key, it % 2)
                    tb = mtb[it % 2]
                    tk = ("mtb", it % 2)
                    A("act", lambda e: e.activation(out=tb[:], in_=banks[7][:, :], func=AF.Copy, scale=0.5), reads=KB(7), writes=[tk])
                    A("act", lambda e: e.activation(out=rdv, in_=banks[2][:, :], func=AF.Copy), reads=KB(2), writes=[rk] + (list(rd_first) if it < 2 else []))
                    A("dve", lambda e: e.reciprocal(out=rdv, in_=rdv), reads=[rk], writes=[rk])
                    A("dve", lambda e: e.tensor_tensor(out=tb[:], in0=tb[:], in1=rdv, op=ALU.mult), reads=[tk, rk], writes=[tk])
                    A("pool", lambda e: e.tensor_tensor(out=yT[:, ychunk0 + mc, qs], in0=yT[:, ychunk0 + mc, qs], in1=tb[:], op=ALU.mult),
                      reads=[tk, ("y", ychunk0 + mc, qd)], writes=[("y", ychunk0 + mc, qd)])
                out.append((S_fn, PV_fn))
            return out

        def proj_unit(wt, wk, col0, qd, bk, evac):
            for kc in range(8):
                A("pe", lambda e, kc=kc: e.matmul(banks[bk][:, :], lhsT=wt[:, kc, col0:col0 + 128],
                                                 rhs=hnT[:, kc, qd * 512:(qd + 1) * 512], start=(kc == 0), stop=(kc == 7)),
                  reads=[wk] + HN[4 * qd:4 * qd + 4], writes=KB(bk))
            evac(qd, banks[bk], KB(bk))

        def proj_fm(wt, wk, col0, evac, bank_ids):
            for qd in range(4):
                bk = bank_ids[qd % len(bank_ids)]
                for kc in range(8):
                    A("pe", lambda e, kc=kc, qd=qd, bk=bk: e.matmul(banks[bk][:, :], lhsT=wt[:, kc, col0:col0 + 128],
                                                                    rhs=hnT[:, kc, qd * 512:(qd + 1) * 512], start=(kc == 0), stop=(kc == 7)),
                      reads=[wk] + HN[4 * qd:4 * qd + 4], writes=KB(bk))
                evac(qd, banks[bk], KB(bk))

        def load_wout(wo, wout_d, wo_keys, extra_reads=()):
            A("pool", lambda e: e.dma_start(out=wo, in_=wout_d.rearrange("(c p) n -> p c n", p=128)),
              reads=list(extra_reads), writes=list(wo_keys), chan="wo")

        def out_proj(yT, nchunk, wo, wo_keys, after_batch=None, after_tile=None):
            for t in range(NT):
                for hf in range(2):
                    bk = (0, 1, 7, 3)[(t * 2 + hf) % 4]
                    for c in range(nchunk):
                        A("pe", lambda e, t=t, hf=hf, c=c, bk=bk: e.matmul(banks[bk][:, :], lhsT=yT[:, c, t * 128:(t + 1) * 128],
                                                                           rhs=wo[:, c, hf * 512:(hf + 1) * 512], start=(c == 0), stop=(c == nchunk - 1)),
                          reads=list(wo_keys) + [("y", c, t // 4)], writes=KB(bk))
                    A("dve", lambda e, t=t, hf=hf, bk=bk: e.tensor_tensor(out=h[:, t, hf * 512:(hf + 1) * 512], in0=h[:, t, hf * 512:(hf + 1) * 512],
                                                                          in1=banks[bk][:, :], op=ALU.add),
                      reads=KB(bk) + [HK[t]], writes=[HK[t]])
                if after_batch is not None and t % 4 == 3:
                    after_batch(t // 4)
                if after_tile is not None:
                    after_tile(t)

        if do_l0:
            yT0 = U[:, 0:12288].rearrange("p (c t) -> p c t", c=6)
            qT = U[:, 12288:14336]
            kT = U[:, 14336:16384]
            Vt = U[:, 16384:18432].rearrange("p (b c) -> p b c", b=16)
            accn = U[:, 18432:22528].bitcast(F32)
            accd = U[:, 22528:26624].bitcast(F32)
            qmT0 = U[:, 12288:16384].rearrange("p (c t) -> p c t", c=2)
            memtmp0 = U[:, 18432:22528].bitcast(F32).rearrange("p (t d) -> p t d", t=2)

            stage(1)
            memtmp0 = U[:, 8192:12288].bitcast(F32).rearrange("p (t d) -> p t d", t=2)
            MK0 = [("y", 4 + i, qd) for i in range(2) for qd in range(4)]

            QB = qkvB
            sets = [
                dict(qT=U[:, 12288:14336], kT=U[:, 14336:16384], V=U[:, 16384:18432].rearrange("p (b c) -> p b c", b=16), kq="qT", kk="kT", kv="V", ix=0),
                dict(qT=QB[:, 0:2048], kT=QB[:, 2048:4096], V=QB[:, 4096:6144].rearrange("p (b c) -> p b c", b=16), kq="qT1", kk="kT1", kv="V1", ix=1),
            ]
            def GK(st0, which, b4):
                return (st0[which], b4)

            def GKALL(st0, which):
                return [(st0[which], b4) for b4 in range(4)]
            tile_ctr = {"n": 0}
            TB = (0, 1, 7)

            def make_inproj(k):
                c, g = divmod(k, 3)
                d = DIL[g]
                st_ = sets[k % 2]
                wt, wk = wload(w0a_d[k], 384)
                units = []
                for b4 in range(4):
                    stg = qkst[b4 % 2]
                    stk = ("qkst", b4 % 2)
                    for bi in range(4):
                        def tile_unit(b4=b4, bi=bi, stg=stg, stk=stk):
                            b = b4 * 4 + bi
                            t0 = tok_start(g, b)
                            if g == 0:
                                hn_keys = [HN[b]]
                            elif g == 1:
                                hn_keys = HN[4 * (b % 4):4 * (b % 4) + 4]
                            else:
                                hn_keys = HN
                            bk = TB[tile_ctr["n"] % 3]
                            tile_ctr["n"] += 1
                            for kc in range(8):
                                A("pe", lambda e, kc=kc, t0=t0, bk=bk: e.matmul(banks[bk][:, 0:384], lhsT=hnT[:, kc, sl(t0, 128, d)],
                                                                               rhs=wt[:, kc, 0:384], start=(kc == 0), stop=(kc == 7)),
                                  reads=[wk] + hn_keys, writes=KB(bk))
                            Vt = st_["V"]
                            if b % 2 == 0:
                                A("act", lambda e: e.activation(out=stg[:, bi, :], in_=banks[bk][:, 0:256], func=AF.Copy), reads=KB(bk), writes=[stk])
                                A("act", lambda e: e.activation(out=Vt[:, b, :], in_=banks[bk][:, 256:384], func=AF.Copy), reads=KB(bk), writes=[GK(st_, "kv", b // 4)])
                            else:
                                A("dve", lambda e: e.tensor_copy(out=stg[:, bi, :], in_=banks[bk][:, 0:256]), reads=KB(bk), writes=[stk])
                                A("dve", lambda e: e.tensor_copy(out=Vt[:, b, :], in_=banks[bk][:, 256:384]), reads=KB(bk), writes=[GK(st_, "kv", b // 4)])
                            if bi == 3:
                                sv = stg[:].rearrange("p b (h d) -> p b h d", h=4)
                                t1 = sv[:, :, :, 0:8]
                                t2 = sv[:, :, :, 8:16]
                                col = g * 16 + b4 * 4
                                cb = cosT[:, col:col + 4, :].unsqueeze(2).to_broadcast([128, 4, 4, 8])
                                sbb = sinT[:, col:col + 4, :].unsqueeze(2).to_broadcast([128, 4, 4, 8])
                                A("dve", lambda e: e.tensor_tensor(out=rt[0][:], in0=t1, in1=cb, op=ALU.mult), reads=[stk, "cosT"], writes=["rt0"])
                                A("pool", lambda e: e.tensor_tensor(out=rt[2][:], in0=t2, in1=cb, op=ALU.mult), reads=[stk, "cosT"], writes=["rt2"])
                                A("dve", lambda e: e.tensor_tensor(out=rt[1][:], in0=t2, in1=sbb, op=ALU.mult), reads=[stk, "sinT"], writes=["rt1"])
                                A("pool", lambda e: e.tensor_tensor(out=rt[3][:], in0=t1, in1=sbb, op=ALU.mult), reads=[stk, "sinT"], writes=["rt3"])
                                A("dve", lambda e: e.tensor_tensor(out=t1, in0=rt[0][:], in1=rt[1][:], op=ALU.subtract), reads=["rt0", "rt1", "rt3"], writes=[stk])
                                A("pool", lambda e: e.tensor_tensor(out=t2, in0=rt[2][:], in1=rt[3][:], op=ALU.add), reads=["rt2", "rt3", "rt1"], writes=[stk])
                        units.append(("t", tile_unit))

                    def tr_unit(b4=b4, stg=stg, stk=stk):
                        pb = banks[2][:].bitcast(BF16)
                        for bi in range(4):
                            A("pe", lambda e, bi=bi: e.transpose(out=pb[:, bi * 128:(bi + 1) * 128], in_=stg[:, bi, 0:128], identity=identb[:]),
                              reads=[stk, "constsP"], writes=KB(2))
                            A("pe", lambda e, bi=bi: e.transpose(out=pb[:, 512 + bi * 128:512 + (bi + 1) * 128], in_=stg[:, bi, 128:256], identity=identb[:]),
                              reads=[stk, "constsP"], writes=KB(2))
                        qT_, kT_ = st_["qT"], st_["kT"]
                        if b4 % 2 == 0:
                            A("act", lambda e: e.activation(out=qT_[:, b4 * 512:(b4 + 1) * 512], in_=pb[:, 0:512], func=AF.Copy), reads=KB(2), writes=[GK(st_, "kq", b4)])
                            A("act", lambda e: e.activation(out=kT_[:, b4 * 512:(b4 + 1) * 512], in_=pb[:, 512:1024], func=AF.Copy), reads=KB(2), writes=[GK(st_, "kk", b4)])
                        else:
                            A("dve", lambda e: e.tensor_copy(out=qT_[:, b4 * 512:(b4 + 1) * 512], in_=pb[:, 0:512]), reads=KB(2), writes=[GK(st_, "kq", b4)])
                            A("dve", lambda e: e.tensor_copy(out=kT_[:, b4 * 512:(b4 + 1) * 512], in_=pb[:, 512:1024]), reads=KB(2), writes=[GK(st_, "kk", b4)])
                    units.append(("r", tr_unit))
                tiles = [u for u in units if u[0] == "t"]
                trs = [u for u in units if u[0] == "r"]
                order = tiles[0:8] + [trs[0]] + tiles[8:12] + [trs[1]] + tiles[12:16] + [trs[2]]
                return order, trs[3][1]

            def make_attn(k):
                c, g = divmod(k, 3)
                d = DIL[g]
                nb = NT // d
                st_ = sets[k % 2]
                qT, kT, Vt = st_["qT"], st_["kT"], st_["V"]
                if g < 2:
                    iters = []
                    for r in range(d):
                        for nh in range(nb // 2):
                            iters.append([(r * nb + 2 * nh + s, (2 * nh + s) > 0) for s in range(2)])
                    mask = maskAb
                else:
                    iters = [[(4 * i + s, False) for s in range(4)] for i in range(4)]
                    mask = maskDb
                out = []
                for iti, qbs in enumerate(iters):
                    lo = 512
                    offs = []
                    for s, (b, hp_) in enumerate(qbs):
                        if g < 2:
                            o_prev, o_diag = s * 256, s * 256 + 128
                        else:
                            o_prev, o_diag = None, s * 128
                        offs.append((o_prev, o_diag))
                        lo = min(lo, o_prev if hp_ else o_diag)
                    par = iti % 2

                    def S_fn(qbs=qbs, offs=offs, lo=lo, par=par):
                        for s, (b, hp_) in enumerate(qbs):
                            o_prev, o_diag = offs[s]
                            for part in ((0, 1) if hp_ else (1,)):
                                for hh in range(2):
                                    hp = slice(hh * 64, hh * 64 + 64)
                                    bk = 3 + hh
                                    kb = b - 1 if part == 0 else b
                                    o = o_prev if part == 0 else o_diag
                                    A("pe", lambda e, hp=hp, bk=bk, b=b, kb=kb, o=o, hh=hh: e.matmul(banks[bk][:, o:o + 128], lhsT=kT[hp, kb * 128:(kb + 1) * 128],
                                                                                             rhs=qT[hp, b * 128:(b + 1) * 128], start=True, stop=True,
                                                                                             tile_position=(64 * hh, 0)),
                                      reads=[GK(st_, "kq", b // 4), GK(st_, "kk", kb // 4)], writes=KB(bk))
                        for hh in range(2):
                            bk = 3 + hh
                            pi = par * 2 + hh
                            A("act", lambda e, bk=bk, pi=pi: e.activation(out=Pt[pi][:, lo:512], in_=banks[bk][:, lo:512], func=AF.Exp, scale=0.125),
                              reads=KB(bk), writes=[("P", pi)])
                            A("dve" if hh == 0 else "pool", lambda e, pi=pi: e.tensor_tensor(out=Pt[pi][:, lo:512], in0=Pt[pi][:, lo:512],
                                                                                          in1=mask[:, lo:512], op=ALU.mult),
                              reads=[("P", pi), "constsP"], writes=[("P", pi)])

                    def PV_fn(qbs=qbs, offs=offs, par=par, iti=iti):
                        onb, odb = 5, 6
                        for (obank, lv) in ((onb, True), (odb, False)):
                            for s, (b, hp_) in enumerate(qbs):
                                o_prev, o_diag = offs[s]
                                oc = s * 128
                                for part in ((0, 1) if hp_ else (1,)):
                                    for hh in range(2):
                                        ph = slice(hh * 64, hh * 64 + 64)
                                        tp = (0, 64 * hh)
                                        pi = par * 2 + hh
                                        kb = b - 1 if part == 0 else b
                                        o = o_prev if part == 0 else o_diag
                                        st_flag = (part == 0) or (not hp_)
                                        sp_flag = (part == 1)
                                        A("pe", lambda e, ph=ph, tp=tp, pi=pi, kb=kb, o=o, oc=oc, obank=obank, lv=lv, hh=hh, st_flag=st_flag, sp_flag=sp_flag: e.matmul(
                                            banks[obank][ph, oc:oc + 128], lhsT=(Vt[:, kb, hh * 64:hh * 64 + 64] if lv else ones64[:]),
                                            rhs=Pt[pi][:, o:o + 128], start=st_flag, stop=sp_flag, tile_position=tp),
                                          reads=[GK(st_, "kv", kb // 4), "ones", ("P", pi)], writes=KB(obank))
                        nq = len(qbs) * 128
                        if g == 0:
                            dn = accn[:, iti * 256:iti * 256 + 256]
                            dd = accd[:, iti * 256:iti * 256 + 256]
                        elif g == 1:
                            r, nh = divmod(iti, 2)
                            dn = accn[:, sl(1024 * nh + r, 256, 4)]
                            dd = accd[:, sl(1024 * nh + r, 256, 4)]
                        else:
                            dn = accn.rearrange("p (i r) -> p r i", r=16)[:, 4 * iti:4 * iti + 4, :]
                            dd = accd.rearrange("p (i r) -> p r i", r=16)[:, 4 * iti:4 * iti + 4, :]
                        srcn = banks[onb][:, 0:nq]
                        srcd = banks[odb][:, 0:nq]
                        if g == 2:
                            srcn = srcn.rearrange("p (r i) -> p r i", r=4)
                            srcd = srcd.rearrange("p (r i) -> p r i", r=4)
                        if g == 0:
                            A("act", lambda e: e.activation(out=dn, in_=srcn, func=AF.Copy), reads=KB(onb), writes=[("accn", iti // 2)])
                            A("dve", lambda e: e.tensor_copy(out=dd, in_=srcd), reads=KB(odb), writes=[("accd", iti // 2)])
                        else:
                            AN = [("accn", q_) for q_ in range(4)]
                            AD = [("accd", q_) for q_ in range(4)]
                            A("dve", lambda e: e.tensor_tensor(out=dn, in0=dn, in1=srcn, op=ALU.add), reads=KB(onb) + AN, writes=AN)
                            A("dve", lambda e: e.tensor_tensor(out=dd, in0=dd, in1=srcd, op=ALU.add), reads=KB(odb) + AD, writes=AD)
                    out.append((S_fn, PV_fn))
                return out

            NK = 12
            S.label = "norm0"
            def rope_tables():
                A("dve", lambda e: e.tensor_copy(out=posf[:], in_=pos_i[:]), reads=["consts"], writes=["posf"])
                A("dve", lambda e: e.tensor_tensor(out=ang[:], in0=posf[:].unsqueeze(2).to_broadcast([128, 48, 8]),
                                                   in1=invfs[:].unsqueeze(1).to_broadcast([128, 48, 8]), op=ALU.mult),
                  reads=["posf", "consts"], writes=["ang"])
                C1 = 6.28125
                C2 = 2.0 * np.pi - C1
                A("dve", lambda e: e.tensor_scalar(out=ang2[:], in0=ang[:], scalar1=1.0 / (2 * PI), scalar2=None, op0=ALU.mult),
                  reads=["ang"], writes=["ang2"])
                A("dve", lambda e: e.tensor_copy(out=angi[:], in_=ang2[:]), reads=["ang2"], writes=["angi"])
                A("dve", lambda e: e.tensor_copy(out=ang2[:], in_=angi[:]), reads=["angi", "ang2"], writes=["angf"])
                A("dve", lambda e: e.scalar_tensor_tensor(out=ang[:], in0=ang2[:], scalar=-C1, in1=ang[:], op0=ALU.mult, op1=ALU.add),
                  reads=["angf", "ang"], writes=["r1"])
                A("dve", lambda e: e.scalar_tensor_tensor(out=ang[:], in0=ang2[:], scalar=-C2, in1=ang[:], op0=ALU.mult, op1=ALU.add),
                  reads=["angf", "r1"], writes=["frac"])
                A("act", lambda e: e.activation(out=sinT[:], in_=ang[:], func=AF.Sin, scale=0.5), reads=["frac"], writes=["sh"])
                A("act", lambda e: e.activation(out=cosT[:], in_=ang[:], func=AF.Sin, scale=-0.5, bias=halfpi[:, 0:1]), reads=["frac", "halfpi"], writes=["ch"])
                A("dve", lambda e: e.tensor_tensor(out=ang2[:], in0=sinT[:], in1=sinT[:], op=ALU.mult), reads=["sh", "angf", "frac"], writes=["s2"])
                A("dve", lambda e: e.scalar_tensor_tensor(out=sinT[:], in0=sinT[:], scalar=2.0, in1=cosT[:], op0=ALU.mult, op1=ALU.mult),
                  reads=["sh", "ch", "s2"], writes=["sinT"])
                A("dve", lambda e: e.tensor_scalar(out=cosT[:], in0=ang2[:], scalar1=-2.0, scalar2=1.0, op0=ALU.mult, op1=ALU.add),
                  reads=["s2", "sinT"], writes=["cosT"])

            units0, carry = make_inproj(0)
            pending_norm = []
            layer_norm_phase(0, units=units0, after_stats1=rope_tables)
            mem_stats_part(0, memtmp0, MK0, [])
            for k in range(NK):
                c, g = divmod(k, 3)
                S.label = "attn%d" % k
                its = make_attn(k)
                nxt, nxt_carry = make_inproj(k + 1) if k + 1 < NK else ([], None)
                n_it = len(its)
                if k == NK - 1:
                    wq_t, wq_k = wload(w0qm_d, 256)
                    qn = 0
                    for mc in range(2):
                        for qd in range(4):
                            def qm_unit(mc=mc, qd=qd, bk=(0, 1)[qn % 2]):
                                def ev_qm(qd_, bank, bkey):
                                    A("act", lambda e: e.activation(out=qmT0[:, mc, qd_ * 512:(qd_ + 1) * 512], in_=bank[:, :], func=AF.Copy),
                                      reads=list(bkey), writes=GKALL(sets[0], ("kq", "kk")[mc]))
                                proj_unit(wq_t, wq_k, mc * 128, qd, bk, ev_qm)
                            nxt.append(("t", qm_unit))
                            qn += 1
                    qm_done = True
                per = [[] for _ in range(n_it)]
                tiles_per = max(1, sum(1 for kd, _ in nxt if kd == "t") // n_it)
                cnt = 0
                slot = 0
                for (kind, u) in nxt:
                    per[min(slot, n_it - 1)].append(u)
                    if kind == "t":
                        cnt += 1
                        if cnt % tiles_per == 0:
                            slot += 1
                def make_norm(c_):
                    fns = []
                    for q_ in range(4):
                        def fn(q_=q_):
                            cs = slice(q_ * 512, (q_ + 1) * 512)
                            A("act", lambda e: e.activation(out=accd[:, cs], in_=accd[:, cs], func=AF.Ln), reads=[("accd", q_)], writes=[("accd", q_)])
                            A("act", lambda e: e.activation(out=accd[:, cs], in_=accd[:, cs], func=AF.Exp, scale=-1.0), reads=[("accd", q_)], writes=[("accd", q_)])
                            A("dve", lambda e: e.scalar_tensor_tensor(out=yT0[:, c_, cs], in0=accn[:, cs], scalar=0.5, in1=accd[:, cs], op0=ALU.mult, op1=ALU.mult),
                              reads=[("accn", q_), ("accd", q_)], writes=[("y", c_, q_)])
                        fns.append(fn)
                    return fns

                its[0][0]()
                for i in range(n_it):
                    if i + 1 < n_it:
                        its[i + 1][0]()
                    if i == 0 and carry is not None:
                        carry()
                    for u in per[i]:
                        u()
                    if g == 0 and pending_norm and i % 2 == 0:
                        pending_norm.pop(0)()
                    its[i][1]()
                carry = nxt_carry
                if k == 0:
                    S.label = "mem0"
                    mem_apply_part(0, memtmp0, MK0)
                    mem_kv_part(0)
                if g == 2:
                    pending_norm = make_norm(c)
            S.label = "qm0"
            ACCK = [("accn", q_) for q_ in range(4)] + [("accd", q_) for q_ in range(4)]
            wo0 = U[:, 18432:24576].rearrange("p (c n) -> p c n", c=6)
            assert qm_done
            S.label = "z0"
            zstate = {"zi": 0, "n": 0}
            zunits = []
            for zb in (1, 0):
                def lazy_w(zb=zb, cache={}):
                    if "w" not in cache:
                        cache["w"] = wload(w0z_d[zb], 384)
                    return cache["w"]
                for zc3 in ((1, 2, 0) if zb == 1 else (0, 1, 2)):
                    zc = zb * 3 + zc3
                    for qd in range(4):
                        def zunit(zc=zc, zc3=zc3, qd=qd, lazy_w=lazy_w):
                            wt, wk = lazy_w()

                            def ev_z(qd, bank, bkey):
                                szb = szt[zstate["zi"] % 2]
                                szk = ("sz", zstate["zi"] % 2)
                                zstate["zi"] += 1
                                qs_ = slice(qd * 512, (qd + 1) * 512)
                                A("act", lambda e: e.activation(out=szb[:], in_=bank[:, :], func=AF.Tanh, scale=0.5), reads=list(bkey), writes=[szk])
                                if zc >= 4:
                                    A("dve", lambda e: e.scalar_tensor_tensor(out=yT0[:, zc, qs_], in0=szb[:], scalar=1.0, in1=bank[:, :], op0=ALU.add, op1=ALU.mult),
                                      reads=list(bkey) + [szk], writes=[("y", zc, qd)])
                                    return
                                A("dve", lambda e: e.scalar_tensor_tensor(out=szb[:], in0=szb[:], scalar=1.0, in1=bank[:, :], op0=ALU.add, op1=ALU.mult),
                                  reads=list(bkey) + [szk], writes=[szk])
                                A("pool", lambda e: e.tensor_tensor(out=yT0[:, zc, qs_], in0=yT0[:, zc, qs_], in1=szb[:], op=ALU.mult),
                                  reads=[szk, ("y", zc, qd)], writes=[("y", zc, qd)])
                            bk = (0, 1)[zstate["n"] % 2]
                            zstate["n"] += 1
                            proj_unit(wt, wk, zc3 * 128, qd, bk, ev_z)
                        zunits.append(zunit)
            for ui, u in enumerate(zunits[:8]):
                u()
                if pending_norm:
                    pending_norm.pop(0)()
            while pending_norm:
                pending_norm.pop(0)()
            if do_l1:
                memtmpX = X[:, 0:4096].bitcast(F32).rearrange("p (t d) -> p t d", t=2)
                mem_stats_part(1, memtmpX, GKALL(sets[1], "kq") + GKALL(sets[1], "kk"), [])
            load_wout(wo0, wout0_d, ACCK)
            rest = zunits[8:]
            rd0 = U[:, 16384:18432].bitcast(F32)
            mits = make_memattn(qmT0, [GKALL(sets[0], "kq"), GKALL(sets[0], "kk")], yT0, 4, rd0, "rdA", rd_first=GKALL(sets[0], "kv"))
            S.label = "memattn0"
            for i, (S_fn, PV_fn) in enumerate(mits):
                S_fn()
                for u in rest[2 * i:2 * i + 2]:
                    u()
                PV_fn()
            S.label = "outproj0"
            l1_deferred = []
            out_proj(yT0, 6, wo0, ACCK, after_batch=(layer_norm_cb(1, defer_last=l1_deferred) if do_l1 else None))
            if do_l1:
                S.label = "mem1n"
                mem_apply_part(1, memtmpX, GKALL(sets[1], "kq") + GKALL(sets[1], "kk"))
                S.label = "mem1kv"
                mem_kv_part(1)

        if do_l1:
            yT1 = U[:, 0:20480].rearrange("p (c t) -> p c t", c=10)
            cgs = U[:, 20480:21504]
            a_sb = U[:, 21504:23560].bitcast(F32)
            cv = U[:, 23560:25608].bitcast(F32)
            memtmp1 = U[:, 20480:24576].bitcast(F32).rearrange("p (t d) -> p t d", t=2)
            qmT1 = U[:, 20480:24576].rearrange("p (c t) -> p c t", c=2)
            rd1 = U[:, 24576:26624].bitcast(F32)
            L1K = ["cg", "a", "cv"]

            S.label = 'L1mem'
            if not do_l0:
                mem_norm_part(1, memtmp1, L1K, [])
                mem_kv_part(1)
            S.label = 'L1norm'
            if not do_l0:
                layer_norm_phase(1)
            wo1 = X[:, :].rearrange("p (c n) -> p c n", c=10)
            load_wout(wo1, wout1_d, ["X"] + ((GKALL(sets[1], "kq") + GKALL(sets[1], "kk")) if do_l0 else []), HK)

            zi = 0
            for j in range(8):
                S.label = 'L1conv%d' % j
                wt, wk = wload(w1b_d[j], 512)
                for half in range(2):
                    tk0 = half * 1024
                    hnk = HN[8 * half:8 * half + 8]

                    def proj_half(col0, evac, bank_ids, wt=wt, wk=wk, tk0=tk0, hnk=hnk):
                        for q2 in range(2):
                            bk = bank_ids[q2]
                            for kc in range(8):
                                A("pe", lambda e, kc=kc, q2=q2, bk=bk, wt=wt, tk0=tk0, col0=col0: e.matmul(banks[bk][:, :], lhsT=wt[:, kc, col0:col0 + 128],
                                                                                rhs=hnT[:, kc, tk0 + q2 * 512:tk0 + (q2 + 1) * 512], start=(kc == 0), stop=(kc == 7)),
                                  reads=[wk] + hnk, writes=KB(bk))
                            evac(q2, banks[bk], KB(bk))

                    if half == 0:
                        A("dve", lambda e: e.memset(a_sb[:, 0:2], 0.0), writes=["a"])
                    else:
                        A("dve", lambda e: e.tensor_copy(out=a_sb[:, 0:2], in_=a_sb[:, 1024:1026]), reads=["a", "cv"], writes=["a"])
                    proj_half(128, lambda q2, bank, bkey: A("act", lambda e: e.activation(out=cgs[:, q2 * 512:(q2 + 1) * 512], in_=bank[:, :], func=AF.Copy),
                                                            reads=list(bkey), writes=["cg"]), (0, 1))
                    proj_half(256, lambda q2, bank, bkey: A("dve", lambda e: e.tensor_tensor(out=a_sb[:, 2 + q2 * 512:2 + (q2 + 1) * 512], in0=bank[:, :],
                                                                                             in1=cgs[:, q2 * 512:(q2 + 1) * 512], op=ALU.mult),
                                                            reads=list(bkey) + ["cg"], writes=["a"]), (3, 4))
                    A("act", lambda e, j=j: e.activation(out=cv[:, :], in_=a_sb[:, 2:1026], func=AF.Copy, scale=cws[:, j, 2:3]), reads=["a", "consts"], writes=["cv"])
                    A("dve", lambda e, j=j: e.scalar_tensor_tensor(out=cv[:, :], in0=a_sb[:, 1:1025], scalar=cws[:, j, 1:2], in1=cv[:, :], op0=ALU.mult, op1=ALU.add),
                      reads=["a", "cv", "consts"], writes=["cv"])
                    A("dve", lambda e, j=j: e.scalar_tensor_tensor(out=cv[:, :], in0=a_sb[:, 0:1024], scalar=cws[:, j, 0:1], in1=cv[:, :], op0=ALU.mult, op1=ALU.add),
                      reads=["a", "cv", "consts"], writes=["cv"])
                    proj_half(0, lambda q2, bank, bkey: A("dve", lambda e: e.tensor_tensor(out=cv[:, q2 * 512:(q2 + 1) * 512], in0=bank[:, :],
                                                                                           in1=cv[:, q2 * 512:(q2 + 1) * 512], op=ALU.mult),
                                                          reads=list(bkey) + ["cv"], writes=["cv"]), (5, 6))

                    def ev_z1(q2, bank, bkey, j=j, tk0=tk0, half=half):
                        nonlocal zi
                        szb = szt[zi % 2]
                        szk = ("sz", zi % 2)
                        zi += 1
                        A("act", lambda e: e.activation(out=szb[:], in_=bank[:, :], func=AF.Silu), reads=list(bkey), writes=[szk])
                        A("pool", lambda e: e.tensor_tensor(out=yT1[:, j, tk0 + q2 * 512:tk0 + (q2 + 1) * 512], in0=cv[:, q2 * 512:(q2 + 1) * 512],
                                                            in1=szb[:], op=ALU.mult),
                          reads=[szk, "cv"], writes=[("y", j, half * 2 + q2)])
                    proj_half(384, ev_z1, (7, 2))
                    if j == 0 and half == 0 and do_l0:
                        for fn in l1_deferred:
                            fn()

            S.label = 'L1qm'
            wt, wk = wload(w1qm_d, 256)
            for mc in range(2):
                def ev_qm1(qd, bank, bkey, mc=mc):
                    A("act", lambda e: e.activation(out=qmT1[:, mc, qd * 512:(qd + 1) * 512], in_=bank[:, :], func=AF.Copy),
                      reads=list(bkey), writes=L1K)
                proj_fm(wt, wk, mc * 128, ev_qm1, (0, 1))
            S.label = 'L1memattn'
            wz2 = wload(w1z2_d, 256)
            mits1 = make_memattn(qmT1, [[L1K[0]], [L1K[0]]], yT1, 8, rd1, "rdB", rd_first=["cv"])
            for i, (S_fn, PV_fn) in enumerate(mits1):
                S_fn()
                zc, qd = divmod(i, 4)

                def ev_z2(qd, bank, bkey, zc=zc, i=i):
                    szb = szt[i % 2]
                    szk = ("sz", i % 2)
                    A("act", lambda e: e.activation(out=szb[:], in_=bank[:, :], func=AF.Tanh, scale=0.5), reads=list(bkey), writes=[szk])
                    A("dve", lambda e: e.scalar_tensor_tensor(out=yT1[:, 8 + zc, qd * 512:(qd + 1) * 512], in0=szb[:], scalar=1.0, in1=bank[:, :],
                                                              op0=ALU.add, op1=ALU.mult),
                      reads=list(bkey) + [szk], writes=[("y", 8 + zc, qd)])
                proj_unit(wz2[0], wz2[1], zc * 128, qd, (0, 1)[i % 2], ev_z2)
                PV_fn()
            S.label = 'L1outproj'
            ov = out_d.rearrange("(t p) d -> p t d", p=128)
            fgs = wr[0][:].rearrange("p a b -> p (a b)")[:, 0:2048].bitcast(F32)
            A("sp", lambda e: e.dma_start(out=fgs, in_=fg_d), writes=[("wr", 0)], chan="fg")
            ost = [U[:, 20480:22528].bitcast(F32), U[:, 22528:24576].bitcast(F32)]
            ykeys = [[("ost", 0)], [("ost", 1)]]

            FG = [(0, 4), (4, 8), (8, 12), (12, 14), (14, 15), (15, 16)]

            def final_apply(gi):
                t0_, t1_ = FG[gi]
                for t in range(t0_, t1_):
                    sidx = 20 + t
                    o = ost[t % 2]
                    A("dve", lambda e, t=t, o=o, sidx=sidx: e.scalar_tensor_tensor(out=o, in0=h[:, t, :], scalar=rstd[:, sidx:sidx + 1], in1=fgs,
                                                                                   op0=ALU.mult, op1=ALU.mult),
                      reads=[HK[t], ("rs", 20 + t0_), ("wr", 0)], writes=ykeys[t % 2] + (L1K if t < 2 else []))
                    A("sp", lambda e, t=t, o=o: e.dma_start(out=ov[:, t, :], in_=o), reads=ykeys[t % 2], chan="o%d" % (t % 2))

            def final_tile_cb(t):
                for gi, (t0_, t1_) in enumerate(FG):
                    if t == t1_ - 1:
                        rms_stats([(h[:, tt, :], [HK[tt]]) for tt in range(t0_, t1_)], 20 + t0_)
                        if gi > 0:
                            final_apply(gi - 1)
                        if gi == len(FG) - 1:
                            final_apply(gi)
            out_proj(yT1, 10, wo1, ["X"], after_tile=final_tile_cb)

        frozen["f"] = False
        if do_l1:
            fw = {"sp": [("o0", 8), ("o1", 8)]}
        else:
            ov = out_d.rearrange("(t p) d -> p t d", p=128)
            for i in range(4):
                A("sp", lambda e, i=i: e.dma_start(out=ov[:, 4 * i:4 * i + 4, :], in_=h[:, 4 * i:4 * i + 4, :]), reads=HK[4 * i:4 * i + 4], chan="o%d" % (i % 2))
            fw = {"sp": [("o0", 2), ("o1", 2)]}
        S.emit_all(block, sems, chans, fw)
        if os.environ.get('KLABELS'):
            import json
            json.dump(S.labels, open(os.environ['KLABELS'], 'w'))
    return nc


def _consts():
    ident = np.eye(128, dtype=np.float32)
    k = np.arange(128)[:, None]
    q = np.arange(128)[None, :]
    diag = (q >= k).astype(np.float32)
    prev = (q <= k).astype(np.float32)
    maskA = np.concatenate([prev, diag, prev, diag], axis=1)
    maskD = np.concatenate([diag, diag, diag, diag], axis=1)
    half = 8
    invf = (np.float32(ROPE_THETA) ** (-np.arange(half, dtype=np.float32) * np.float32(2.0 / 16))).astype(np.float32)
    invf = np.broadcast_to(invf[None, :], (128, half)).copy()
    return ident, maskA, maskD, invf


def _vec_layout(v):
    L = v.shape[0]
    return np.ascontiguousarray(v.reshape(L, 8, 128).transpose(2, 0, 1))


def _prep_shared(norm_g, mem_norm_g, w_mem_kv, attn_w_in, attn_w_out, conv_w_in, conv_w, conv_w_out, final_g):
    ident, maskA, maskD, invf = _consts()
    d = {"ident": ident, "maskA": maskA, "maskD": maskD}
    ng = _vec_layout(np.asarray(norm_g, np.float32)).reshape(128, 16)
    mg = _vec_layout(np.asarray(mem_norm_g, np.float32)).reshape(128, 16)
    d["wkv"] = np.ascontiguousarray(w_mem_kv, dtype=np.float32)
    w0 = np.asarray(attn_w_in[0], np.float32)
    blocks = []
    for c in range(4):
        for g in range(3):
            o = g * 512 + c * 128
            blocks.append(np.concatenate([w0[:, o:o + 128], w0[:, 1536 + o:1536 + o + 128], w0[:, 3072 + o:3072 + o + 128]], axis=1))
    d["w0a"] = np.ascontiguousarray(np.stack(blocks))
    d["w0qm"] = np.ascontiguousarray(w0[:, 4608:4864])
    d["w0z"] = np.ascontiguousarray(np.stack([w0[:, 4864:4864 + 384], w0[:, 4864 + 384:5632]]))
    d["wout0"] = np.ascontiguousarray(attn_w_out[0], dtype=np.float32)
    w1 = np.asarray(conv_w_in[0], np.float32)
    b1 = []
    for j in range(8):
        o = j * 128
        b1.append(np.concatenate([w1[:, o:o + 128], w1[:, 1024 + o:1024 + o + 128], w1[:, 2048 + o:2048 + o + 128],
                                  w1[:, 3328 + o:3328 + o + 128]], axis=1))
    d["w1b"] = np.ascontiguousarray(np.stack(b1))
    d["w1qm"] = np.ascontiguousarray(w1[:, 3072:3328])
    d["w1z2"] = np.ascontiguousarray(w1[:, 3328 + 1024:3328 + 1280])
    d["wout1"] = np.ascontiguousarray(conv_w_out[0], dtype=np.float32)
    cw = np.asarray(conv_w[0], np.float32)
    cwl = np.ascontiguousarray(cw.reshape(3, 8, 128).transpose(2, 1, 0)).reshape(128, 24)
    d["_cpk_head"] = np.ascontiguousarray(np.concatenate([ng, mg, invf, cwl], axis=1))
    d["fg"] = np.ascontiguousarray(np.broadcast_to(np.asarray(final_g, np.float32)[None, :], (128, DM)))
    return d


def _pos_layout(pos_row):
    out = np.empty((128, 48), np.int32)
    p = np.arange(128)
    for g in range(3):
        for b in range(16):
            out[:, g * 16 + b] = pos_row[tok_start(g, b) + p * DIL[g]]
    return out


L0_KEYS = ["x", "mem", "ident", "cpk", "wkv", "maskA", "maskD", "w0a", "w0z", "w0qm", "wout0"]
L1_KEYS = ["x", "mem", "ident", "cpk", "wkv", "fg", "w1b", "w1z2", "w1qm", "wout1"]
FULL_KEYS = L0_KEYS + [k for k in L1_KEYS if k not in L0_KEYS]

_CACHE = {}


def _get_nc(mode):
    if mode not in _CACHE:
        st = os.environ.get("KSTOP")
        _CACHE[mode] = build(mode, stop=int(st) if st else None)
    return _CACHE[mode]


def kernel(x, mem, positions, norm_g, mem_norm_g, w_mem_kv, attn_w_in, attn_w_out,
           conv_w_in, conv_w, conv_w_out, final_g, _mode="full"):
    x = np.asarray(x, np.float32)
    mem = np.asarray(mem, np.float32)
    positions = np.asarray(positions, np.int32)
    B = x.shape[0]
    shared = _prep_shared(norm_g, mem_norm_g, w_mem_kv, attn_w_in, attn_w_out, conv_w_in, conv_w, conv_w_out, final_g)

    def run(mode, xs, keys):
        nc = _get_nc(mode)
        in_maps = []
        for b in range(B):
            m = dict(shared)
            m["x"] = np.ascontiguousarray(xs[b])
            m["mem"] = np.ascontiguousarray(mem[b])
            m["cpk"] = np.ascontiguousarray(np.concatenate([shared["_cpk_head"], _pos_layout(positions[b]).view(np.float32)], axis=1))
            in_maps.append({k: m[k] for k in keys})
        res = run_bass_kernel_spmd(nc, in_maps, core_ids=list(range(B)))
        return np.stack([r["out"] for r in res.results], axis=0)

    if _mode == "full":
        return run("full", x, FULL_KEYS)
    if _mode == "l0":
        return run("l0", x, L0_KEYS)
    if _mode == "unfused":
        h1 = run("l0", x, L0_KEYS)
        return run("l1", h1, L1_KEYS)
    raise ValueError(_mode)
```

```python
import os
import numpy as np
from contextlib import ExitStack
import concourse.bass as bass
import concourse.mybir as mybir
from concourse.bass_utils import run_bass_kernel_spmd

F32 = mybir.dt.float32
BF16 = mybir.dt.bfloat16
I32 = mybir.dt.int32
AF = mybir.ActivationFunctionType
ALU = mybir.AluOpType
AX = mybir.AxisListType

ENGS = ("pe", "act", "dve", "pool", "sp")

S_TOK = 2048
DM = 1024
NT = 16
DIL = (1, 4, 16)
EPS = 1e-6
ROPE_THETA = 500000.0
PI = float(np.pi)


class _Op:
    __slots__ = ("eng", "emit", "waits", "signal", "idx", "chan", "vc")


class Sched:
    SAME_WIN = {"pe": 0, "act": 1, "dve": 1, "pool": 1 << 30, "sp": 0}

    def __init__(self):
        self.streams = {e: [] for e in ENGS}
        self.last_w = {}
        self.readers = {}
        self.clock = {e: {} for e in ENGS}
        self.chan_cnt = {}
        self.label = ''
        self.labels = {e: [] for e in ENGS}

    def add(self, eng, emit, reads=(), writes=(), chan=None):
        op = _Op()
        op.eng, op.emit, op.chan, op.signal = eng, emit, chan, False
        op.idx = len(self.streams[eng])
        deps = []
        for r in reads:
            t = self.last_w.get(r)
            if t is not None:
                deps.append(t)
        for w in writes:
            t = self.last_w.get(w)
            if t is not None:
                deps.append(t)
            deps.extend(self.readers.get(w, ()))
        clk = self.clock[eng]
        waits = []
        for t in deps:
            if t[0] == "e":
                _, E, k, vc = t
                if E == eng:
                    if op.idx - k <= self.SAME_WIN[eng] and clk.get(("self", E), -1) < k:
                        waits.append(("e", E, k))
                        clk[("self", E)] = k
                        self.streams[E][k].signal = True
                    continue
                if clk.get(E, -1) >= k:
                    continue
                waits.append(("e", E, k))
                self.streams[E][k].signal = True
            else:
                _, E, k, vc = t
                if clk.get(E, -1) >= k:
                    continue
                waits.append(("d", E, k))
            for kk, vv in vc.items():
                if clk.get(kk, -1) < vv:
                    clk[kk] = vv
            clk[E] = max(clk.get(E, -1), k)
        best = {}
        for w in waits:
            key = (w[0], w[1])
            if key not in best or best[key][2] < w[2]:
                best[key] = w
        op.waits = list(best.values())
        vc = {k: v for k, v in clk.items() if not isinstance(k, tuple)}
        if chan is None:
            vc[eng] = op.idx
            tok = ("e", eng, op.idx, vc)
        else:
            n = self.chan_cnt.get(chan, 0) + 1
            self.chan_cnt[chan] = n
            tok = ("d", chan, n, vc)
        self.streams[eng].append(op)
        self.labels[eng].append(self.label)
        for r in reads:
            self.readers.setdefault(r, []).append(tok)
        for w in writes:
            self.last_w[w] = tok
            self.readers[w] = []
        return tok

    def emit_all(self, block, sems, chan_sems, final_waits):
        rank = {}
        for e in ENGS:
            c = 0
            rk = {}
            for op in self.streams[e]:
                if op.signal and op.chan is None:
                    c += 1
                    rk[op.idx] = c
            rank[e] = rk

        def run(e, engine):
            for op in self.streams[e]:
                for w in op.waits:
                    if w[0] == "e":
                        engine.wait_ge(sems[w[1]], rank[w[1]][w[2]])
                    else:
                        engine.wait_ge(chan_sems[w[1]], 16 * w[2])
                ins = op.emit(engine)
                if op.chan is not None:
                    ins.then_inc(chan_sems[op.chan], 16)
                elif op.signal:
                    ins.then_inc(sems[e], 1)
            for (C, n) in final_waits.get(e, ()):
                engine.wait_ge(chan_sems[C], 16 * n)

        names = {"pe": "tensor", "act": "scalar", "dve": "vector", "pool": "gpsimd", "sp": "sync"}
        for e in ENGS:
            if not self.streams[e] and e not in final_waits:
                continue
            getattr(block, names[e])(lambda engine, e=e: run(e, engine))


def sl(start, n, step=1):
    return slice(start, start + (n - 1) * step + 1, step)


def tok_start(g, b):
    d = DIL[g]
    nb = NT // d
    r, n = divmod(b, nb)
    return n * 128 * d + r


class _Stop(Exception):
    pass


def build(mode="full", stop=None):
    do_l0 = mode in ("full", "l0")
    do_l1 = mode in ("full", "l1")
    nc = bass.Bass("TRN2", target_bir_lowering=False)

    def din(name, shape, dt=F32):
        return nc.dram_tensor(name, list(shape), dt, kind="ExternalInput").ap()

    x_d = din("x", [S_TOK, DM])
    mem_d = din("mem", [256, DM])
    ident_d = din("ident", [128, 128])
    cpk_d = din("cpk", [128, 112])
    wkv_d = din("wkv", [2, DM, 512])
    if do_l0:
        maskA_d = din("maskA", [128, 512])
        maskD_d = din("maskD", [128, 512])
        w0a_d = din("w0a", [12, DM, 384])
        w0z_d = din("w0z", [2, DM, 384])
        w0qm_d = din("w0qm", [DM, 256])
        wout0_d = din("wout0", [768, DM])
    if do_l1:
        fg_d = din("fg", [128, DM])
        w1b_d = din("w1b", [8, DM, 512])
        w1z2_d = din("w1z2", [DM, 256])
        w1qm_d = din("w1qm", [DM, 256])
        wout1_d = din("wout1", [1280, DM])
    out_d = nc.dram_tensor("out", [S_TOK, DM], F32, kind="ExternalOutput").ap()

    S = Sched()
    frozen = {"f": False}

    def A(eng, emit, reads=(), writes=(), chan=None):
        if frozen["f"]:
            return None
        return S.add(eng, emit, reads, writes, chan)

    def stage(n):
        S.label = 'st%d' % n
        if stop is not None and n >= stop:
            frozen["f"] = True
    with ExitStack() as es:
        def sb(name, shape, dt):
            return es.enter_context(nc.sbuf_tensor(name, list(shape), dt))

        h = sb("h", [128, NT, DM], F32)
        hnT = sb("hnT", [128, 8, S_TOK], BF16)
        U = sb("U", [128, 26624], BF16)
        wr = [sb("wr%d" % i, [128, 8, 512], BF16) for i in range(2)]
        memnT = sb("memnT", [128, 8, 256], BF16)
        kmT = sb("kmT", [128, 2, 256], BF16)
        vm = sb("vm", [128, 2, 256], BF16)
        identb = sb("identb", [128, 128], BF16)
        ones64 = sb("ones64", [128, 64], BF16)
        cpk = sb("cpks", [128, 112], F32)
        ngs = cpk[:, 0:16].rearrange("p (l k) -> p l k", l=2)
        mgs = cpk[:, 16:32].rearrange("p (l k) -> p l k", l=2)
        invfs = cpk[:, 32:40]
        cws = cpk[:, 40:64].rearrange("p (j k) -> p j k", j=8)
        pos_i = cpk[:, 64:112].bitcast(I32)
        xn = [sb("xn%d" % i, [128, DM], BF16) for i in range(2)]
        ss = sb("ss", [128, 40], F32)
        rstd = sb("rstd", [128, 40], F32)
        Ptt = sb("Ptt", [128, 4, 512], BF16)
        Pt = [Ptt[:, i, :] for i in range(4)]
        sqjunk = Ptt[:, 0:2, :].rearrange("p a b -> p (a b)")
        szt = [sb("sz%d" % i, [128, 512], BF16) for i in range(2)]
        mtb = [sb("mtb%d" % i, [128, 512], BF16) for i in range(2)]
        X = sb("X", [128, 10240], BF16)
        if do_l0:
            qkvB = X[:, 0:6144]
            qkst = [X[:, 6144 + 1024 * i:6144 + 1024 * (i + 1)].rearrange("p (b c) -> p b c", b=4) for i in range(2)]
            rt = [X[:, 8192 + 256 * i:8192 + 256 * (i + 1)].bitcast(F32).rearrange("p (a b c) -> p a b c", a=4, b=4) for i in range(4)]
            maskAb = X[:, 9216:9728]
            maskDb = X[:, 9728:10240]
            posf = sb("posf", [128, 48], F32)
            ang = qkvB[:, 0:768].bitcast(F32).rearrange("p (a b) -> p a b", a=48)
            ang2 = qkvB[:, 768:1536].bitcast(F32).rearrange("p (a b) -> p a b", a=48)
            angi = qkvB[:, 1536:2304].bitcast(I32).rearrange("p (a b) -> p a b", a=48)
            halfpi = sb("halfpi", [128, 1], F32)
            cosT = sb("cosT", [128, 48, 8], F32)
            sinT = sb("sinT", [128, 48, 8], F32)
        banks = [es.enter_context(nc.psum_tensor("bank%d" % i, [128, 512], F32)) for i in range(8)]
        sems = {e: es.enter_context(nc.semaphore("s_" + e)) for e in ENGS}
        chan_names = ["x0", "x1", "x2", "x3", "c", "cp", "w0", "w1", "wo", "mem", "fg", "o0", "o1"]
        chans = {c: es.enter_context(nc.semaphore("c_" + c)) for c in chan_names}
        block = es.enter_context(nc.Block())

        def KB(i):
            return [("bank", i)]
        HK = [("h", t) for t in range(NT)]
        HN = [("hnT", t) for t in range(NT)]

        xv = x_d.rearrange("(t p) d -> p t d", p=128)

        def load_x(chunks=(0, 1, 2, 3)):
            for i in chunks:
                A("sp", lambda e, i=i: e.dma_start(out=h[:, 4 * i:4 * i + 4, :], in_=xv[:, 4 * i:4 * i + 4, :]),
                  writes=HK[4 * i:4 * i + 4], chan="x%d" % i)
        cl = [(cpk, cpk_d, "sp"), (identb, ident_d, "pool")]
        if do_l0:
            cl += [(maskAb, maskA_d, "pool"), (maskDb, maskD_d, "pool")]
        for (dst, src, q) in cl:
            A(q, lambda e, dst=dst, src=src: e.dma_start(out=dst[:], in_=src), writes=["constsP" if q == "pool" else "consts"],
              chan=("cp" if q == "pool" else "c"))
        load_x((0, 1))
        load_x((2, 3))
        A("pool", lambda e: e.memset(ones64[:], 1.0), writes=["ones"])
        A("dve", lambda e: e.memset(ss[:], 0.0), writes=["ss"])
        if do_l0:
            A("dve", lambda e: e.memset(halfpi[:], PI / 2), writes=["halfpi"])

        wplan = []
        if do_l0:
            wplan += [(w0a_d[0], 384), (w0a_d[1], 384), (wkv_d[0], 512)] + [(w0a_d[i], 384) for i in range(2, 12)] + [(w0qm_d, 256), (w0z_d[1], 384), (w0z_d[0], 384)]
        if do_l1:
            wplan += [(wkv_d[1], 512)] + [(w1b_d[j], 512) for j in range(8)] + [(w1qm_d, 256), (w1z2_d, 256)]
        wstate = {"cur": 0, "issued": 0}

        def _wissue():
            k = wstate["issued"]
            if k >= len(wplan):
                return
            src_ap, ncols = wplan[k]
            i = k % 2
            wstate["issued"] += 1
            A("pool", lambda e: e.dma_start(out=wr[i][:, :, 0:ncols], in_=src_ap.rearrange("(kc p) n -> p kc n", p=128)),
              writes=[("wr", i)], chan="w%d" % i)

        def wload(src_ap, ncols, prefetch=True):
            k = wstate["cur"]
            assert wplan[k][1] == ncols, (k, ncols, wplan[k][1])
            while wstate["issued"] <= min(k + (1 if prefetch else 0), len(wplan) - 1):
                _wissue()
            wstate["cur"] += 1
            return wr[k % 2], ("wr", k % 2)

        def rms_squares(tiles, s0):
            for i, (src_tile, src_keys) in enumerate(tiles):
                A("act", lambda e, src_tile=src_tile, i=i: e.activation(out=sqjunk, in_=src_tile, func=AF.Square, accum_out=ss[:, s0 + i:s0 + i + 1]),
                  reads=list(src_keys) + ["ss"], writes=[("ss", s0 + i), ("P", 0), ("P", 1)])

        def rms_rstd(n, s0):
            A("dve", lambda e: e.tensor_scalar(out=rstd[:, s0:s0 + n], in0=ss[:, s0:s0 + n], scalar1=1.0 / DM, scalar2=EPS,
                                               op0=ALU.mult, op1=ALU.add), reads=[("ss", s0 + i) for i in range(n)] + ["ss"], writes=[("rs0", s0)])
            A("act", lambda e: e.activation(out=rstd[:, s0:s0 + n], in_=rstd[:, s0:s0 + n], func=AF.Sqrt), reads=[("rs0", s0)], writes=[("rs1", s0)])
            A("dve", lambda e: e.reciprocal(out=rstd[:, s0:s0 + n], in_=rstd[:, s0:s0 + n]), reads=[("rs1", s0)], writes=[("rs", s0)])

        def rms_stats(tiles, s0):
            rms_squares(tiles, s0)
            rms_rstd(len(tiles), s0)

        def rmsnorm_T(src_tile, src_keys, gvec, dstT, dst_col0, dst_keys, sidx, s0, i):
            xb = xn[i % 2]
            xk = ("xn", i % 2)
            A("act", lambda e: e.activation(out=xb[:], in_=src_tile, func=AF.Copy, scale=rstd[:, sidx:sidx + 1]),
              reads=list(src_keys) + [("rs", s0)], writes=[xk])
            tb_ = (2, 4)[i % 2]
            pb = banks[tb_][:].bitcast(BF16)
            for kc in range(8):
                A("pe", lambda e, kc=kc: e.transpose(out=pb[:, kc * 128:(kc + 1) * 128], in_=xb[:, kc * 128:(kc + 1) * 128], identity=identb[:]),
                  reads=[xk, "constsP"], writes=KB(tb_))
            A("dve", lambda e: e.tensor_tensor(out=dstT[:, :, dst_col0:dst_col0 + 128],
                                               in0=pb.rearrange("p (k t) -> p k t", k=8),
                                               in1=gvec.unsqueeze(2).to_broadcast([128, 8, 128]), op=ALU.mult),
              reads=KB(tb_) + ["consts"], writes=list(dst_keys))

        def layer_norm_cb(l, defer_last=None):
            def apply(b):
                for t in range(4 * b, 4 * b + 4):
                    rmsnorm_T(h[:, t, :], [HK[t]], ngs[:, l, :], hnT, t * 128, [HN[t]], t, 4 * b, t)

            def cb(b):
                rms_squares([(h[:, t, :], [HK[t]]) for t in range(4 * b, 4 * b + 4)], 4 * b)
                if b >= 1:
                    rms_rstd(4, 4 * (b - 1))
                if b >= 2:
                    apply(b - 2)
                if b == 3:
                    apply(1)
                    rms_rstd(4, 12)
                    if defer_last is not None:
                        defer_last.append(lambda: (apply(2), apply(3)))
                    else:
                        apply(2)
                        apply(3)
            return cb

        def layer_norm_phase(l, units=(), after_stats1=None):
            units = list(units)
            tiles_done = 0

            def stats(b):
                rms_stats([(h[:, t, :], [HK[t]]) for t in range(4 * b, 4 * b + 4)], 4 * b)

            stats(0)
            for b in range(4):
                if b + 1 < 4:
                    stats(b + 1)
                if b == 0 and after_stats1 is not None:
                    after_stats1()
                ready = 4 * b
                while units and tiles_done < ready:
                    kind, u = units.pop(0)
                    u()
                    if kind == "t":
                        tiles_done += 1
                while units and units[0][0] != "t":
                    units.pop(0)[1]()
                for t in range(4 * b, 4 * b + 4):
                    rmsnorm_T(h[:, t, :], [HK[t]], ngs[:, l, :], hnT, t * 128, [HN[t]], t, 4 * b, t)
            for kind, u in units:
                u()

        def mem_stats_part(l, tmp_ap, tmp_keys, extra_reads):
            A("sp", lambda e: e.dma_start(out=tmp_ap, in_=mem_d.rearrange("(t p) d -> p t d", p=128)),
              reads=list(extra_reads), writes=list(tmp_keys), chan="mem")
            rms_stats([(tmp_ap[:, t, :], tmp_keys) for t in range(2)], 16 + 2 * l)

        def mem_apply_part(l, tmp_ap, tmp_keys):
            for t in range(2):
                rmsnorm_T(tmp_ap[:, t, :], tmp_keys, mgs[:, l, :], memnT, t * 128, ["memnT"], 16 + 2 * l + t, 16 + 2 * l, t)

        def mem_norm_part(l, tmp_ap, tmp_keys, extra_reads):
            mem_stats_part(l, tmp_ap, tmp_keys, extra_reads)
            mem_apply_part(l, tmp_ap, tmp_keys)

        def mem_kv_part(l):
            wt, wk = wload(wkv_d[l], 512)
            for mc in range(2):
                for kc in range(8):
                    A("pe", lambda e, mc=mc, kc=kc: e.matmul(banks[0][:, mc * 256:(mc + 1) * 256], lhsT=wt[:, kc, mc * 128:(mc + 1) * 128],
                                                             rhs=memnT[:, kc, :], start=(kc == 0), stop=(kc == 7)),
                      reads=[wk, "memnT"], writes=KB(0))
            A("act", lambda e: e.activation(out=kmT[:].rearrange("p a b -> p (a b)"), in_=banks[0][:, :], func=AF.Copy),
              reads=KB(0), writes=["kmT"])
            for mb in range(2):
                for kc in range(8):
                    A("pe", lambda e, mb=mb, kc=kc: e.matmul(banks[1][:, mb * 256:(mb + 1) * 256], lhsT=memnT[:, kc, mb * 128:(mb + 1) * 128],
                                                             rhs=wt[:, kc, 256:512], start=(kc == 0), stop=(kc == 7)),
                      reads=[wk, "memnT"], writes=KB(1))
            A("dve", lambda e: e.tensor_copy(out=vm[:].rearrange("p a b -> p (a b)"), in_=banks[1][:, :]),
              reads=KB(1), writes=["vm"])

        def make_memattn(qmT, qm_keys, yT, ychunk0, rd_ap, rd_key, rd_first=()):
            out = []
            sbank = {(0, 0): 3, (1, 0): 4, (0, 1): 5, (1, 1): 6}
            for it in range(8):
                mc, qd = divmod(it, 4)
                qs = slice(qd * 512, (qd + 1) * 512)

                def S_fn(mc=mc, qs=qs):
                    for mb in range(2):
                        for hh in range(2):
                            hp = slice(hh * 64, hh * 64 + 64)
                            bk = sbank[(hh, mb)]
                            A("pe", lambda e, mb=mb, hp=hp, bk=bk: e.matmul(
                                banks[bk][:, :], lhsT=kmT[hp, mc, mb * 128:(mb + 1) * 128], rhs=qmT[hp, mc, qs], start=True, stop=True),
                              reads=["kmT"] + list(qm_keys[mc]), writes=KB(bk))
                    for mb in range(2):
                        for hh in range(2):
                            bk = sbank[(hh, mb)]
                            pi = hh * 2 + mb
                            A("act", lambda e, bk=bk, pi=pi: e.activation(out=Pt[pi][:], in_=banks[bk][:, :], func=AF.Exp, scale=0.125),
                              reads=KB(bk), writes=[("P", pi)])

                def PV_fn(mc=mc, qd=qd, qs=qs, it=it):
                    for hh in range(2):
                        ph = slice(hh * 64, hh * 64 + 64)
                        tp = (0, 64) if hh else None
                        for mb in range(2):
                            pi = hh * 2 + mb
                            A("pe", lambda e, mb=mb, hh=hh, ph=ph, pi=pi, tp=tp: e.matmul(
                                banks[7][ph, :], lhsT=vm[:, mb, mc * 128 + hh * 64:mc * 128 + hh * 64 + 64], rhs=Pt[pi][:],
                                start=(mb == 0), stop=(mb == 1), tile_position=tp),
                              reads=["vm", ("P", pi)], writes=KB(7))
                        for mb in range(2):
                            pi = hh * 2 + mb
                            A("pe", lambda e, mb=mb, ph=ph, pi=pi, tp=tp: e.matmul(
                                banks[2][ph, :], lhsT=ones64[:], rhs=Pt[pi][:], start=(mb == 0), stop=(mb == 1), tile_position=tp),
                              reads=["ones", ("P", pi)], writes=KB(2))
                    rdv = rd_ap[:, (it % 2) * 512:(it % 2) * 512 + 512]
                    rk = (rd_key, it % 2)
                    tb = mtb[it % 2]
                    tk = ("mtb", it % 2)
                    A("act", lambda e: e.activation(out=tb[:], in_=banks[7][:, :], func=AF.Copy, scale=0.5), reads=KB(7), writes=[tk])
                    A("act", lambda e: e.activation(out=rdv, in_=banks[2][:, :], func=AF.Copy), reads=KB(2), writes=[rk] + (list(rd_first) if it < 2 else []))
                    A("dve", lambda e: e.reciprocal(out=rdv, in_=rdv), reads=[rk], writes=[rk])
                    A("dve", lambda e: e.tensor_tensor(out=tb[:], in0=tb[:], in1=rdv, op=ALU.mult), reads=[tk, rk], writes=[tk])
                    A("pool", lambda e: e.tensor_tensor(out=yT[:, ychunk0 + mc, qs], in0=yT[:, ychunk0 + mc, qs], in1=tb[:], op=ALU.mult),
                      reads=[tk, ("y", ychunk0 + mc, qd)], writes=[("y", ychunk0 + mc, qd)])
                out.append((S_fn, PV_fn))
            return out

        def proj_unit(wt, wk, col0, qd, bk, evac):
            for kc in range(8):
                A("pe", lambda e, kc=kc: e.matmul(banks[bk][:, :], lhsT=wt[:, kc, col0:col0 + 128],
                                                 rhs=hnT[:, kc, qd * 512:(qd + 1) * 512], start=(kc == 0), stop=(kc == 7)),
                  reads=[wk] + HN[4 * qd:4 * qd + 4], writes=KB(bk))
            evac(qd, banks[bk], KB(bk))

        def proj_fm(wt, wk, col0, evac, bank_ids):
            for qd in range(4):
                bk = bank_ids[qd % len(bank_ids)]
                for kc in range(8):
                    A("pe", lambda e, kc=kc, qd=qd, bk=bk: e.matmul(banks[bk][:, :], lhsT=wt[:, kc, col0:col0 + 128],
                                                                    rhs=hnT[:, kc, qd * 512:(qd + 1) * 512], start=(kc == 0), stop=(kc == 7)),
                      reads=[wk] + HN[4 * qd:4 * qd + 4], writes=KB(bk))
                evac(qd, banks[bk], KB(bk))

        def load_wout(wo, wout_d, wo_keys, extra_reads=()):
            A("pool", lambda e: e.dma_start(out=wo, in_=wout_d.rearrange("(c p) n -> p c n", p=128)),
              reads=list(extra_reads), writes=list(wo_keys), chan="wo")

        def out_proj(yT, nchunk, wo, wo_keys, after_batch=None, after_tile=None):
            for t in range(NT):
                for hf in range(2):
                    bk = (0, 1, 7, 3)[(t * 2 + hf) % 4]
                    for c in range(nchunk):
                        A("pe", lambda e, t=t, hf=hf, c=c, bk=bk: e.matmul(banks[bk][:, :], lhsT=yT[:, c, t * 128:(t + 1) * 128],
                                                                           rhs=wo[:, c, hf * 512:(hf + 1) * 512], start=(c == 0), stop=(c == nchunk - 1)),
                          reads=list(wo_keys) + [("y", c, t // 4)], writes=KB(bk))
                    A("dve", lambda e, t=t, hf=hf, bk=bk: e.tensor_tensor(out=h[:, t, hf * 512:(hf + 1) * 512], in0=h[:, t, hf * 512:(hf + 1) * 512],
                                                                          in1=banks[bk][:, :], op=ALU.add),
                      reads=KB(bk) + [HK[t]], writes=[HK[t]])
                if after_batch is not None and t % 4 == 3:
                    after_batch(t // 4)
                if after_tile is not None:
                    after_tile(t)

        if do_l0:
            yT0 = U[:, 0:12288].rearrange("p (c t) -> p c t", c=6)
            qT = U[:, 12288:14336]
            kT = U[:, 14336:16384]
            Vt = U[:, 16384:18432].rearrange("p (b c) -> p b c", b=16)
            accn = U[:, 18432:22528].bitcast(F32)
            accd = U[:, 22528:26624].bitcast(F32)
            qmT0 = U[:, 12288:16384].rearrange("p (c t) -> p c t", c=2)
            memtmp0 = U[:, 18432:22528].bitcast(F32).rearrange("p (t d) -> p t d", t=2)

            stage(1)
            memtmp0 = U[:, 8192:12288].bitcast(F32).rearrange("p (t d) -> p t d", t=2)
            MK0 = [("y", 4 + i, qd) for i in range(2) for qd in range(4)]

            QB = qkvB
            sets = [
                dict(qT=U[:, 12288:14336], kT=U[:, 14336:16384], V=U[:, 16384:18432].rearrange("p (b c) -> p b c", b=16), kq="qT", kk="kT", kv="V", ix=0),
                dict(qT=QB[:, 0:2048], kT=QB[:, 2048:4096], V=QB[:, 4096:6144].rearrange("p (b c) -> p b c", b=16), kq="qT1", kk="kT1", kv="V1", ix=1),
            ]
            def GK(st0, which, b4):
                return (st0[which], b4)

            def GKALL(st0, which):
                return [(st0[which], b4) for b4 in range(4)]
            tile_ctr = {"n": 0}
            TB = (0, 1, 7)

            def make_inproj(k):
                c, g = divmod(k, 3)
                d = DIL[g]
                st_ = sets[k % 2]
                wt, wk = wload(w0a_d[k], 384)
                units = []
                for b4 in range(4):
                    stg = qkst[b4 % 2]
                    stk = ("qkst", b4 % 2)
                    for bi in range(4):
                        def tile_unit(b4=b4, bi=bi, stg=stg, stk=stk):
                            b = b4 * 4 + bi
                            t0 = tok_start(g, b)
                            if g == 0:
                                hn_keys = [HN[b]]
                            elif g == 1:
                                hn_keys = HN[4 * (b % 4):4 * (b % 4) + 4]
                            else:
                                hn_keys = HN
                            bk = TB[tile_ctr["n"] % 3]
                            tile_ctr["n"] += 1
                            for kc in range(8):
                                A("pe", lambda e, kc=kc, t0=t0, bk=bk: e.matmul(banks[bk][:, 0:384], lhsT=hnT[:, kc, sl(t0, 128, d)],
                                                                               rhs=wt[:, kc, 0:384], start=(kc == 0), stop=(kc == 7)),
                                  reads=[wk] + hn_keys, writes=KB(bk))
                            Vt = st_["V"]
                            if b % 2 == 0:
                                A("act", lambda e: e.activation(out=stg[:, bi, :], in_=banks[bk][:, 0:256], func=AF.Copy), reads=KB(bk), writes=[stk])
                                A("act", lambda e: e.activation(out=Vt[:, b, :], in_=banks[bk][:, 256:384], func=AF.Copy), reads=KB(bk), writes=[GK(st_, "kv", b // 4)])
                            else:
                                A("dve", lambda e: e.tensor_copy(out=stg[:, bi, :], in_=banks[bk][:, 0:256]), reads=KB(bk), writes=[stk])
                                A("dve", lambda e: e.tensor_copy(out=Vt[:, b, :], in_=banks[bk][:, 256:384]), reads=KB(bk), writes=[GK(st_, "kv", b // 4)])
                            if bi == 3:
                                sv = stg[:].rearrange("p b (h d) -> p b h d", h=4)
                                t1 = sv[:, :, :, 0:8]
                                t2 = sv[:, :, :, 8:16]
                                col = g * 16 + b4 * 4
                                cb = cosT[:, col:col + 4, :].unsqueeze(2).to_broadcast([128, 4, 4, 8])
                                sbb = sinT[:, col:col + 4, :].unsqueeze(2).to_broadcast([128, 4, 4, 8])
                                A("dve", lambda e: e.tensor_tensor(out=rt[0][:], in0=t1, in1=cb, op=ALU.mult), reads=[stk, "cosT"], writes=["rt0"])
                                A("pool", lambda e: e.tensor_tensor(out=rt[2][:], in0=t2, in1=cb, op=ALU.mult), reads=[stk, "cosT"], writes=["rt2"])
                                A("dve", lambda e: e.tensor_tensor(out=rt[1][:], in0=t2, in1=sbb, op=ALU.mult), reads=[stk, "sinT"], writes=["rt1"])
                                A("pool", lambda e: e.tensor_tensor(out=rt[3][:], in0=t1, in1=sbb, op=ALU.mult), reads=[stk, "sinT"], writes=["rt3"])
                                A("dve", lambda e: e.tensor_tensor(out=t1, in0=rt[0][:], in1=rt[1][:], op=ALU.subtract), reads=["rt0", "rt1", "rt3"], writes=[stk])
                                A("pool", lambda e: e.tensor_tensor(out=t2, in0=rt[2][:], in1=rt[3][:], op=ALU.add), reads=["rt2", "rt3", "rt1"], writes=[stk])
                        units.append(("t", tile_unit))

                    def tr_unit(b4=b4, stg=stg, stk=stk):
                        pb = banks[2][:].bitcast(BF16)
                        for bi in range(4):
                            A("pe", lambda e, bi=bi: e.transpose(out=pb[:, bi * 128:(bi + 1) * 128], in_=stg[:, bi, 0:128], identity=identb[:]),
                              reads=[stk, "constsP"], writes=KB(2))
                            A("pe", lambda e, bi=bi: e.transpose(out=pb[:, 512 + bi * 128:512 + (bi + 1) * 128], in_=stg[:, bi, 128:256], identity=identb[:]),
                              reads=[stk, "constsP"], writes=KB(2))
                        qT_, kT_ = st_["qT"], st_["kT"]
                        if b4 % 2 == 0:
                            A("act", lambda e: e.activation(out=qT_[:, b4 * 512:(b4 + 1) * 512], in_=pb[:, 0:512], func=AF.Copy), reads=KB(2), writes=[GK(st_, "kq", b4)])
                            A("act", lambda e: e.activation(out=kT_[:, b4 * 512:(b4 + 1) * 512], in_=pb[:, 512:1024], func=AF.Copy), reads=KB(2), writes=[GK(st_, "kk", b4)])
                        else:
                            A("dve", lambda e: e.tensor_copy(out=qT_[:, b4 * 512:(b4 + 1) * 512], in_=pb[:, 0:512]), reads=KB(2), writes=[GK(st_, "kq", b4)])
                            A("dve", lambda e: e.tensor_copy(out=kT_[:, b4 * 512:(b4 + 1) * 512], in_=pb[:, 512:1024]), reads=KB(2), writes=[GK(st_, "kk", b4)])
                    units.append(("r", tr_unit))
                tiles = [u for u in units if u[0] == "t"]
                trs = [u for u in units if u[0] == "r"]
                order = tiles[0:8] + [trs[0]] + tiles[8:12] + [trs[1]] + tiles[12:16] + [trs[2]]
                return order, trs[3][1]

            def make_attn(k):
                c, g = divmod(k, 3)
                d = DIL[g]
                nb = NT // d
                st_ = sets[k % 2]
                qT, kT, Vt = st_["qT"], st_["kT"], st_["V"]
                if g < 2:
                    iters = []
                    for r in range(d):
                        for nh in range(nb // 2):
                            iters.append([(r * nb + 2 * nh + s, (2 * nh + s) > 0) for s in range(2)])
                    mask = maskAb
                else:
                    iters = [[(4 * i + s, False) for s in range(4)] for i in range(4)]
                    mask = maskDb
                out = []
                for iti, qbs in enumerate(iters):
                    lo = 512
                    offs = []
                    for s, (b, hp_) in enumerate(qbs):
                        if g < 2:
                            o_prev, o_diag = s * 256, s * 256 + 128
                        else:
                            o_prev, o_diag = None, s * 128
                        offs.append((o_prev, o_diag))
                        lo = min(lo, o_prev if hp_ else o_diag)
                    par = iti % 2

                    def S_fn(qbs=qbs, offs=offs, lo=lo, par=par):
                        for s, (b, hp_) in enumerate(qbs):
                            o_prev, o_diag = offs[s]
                            for part in ((0, 1) if hp_ else (1,)):
                                for hh in range(2):
                                    hp = slice(hh * 64, hh * 64 + 64)
                                    bk = 3 + hh
                                    kb = b - 1 if part == 0 else b
                                    o = o_prev if part == 0 else o_diag
                                    A("pe", lambda e, hp=hp, bk=bk, b=b, kb=kb, o=o, hh=hh: e.matmul(banks[bk][:, o:o + 128], lhsT=kT[hp, kb * 128:(kb + 1) * 128],
                                                                                             rhs=qT[hp, b * 128:(b + 1) * 128], start=True, stop=True,
                                                                                             tile_position=(64 * hh, 0)),
                                      reads=[GK(st_, "kq", b // 4), GK(st_, "kk", kb // 4)], writes=KB(bk))
                        for hh in range(2):
                            bk = 3 + hh
                            pi = par * 2 + hh
                            A("act", lambda e, bk=bk, pi=pi: e.activation(out=Pt[pi][:, lo:512], in_=banks[bk][:, lo:512], func=AF.Exp, scale=0.125),
                              reads=KB(bk), writes=[("P", pi)])
                            A("dve" if hh == 0 else "pool", lambda e, pi=pi: e.tensor_tensor(out=Pt[pi][:, lo:512], in0=Pt[pi][:, lo:512],
                                                                                          in1=mask[:, lo:512], op=ALU.mult),
                              reads=[("P", pi), "constsP"], writes=[("P", pi)])

                    def PV_fn(qbs=qbs, offs=offs, par=par, iti=iti):
                        onb, odb = 5, 6
                        for (obank, lv) in ((onb, True), (odb, False)):
                            for s, (b, hp_) in enumerate(qbs):
                                o_prev, o_diag = offs[s]
                                oc = s * 128
                                for part in ((0, 1) if hp_ else (1,)):
                                    for hh in range(2):
                                        ph = slice(hh * 64, hh * 64 + 64)
                                        tp = (0, 64 * hh)
                                        pi = par * 2 + hh
                                        kb = b - 1 if part == 0 else b
                                        o = o_prev if part == 0 else o_diag
                                        st_flag = (part == 0) or (not hp_)
                                        sp_flag = (part == 1)
                                        A("pe", lambda e, ph=ph, tp=tp, pi=pi, kb=kb, o=o, oc=oc, obank=obank, lv=lv, hh=hh, st_flag=st_flag, sp_flag=sp_flag: e.matmul(
                                            banks[obank][ph, oc:oc + 128], lhsT=(Vt[:, kb, hh * 64:hh * 64 + 64] if lv else ones64[:]),
                                            rhs=Pt[pi][:, o:o + 128], start=st_flag, stop=sp_flag, tile_position=tp),
                                          reads=[GK(st_, "kv", kb // 4), "ones", ("P", pi)], writes=KB(obank))
                        nq = len(qbs) * 128
                        if g == 0:
                            dn = accn[:, iti * 256:iti * 256 + 256]
                            dd = accd[:, iti * 256:iti * 256 + 256]
                        elif g == 1:
                            r, nh = divmod(iti, 2)
                            dn = accn[:, sl(1024 * nh + r, 256, 4)]
                            dd = accd[:, sl(1024 * nh + r, 256, 4)]
                        else:
                            dn = accn.rearrange("p (i r) -> p r i", r=16)[:, 4 * iti:4 * iti + 4, :]
                            dd = accd.rearrange("p (i r) -> p r i", r=16)[:, 4 * iti:4 * iti + 4, :]
                        srcn = banks[onb][:, 0:nq]
                        srcd = banks[odb][:, 0:nq]
                        if g == 2:
                            srcn = srcn.rearrange("p (r i) -> p r i", r=4)
                            srcd = srcd.rearrange("p (r i) -> p r i", r=4)
                        if g == 0:
                            A("act", lambda e: e.activation(out=dn, in_=srcn, func=AF.Copy), reads=KB(onb), writes=[("accn", iti // 2)])
                            A("dve", lambda e: e.tensor_copy(out=dd, in_=srcd), reads=KB(odb), writes=[("accd", iti // 2)])
                        else:
                            AN = [("accn", q_) for q_ in range(4)]
                            AD = [("accd", q_) for q_ in range(4)]
                            A("dve", lambda e: e.tensor_tensor(out=dn, in0=dn, in1=srcn, op=ALU.add), reads=KB(onb) + AN, writes=AN)
                            A("dve", lambda e: e.tensor_tensor(out=dd, in0=dd, in1=srcd, op=ALU.add), reads=KB(odb) + AD, writes=AD)
                    out.append((S_fn, PV_fn))
                return out

            NK = 12
            S.label = "norm0"
            def rope_tables():
                A("dve", lambda e: e.tensor_copy(out=posf[:], in_=pos_i[:]), reads=["consts"], writes=["posf"])
                A("dve", lambda e: e.tensor_tensor(out=ang[:], in0=posf[:].unsqueeze(2).to_broadcast([128, 48, 8]),
                                                   in1=invfs[:].unsqueeze(1).to_broadcast([128, 48, 8]), op=ALU.mult),
                  reads=["posf", "consts"], writes=["ang"])
                C1 = 6.28125
                C2 = 2.0 * np.pi - C1
                A("dve", lambda e: e.tensor_scalar(out=ang2[:], in0=ang[:], scalar1=1.0 / (2 * PI), scalar2=None, op0=ALU.mult),
                  reads=["ang"], writes=["ang2"])
                A("dve", lambda e: e.tensor_copy(out=angi[:], in_=ang2[:]), reads=["ang2"], writes=["angi"])
                A("dve", lambda e: e.tensor_copy(out=ang2[:], in_=angi[:]), reads=["angi", "ang2"], writes=["angf"])
                A("dve", lambda e: e.scalar_tensor_tensor(out=ang[:], in0=ang2[:], scalar=-C1, in1=ang[:], op0=ALU.mult, op1=ALU.add),
                  reads=["angf", "ang"], writes=["r1"])
                A("dve", lambda e: e.scalar_tensor_tensor(out=ang[:], in0=ang2[:], scalar=-C2, in1=ang[:], op0=ALU.mult, op1=ALU.add),
                  reads=["angf", "r1"], writes=["frac"])
                A("act", lambda e: e.activation(out=sinT[:], in_=ang[:], func=AF.Sin, scale=0.5), reads=["frac"], writes=["sh"])
                A("act", lambda e: e.activation(out=cosT[:], in_=ang[:], func=AF.Sin, scale=-0.5, bias=halfpi[:, 0:1]), reads=["frac", "halfpi"], writes=["ch"])
                A("dve", lambda e: e.tensor_tensor(out=ang2[:], in0=sinT[:], in1=sinT[:], op=ALU.mult), reads=["sh", "angf", "frac"], writes=["s2"])
                A("dve", lambda e: e.scalar_tensor_tensor(out=sinT[:], in0=sinT[:], scalar=2.0, in1=cosT[:], op0=ALU.mult, op1=ALU.mult),
                  reads=["sh", "ch", "s2"], writes=["sinT"])
                A("dve", lambda e: e.tensor_scalar(out=cosT[:], in0=ang2[:], scalar1=-2.0, scalar2=1.0, op0=ALU.mult, op1=ALU.add),
                  reads=["s2", "sinT"], writes=["cosT"])

            units0, carry = make_inproj(0)
            pending_norm = []
            layer_norm_phase(0, units=units0, after_stats1=rope_tables)
            mem_stats_part(0, memtmp0, MK0, [])
            for k in range(NK):
                c, g = divmod(k, 3)
                S.label = "attn%d" % k
                its = make_attn(k)
                nxt, nxt_carry = make_inproj(k + 1) if k + 1 < NK else ([], None)
                n_it = len(its)
                if k == NK - 1:
                    wq_t, wq_k = wload(w0qm_d, 256)
                    qn = 0
                    for mc in range(2):
                        for qd in range(4):
                            def qm_unit(mc=mc, qd=qd, bk=(0, 1)[qn % 2]):
                                def ev_qm(qd_, bank, bkey):
                                    A("act", lambda e: e.activation(out=qmT0[:, mc, qd_ * 512:(qd_ + 1) * 512], in_=bank[:, :], func=AF.Copy),
                                      reads=list(bkey), writes=GKALL(sets[0], ("kq", "kk")[mc]))
                                proj_unit(wq_t, wq_k, mc * 128, qd, bk, ev_qm)
                            nxt.append(("t", qm_unit))
                            qn += 1
                    qm_done = True
                per = [[] for _ in range(n_it)]
                tiles_per = max(1, sum(1 for kd, _ in nxt if kd == "t") // n_it)
                cnt = 0
                slot = 0
                for (kind, u) in nxt:
                    per[min(slot, n_it - 1)].append(u)
                    if kind == "t":
                        cnt += 1
                        if cnt % tiles_per == 0:
                            slot += 1
                def make_norm(c_):
                    fns = []
                    for q_ in range(4):
                        def fn(q_=q_):
                            cs = slice(q_ * 512, (q_ + 1) * 512)
                            A("act", lambda e: e.activation(out=accd[:, cs], in_=accd[:, cs], func=AF.Ln), reads=[("accd", q_)], writes=[("accd", q_)])
                            A("act", lambda e: e.activation(out=accd[:, cs], in_=accd[:, cs], func=AF.Exp, scale=-1.0), reads=[("accd", q_)], writes=[("accd", q_)])
                            A("dve", lambda e: e.scalar_tensor_tensor(out=yT0[:, c_, cs], in0=accn[:, cs], scalar=0.5, in1=accd[:, cs], op0=ALU.mult, op1=ALU.mult),
                              reads=[("accn", q_), ("accd", q_)], writes=[("y", c_, q_)])
                        fns.append(fn)
                    return fns

                its[0][0]()
                for i in range(n_it):
                    if i + 1 < n_it:
                        its[i + 1][0]()
                    if i == 0 and carry is not None:
                        carry()
                    for u in per[i]:
                        u()
                    if g == 0 and pending_norm and i % 2 == 0:
                        pending_norm.pop(0)()
                    its[i][1]()
                carry = nxt_carry
                if k == 0:
                    S.label = "mem0"
                    mem_apply_part(0, memtmp0, MK0)
                    mem_kv_part(0)
                if g == 2:
                    pending_norm = make_norm(c)
                    if c == 3:
                        for fn in pending_norm:
                            fn()
                        pending_norm = []
            S.label = "qm0"
            ACCK = [("accn", q_) for q_ in range(4)] + [("accd", q_) for q_ in range(4)]
            wo0 = U[:, 18432:24576].rearrange("p (c n) -> p c n", c=6)
            assert qm_done
            if do_l1:
                memtmpX = X[:, 0:4096].bitcast(F32).rearrange("p (t d) -> p t d", t=2)
                mem_stats_part(1, memtmpX, GKALL(sets[1], "kq") + GKALL(sets[1], "kk"), [])
            S.label = "z0"
            zstate = {"zi": 0, "n": 0}
            zunits = []
            for zb in (1, 0):
                def lazy_w(zb=zb, cache={}):
                    if "w" not in cache:
                        cache["w"] = wload(w0z_d[zb], 384)
                    return cache["w"]
                for zc3 in ((1, 2, 0) if zb == 1 else (0, 1, 2)):
                    zc = zb * 3 + zc3
                    for qd in range(4):
                        def zunit(zc=zc, zc3=zc3, qd=qd, lazy_w=lazy_w):
                            wt, wk = lazy_w()

                            def ev_z(qd, bank, bkey):
                                szb = szt[zstate["zi"] % 2]
                                szk = ("sz", zstate["zi"] % 2)
                                zstate["zi"] += 1
                                qs_ = slice(qd * 512, (qd + 1) * 512)
                                A("act", lambda e: e.activation(out=szb[:], in_=bank[:, :], func=AF.Tanh, scale=0.5), reads=list(bkey), writes=[szk])
                                if zc >= 4:
                                    A("dve", lambda e: e.scalar_tensor_tensor(out=yT0[:, zc, qs_], in0=szb[:], scalar=1.0, in1=bank[:, :], op0=ALU.add, op1=ALU.mult),
                                      reads=list(bkey) + [szk], writes=[("y", zc, qd)])
                                    return
                                A("dve", lambda e: e.scalar_tensor_tensor(out=szb[:], in0=szb[:], scalar=1.0, in1=bank[:, :], op0=ALU.add, op1=ALU.mult),
                                  reads=list(bkey) + [szk], writes=[szk])
                                A("pool", lambda e: e.tensor_tensor(out=yT0[:, zc, qs_], in0=yT0[:, zc, qs_], in1=szb[:], op=ALU.mult),
                                  reads=[szk, ("y", zc, qd)], writes=[("y", zc, qd)])
                            bk = (0, 1)[zstate["n"] % 2]
                            zstate["n"] += 1
                            proj_unit(wt, wk, zc3 * 128, qd, bk, ev_z)
                        zunits.append(zunit)
            for u in zunits[:8]:
                u()
            load_wout(wo0, wout0_d, ACCK)
            rest = zunits[8:]
            rd0 = U[:, 16384:18432].bitcast(F32)
            mits = make_memattn(qmT0, [GKALL(sets[0], "kq"), GKALL(sets[0], "kk")], yT0, 4, rd0, "rdA", rd_first=GKALL(sets[0], "kv"))
            S.label = "memattn0"
            for i, (S_fn, PV_fn) in enumerate(mits):
                S_fn()
                for u in rest[2 * i:2 * i + 2]:
                    u()
                PV_fn()
            S.label = "outproj0"
            l1_deferred = []
            out_proj(yT0, 6, wo0, ACCK, after_batch=(layer_norm_cb(1, defer_last=l1_deferred) if do_l1 else None))
            if do_l1:
                S.label = "mem1n"
                mem_apply_part(1, memtmpX, GKALL(sets[1], "kq") + GKALL(sets[1], "kk"))
                S.label = "mem1kv"
                mem_kv_part(1)

        if do_l1:
            yT1 = U[:, 0:20480].rearrange("p (c t) -> p c t", c=10)
            cgs = U[:, 20480:21504]
            a_sb = U[:, 21504:23560].bitcast(F32)
            cv = U[:, 23560:25608].bitcast(F32)
            memtmp1 = U[:, 20480:24576].bitcast(F32).rearrange("p (t d) -> p t d", t=2)
            qmT1 = U[:, 20480:24576].rearrange("p (c t) -> p c t", c=2)
            rd1 = U[:, 24576:26624].bitcast(F32)
            L1K = ["cg", "a", "cv"]

            S.label = 'L1mem'
            if not do_l0:
                mem_norm_part(1, memtmp1, L1K, [])
                mem_kv_part(1)
            S.label = 'L1norm'
            if not do_l0:
                layer_norm_phase(1)
            wo1 = X[:, :].rearrange("p (c n) -> p c n", c=10)
            load_wout(wo1, wout1_d, ["X"] + ((GKALL(sets[1], "kq") + GKALL(sets[1], "kk")) if do_l0 else []), HK)

            zi = 0
            for j in range(8):
                S.label = 'L1conv%d' % j
                wt, wk = wload(w1b_d[j], 512)
                for half in range(2):
                    tk0 = half * 1024
                    hnk = HN[8 * half:8 * half + 8]

                    def proj_half(col0, evac, bank_ids, wt=wt, wk=wk, tk0=tk0, hnk=hnk):
                        for q2 in range(2):
                            bk = bank_ids[q2]
                            for kc in range(8):
                                A("pe", lambda e, kc=kc, q2=q2, bk=bk, wt=wt, tk0=tk0, col0=col0: e.matmul(banks[bk][:, :], lhsT=wt[:, kc, col0:col0 + 128],
                                                                                rhs=hnT[:, kc, tk0 + q2 * 512:tk0 + (q2 + 1) * 512], start=(kc == 0), stop=(kc == 7)),
                                  reads=[wk] + hnk, writes=KB(bk))
                            evac(q2, banks[bk], KB(bk))

                    if half == 0:
                        A("dve", lambda e: e.memset(a_sb[:, 0:2], 0.0), writes=["a"])
                    else:
                        A("dve", lambda e: e.tensor_copy(out=a_sb[:, 0:2], in_=a_sb[:, 1024:1026]), reads=["a", "cv"], writes=["a"])
                    proj_half(128, lambda q2, bank, bkey: A("act", lambda e: e.activation(out=cgs[:, q2 * 512:(q2 + 1) * 512], in_=bank[:, :], func=AF.Copy),
                                                            reads=list(bkey), writes=["cg"]), (0, 1))
                    proj_half(256, lambda q2, bank, bkey: A("dve", lambda e: e.tensor_tensor(out=a_sb[:, 2 + q2 * 512:2 + (q2 + 1) * 512], in0=bank[:, :],
                                                                                             in1=cgs[:, q2 * 512:(q2 + 1) * 512], op=ALU.mult),
                                                            reads=list(bkey) + ["cg"], writes=["a"]), (3, 4))
                    A("act", lambda e, j=j: e.activation(out=cv[:, :], in_=a_sb[:, 2:1026], func=AF.Copy, scale=cws[:, j, 2:3]), reads=["a", "consts"], writes=["cv"])
                    A("dve", lambda e, j=j: e.scalar_tensor_tensor(out=cv[:, :], in0=a_sb[:, 1:1025], scalar=cws[:, j, 1:2], in1=cv[:, :], op0=ALU.mult, op1=ALU.add),
                      reads=["a", "cv", "consts"], writes=["cv"])
                    A("dve", lambda e, j=j: e.scalar_tensor_tensor(out=cv[:, :], in0=a_sb[:, 0:1024], scalar=cws[:, j, 0:1], in1=cv[:, :], op0=ALU.mult, op1=ALU.add),
                      reads=["a", "cv", "consts"], writes=["cv"])
                    proj_half(0, lambda q2, bank, bkey: A("dve", lambda e: e.tensor_tensor(out=cv[:, q2 * 512:(q2 + 1) * 512], in0=bank[:, :],
                                                                                           in1=cv[:, q2 * 512:(q2 + 1) * 512], op=ALU.mult),
                                                          reads=list(bkey) + ["cv"], writes=["cv"]), (5, 6))

                    def ev_z1(q2, bank, bkey, j=j, tk0=tk0, half=half):
                        nonlocal zi
                        szb = szt[zi % 2]
                        szk = ("sz", zi % 2)
                        zi += 1
                        A("act", lambda e: e.activation(out=szb[:], in_=bank[:, :], func=AF.Silu), reads=list(bkey), writes=[szk])
                        A("pool", lambda e: e.tensor_tensor(out=yT1[:, j, tk0 + q2 * 512:tk0 + (q2 + 1) * 512], in0=cv[:, q2 * 512:(q2 + 1) * 512],
                                                            in1=szb[:], op=ALU.mult),
                          reads=[szk, "cv"], writes=[("y", j, half * 2 + q2)])
                    proj_half(384, ev_z1, (7, 2))
                    if j == 0 and half == 0 and do_l0:
                        for fn in l1_deferred:
                            fn()

            S.label = 'L1qm'
            wt, wk = wload(w1qm_d, 256)
            for mc in range(2):
                def ev_qm1(qd, bank, bkey, mc=mc):
                    A("act", lambda e: e.activation(out=qmT1[:, mc, qd * 512:(qd + 1) * 512], in_=bank[:, :], func=AF.Copy),
                      reads=list(bkey), writes=L1K)
                proj_fm(wt, wk, mc * 128, ev_qm1, (0, 1))
            S.label = 'L1memattn'
            wz2 = wload(w1z2_d, 256)
            mits1 = make_memattn(qmT1, [[L1K[0]], [L1K[0]]], yT1, 8, rd1, "rdB", rd_first=["cv"])
            for i, (S_fn, PV_fn) in enumerate(mits1):
                S_fn()
                zc, qd = divmod(i, 4)

                def ev_z2(qd, bank, bkey, zc=zc, i=i):
                    szb = szt[i % 2]
                    szk = ("sz", i % 2)
                    A("act", lambda e: e.activation(out=szb[:], in_=bank[:, :], func=AF.Tanh, scale=0.5), reads=list(bkey), writes=[szk])
                    A("dve", lambda e: e.scalar_tensor_tensor(out=yT1[:, 8 + zc, qd * 512:(qd + 1) * 512], in0=szb[:], scalar=1.0, in1=bank[:, :],
                                                              op0=ALU.add, op1=ALU.mult),
                      reads=list(bkey) + [szk], writes=[("y", 8 + zc, qd)])
                proj_unit(wz2[0], wz2[1], zc * 128, qd, (0, 1)[i % 2], ev_z2)
                PV_fn()
            S.label = 'L1outproj'
            ov = out_d.rearrange("(t p) d -> p t d", p=128)
            fgs = wr[0][:].rearrange("p a b -> p (a b)")[:, 0:2048].bitcast(F32)
            A("sp", lambda e: e.dma_start(out=fgs, in_=fg_d), writes=[("wr", 0)], chan="fg")
            ost = [U[:, 20480:22528].bitcast(F32), U[:, 22528:24576].bitcast(F32)]
            ykeys = [[("ost", 0)], [("ost", 1)]]

            FG = [(0, 4), (4, 8), (8, 12), (12, 14), (14, 15), (15, 16)]

            def final_apply(gi):
                t0_, t1_ = FG[gi]
                for t in range(t0_, t1_):
                    sidx = 20 + t
                    o = ost[t % 2]
                    A("dve", lambda e, t=t, o=o, sidx=sidx: e.scalar_tensor_tensor(out=o, in0=h[:, t, :], scalar=rstd[:, sidx:sidx + 1], in1=fgs,
                                                                                   op0=ALU.mult, op1=ALU.mult),
                      reads=[HK[t], ("rs", 20 + t0_), ("wr", 0)], writes=ykeys[t % 2] + (L1K if t < 2 else []))
                    A("sp", lambda e, t=t, o=o: e.dma_start(out=ov[:, t, :], in_=o), reads=ykeys[t % 2], chan="o%d" % (t % 2))

            def final_tile_cb(t):
                for gi, (t0_, t1_) in enumerate(FG):
                    if t == t1_ - 1:
                        rms_stats([(h[:, tt, :], [HK[tt]]) for tt in range(t0_, t1_)], 20 + t0_)
                        if gi > 0:
                            final_apply(gi - 1)
                        if gi == len(FG) - 1:
                            final_apply(gi)
            out_proj(yT1, 10, wo1, ["X"], after_tile=final_tile_cb)

        frozen["f"] = False
        if do_l1:
            fw = {"sp": [("o0", 8), ("o1", 8)]}
        else:
            ov = out_d.rearrange("(t p) d -> p t d", p=128)
            for i in range(4):
                A("sp", lambda e, i=i: e.dma_start(out=ov[:, 4 * i:4 * i + 4, :], in_=h[:, 4 * i:4 * i + 4, :]), reads=HK[4 * i:4 * i + 4], chan="o%d" % (i % 2))
            fw = {"sp": [("o0", 2), ("o1", 2)]}
        S.emit_all(block, sems, chans, fw)
        if os.environ.get('KLABELS'):
            import json
            json.dump(S.labels, open(os.environ['KLABELS'], 'w'))
    return nc


def _consts():
    ident = np.eye(128, dtype=np.float32)
    k = np.arange(128)[:, None]
    q = np.arange(128)[None, :]
    diag = (q >= k).astype(np.float32)
    prev = (q <= k).astype(np.float32)
    maskA = np.concatenate([prev, diag, prev, diag], axis=1)
    maskD = np.concatenate([diag, diag, diag, diag], axis=1)
    half = 8
    invf = (np.float32(ROPE_THETA) ** (-np.arange(half, dtype=np.float32) * np.float32(2.0 / 16))).astype(np.float32)
    invf = np.broadcast_to(invf[None, :], (128, half)).copy()
    return ident, maskA, maskD, invf


def _vec_layout(v):
    L = v.shape[0]
    return np.ascontiguousarray(v.reshape(L, 8, 128).transpose(2, 0, 1))


def _prep_shared(norm_g, mem_norm_g, w_mem_kv, attn_w_in, attn_w_out, conv_w_in, conv_w, conv_w_out, final_g):
    ident, maskA, maskD, invf = _consts()
    d = {"ident": ident, "maskA": maskA, "maskD": maskD}
    ng = _vec_layout(np.asarray(norm_g, np.float32)).reshape(128, 16)
    mg = _vec_layout(np.asarray(mem_norm_g, np.float32)).reshape(128, 16)
    d["wkv"] = np.ascontiguousarray(w_mem_kv, dtype=np.float32)
    w0 = np.asarray(attn_w_in[0], np.float32)
    blocks = []
    for c in range(4):
        for g in range(3):
            o = g * 512 + c * 128
            blocks.append(np.concatenate([w0[:, o:o + 128], w0[:, 1536 + o:1536 + o + 128], w0[:, 3072 + o:3072 + o + 128]], axis=1))
    d["w0a"] = np.ascontiguousarray(np.stack(blocks))
    d["w0qm"] = np.ascontiguousarray(w0[:, 4608:4864])
    d["w0z"] = np.ascontiguousarray(np.stack([w0[:, 4864:4864 + 384], w0[:, 4864 + 384:5632]]))
    d["wout0"] = np.ascontiguousarray(attn_w_out[0], dtype=np.float32)
    w1 = np.asarray(conv_w_in[0], np.float32)
    b1 = []
    for j in range(8):
        o = j * 128
        b1.append(np.concatenate([w1[:, o:o + 128], w1[:, 1024 + o:1024 + o + 128], w1[:, 2048 + o:2048 + o + 128],
                                  w1[:, 3328 + o:3328 + o + 128]], axis=1))
    d["w1b"] = np.ascontiguousarray(np.stack(b1))
    d["w1qm"] = np.ascontiguousarray(w1[:, 3072:3328])
    d["w1z2"] = np.ascontiguousarray(w1[:, 3328 + 1024:3328 + 1280])
    d["wout1"] = np.ascontiguousarray(conv_w_out[0], dtype=np.float32)
    cw = np.asarray(conv_w[0], np.float32)
    cwl = np.ascontiguousarray(cw.reshape(3, 8, 128).transpose(2, 1, 0)).reshape(128, 24)
    d["_cpk_head"] = np.ascontiguousarray(np.concatenate([ng, mg, invf, cwl], axis=1))
    d["fg"] = np.ascontiguousarray(np.broadcast_to(np.asarray(final_g, np.float32)[None, :], (128, DM)))
    return d


def _pos_layout(pos_row):
    out = np.empty((128, 48), np.int32)
    p = np.arange(128)
    for g in range(3):
        for b in range(16):
            out[:, g * 16 + b] = pos_row[tok_start(g, b) + p * DIL[g]]
    return out


L0_KEYS = ["x", "mem", "ident", "cpk", "wkv", "maskA", "maskD", "w0a", "w0z", "w0qm", "wout0"]
L1_KEYS = ["x", "mem", "ident", "cpk", "wkv", "fg", "w1b", "w1z2", "w1qm", "wout1"]
FULL_KEYS = L0_KEYS + [k for k in L1_KEYS if k not in L0_KEYS]

_CACHE = {}


def _get_nc(mode):
    if mode not in _CACHE:
        st = os.environ.get("KSTOP")
        _CACHE[mode] = build(mode, stop=int(st) if st else None)
    return _CACHE[mode]


def kernel(x, mem, positions, norm_g, mem_norm_g, w_mem_kv, attn_w_in, attn_w_out,
           conv_w_in, conv_w, conv_w_out, final_g, _mode="full"):
    x = np.asarray(x, np.float32)
    mem = np.asarray(mem, np.float32)
    positions = np.asarray(positions, np.int32)
    B = x.shape[0]
    shared = _prep_shared(norm_g, mem_norm_g, w_mem_kv, attn_w_in, attn_w_out, conv_w_in, conv_w, conv_w_out, final_g)

    def run(mode, xs, keys):
        nc = _get_nc(mode)
        in_maps = []
        for b in range(B):
            m = dict(shared)
            m["x"] = np.ascontiguousarray(xs[b])
            m["mem"] = np.ascontiguousarray(mem[b])
            m["cpk"] = np.ascontiguousarray(np.concatenate([shared["_cpk_head"], _pos_layout(positions[b]).view(np.float32)], axis=1))
            in_maps.append({k: m[k] for k in keys})
        res = run_bass_kernel_spmd(nc, in_maps, core_ids=list(range(B)))
        return np.stack([r["out"] for r in res.results], axis=0)

    if _mode == "full":
        return run("full", x, FULL_KEYS)
    if _mode == "l0":
        return run("l0", x, L0_KEYS)
    if _mode == "unfused":
        h1 = run("l0", x, L0_KEYS)
        return run("l1", h1, L1_KEYS)
    raise ValueError(_mode)
```

```python
import os
import numpy as np
from contextlib import ExitStack
import concourse.bass as bass
import concourse.mybir as mybir
from concourse.bass_utils import run_bass_kernel_spmd

F32 = mybir.dt.float32
BF16 = mybir.dt.bfloat16
I32 = mybir.dt.int32
AF = mybir.ActivationFunctionType
ALU = mybir.AluOpType
AX = mybir.AxisListType

ENGS = ("pe", "act", "dve", "pool", "sp")

S_TOK = 2048
DM = 1024
NT = 16
DIL = (1, 4, 16)
EPS = 1e-6
ROPE_THETA = 500000.0
PI = float(np.pi)


class _Op:
    __slots__ = ("eng", "emit", "waits", "signal", "idx", "chan", "vc")


class Sched:
    SAME_WIN = {"pe": 0, "act": 1, "dve": 2, "pool": 1 << 30, "sp": 0}

    def __init__(self):
        self.streams = {e: [] for e in ENGS}
        self.last_w = {}
        self.readers = {}
        self.clock = {e: {} for e in ENGS}
        self.chan_cnt = {}
        self.label = ''
        self.labels = {e: [] for e in ENGS}

    def add(self, eng, emit, reads=(), writes=(), chan=None):
        op = _Op()
        op.eng, op.emit, op.chan, op.signal = eng, emit, chan, False
        op.idx = len(self.streams[eng])
        deps = []
        for r in reads:
            t = self.last_w.get(r)
            if t is not None:
                deps.append(t)
        for w in writes:
            t = self.last_w.get(w)
            if t is not None:
                deps.append(t)
            deps.extend(self.readers.get(w, ()))
        clk = self.clock[eng]
        waits = []
        for t in deps:
            if t[0] == "e":
                _, E, k, vc = t
                if E == eng:
                    if op.idx - k <= self.SAME_WIN[eng] and clk.get(("self", E), -1) < k:
                        waits.append(("e", E, k))
                        clk[("self", E)] = k
                        self.streams[E][k].signal = True
                    continue
                if clk.get(E, -1) >= k:
                    continue
                waits.append(("e", E, k))
                self.streams[E][k].signal = True
            else:
                _, E, k, vc = t
                if clk.get(E, -1) >= k:
                    continue
                waits.append(("d", E, k))
            for kk, vv in vc.items():
                if clk.get(kk, -1) < vv:
                    clk[kk] = vv
            clk[E] = max(clk.get(E, -1), k)
        best = {}
        for w in waits:
            key = (w[0], w[1])
            if key not in best or best[key][2] < w[2]:
                best[key] = w
        op.waits = list(best.values())
        vc = {k: v for k, v in clk.items() if not isinstance(k, tuple)}
        if chan is None:
            vc[eng] = op.idx
            tok = ("e", eng, op.idx, vc)
        else:
            n = self.chan_cnt.get(chan, 0) + 1
            self.chan_cnt[chan] = n
            tok = ("d", chan, n, vc)
        self.streams[eng].append(op)
        self.labels[eng].append(self.label)
        for r in reads:
            self.readers.setdefault(r, []).append(tok)
        for w in writes:
            self.last_w[w] = tok
            self.readers[w] = []
        return tok

    def emit_all(self, block, sems, chan_sems, final_waits):
        rank = {}
        for e in ENGS:
            c = 0
            rk = {}
            for op in self.streams[e]:
                if op.signal and op.chan is None:
                    c += 1
                    rk[op.idx] = c
            rank[e] = rk

        def run(e, engine):
            for op in self.streams[e]:
                for w in op.waits:
                    if w[0] == "e":
                        engine.wait_ge(sems[w[1]], rank[w[1]][w[2]])
                    else:
                        engine.wait_ge(chan_sems[w[1]], 16 * w[2])
                ins = op.emit(engine)
                if op.chan is not None:
                    ins.then_inc(chan_sems[op.chan], 16)
                elif op.signal:
                    ins.then_inc(sems[e], 1)
            for (C, n) in final_waits.get(e, ()):
                engine.wait_ge(chan_sems[C], 16 * n)

        names = {"pe": "tensor", "act": "scalar", "dve": "vector", "pool": "gpsimd", "sp": "sync"}
        for e in ENGS:
            if not self.streams[e] and e not in final_waits:
                continue
            getattr(block, names[e])(lambda engine, e=e: run(e, engine))


def sl(start, n, step=1):
    return slice(start, start + (n - 1) * step + 1, step)


def tok_start(g, b):
    d = DIL[g]
    nb = NT // d
    r, n = divmod(b, nb)
    return n * 128 * d + r


class _Stop(Exception):
    pass


def build(mode="full", stop=None):
    do_l0 = mode in ("full", "l0")
    do_l1 = mode in ("full", "l1")
    nc = bass.Bass("TRN2", target_bir_lowering=False)

    def din(name, shape, dt=F32):
        return nc.dram_tensor(name, list(shape), dt, kind="ExternalInput").ap()

    x_d = din("x", [S_TOK, DM])
    mem_d = din("mem", [256, DM])
    ident_d = din("ident", [128, 128])
    cpk_d = din("cpk", [128, 112])
    wkv_d = din("wkv", [2, DM, 512])
    if do_l0:
        maskA_d = din("maskA", [128, 512])
        maskD_d = din("maskD", [128, 512])
        w0a_d = din("w0a", [12, DM, 384])
        w0z_d = din("w0z", [2, DM, 384])
        w0qm_d = din("w0qm", [DM, 256])
        wout0_d = din("wout0", [768, DM])
    if do_l1:
        fg_d = din("fg", [128, DM])
        w1b_d = din("w1b", [8, DM, 512])
        w1z2_d = din("w1z2", [DM, 256])
        w1qm_d = din("w1qm", [DM, 256])
        wout1_d = din("wout1", [1280, DM])
    out_d = nc.dram_tensor("out", [S_TOK, DM], F32, kind="ExternalOutput").ap()

    S = Sched()
    frozen = {"f": False}

    def A(eng, emit, reads=(), writes=(), chan=None):
        if frozen["f"]:
            return None
        return S.add(eng, emit, reads, writes, chan)

    def stage(n):
        S.label = 'st%d' % n
        if stop is not None and n >= stop:
            frozen["f"] = True
    with ExitStack() as es:
        def sb(name, shape, dt):
            return es.enter_context(nc.sbuf_tensor(name, list(shape), dt))

        h = sb("h", [128, NT, DM], F32)
        hnT = sb("hnT", [128, 8, S_TOK], BF16)
        U = sb("U", [128, 26624], BF16)
        wr = [sb("wr%d" % i, [128, 8, 512], BF16) for i in range(2)]
        memnT = sb("memnT", [128, 8, 256], BF16)
        kmT = sb("kmT", [128, 2, 256], BF16)
        vm = sb("vm", [128, 2, 256], BF16)
        identb = sb("identb", [128, 128], BF16)
        ones64 = sb("ones64", [128, 64], BF16)
        cpk = sb("cpks", [128, 112], F32)
        ngs = cpk[:, 0:16].rearrange("p (l k) -> p l k", l=2)
        mgs = cpk[:, 16:32].rearrange("p (l k) -> p l k", l=2)
        invfs = cpk[:, 32:40]
        cws = cpk[:, 40:64].rearrange("p (j k) -> p j k", j=8)
        pos_i = cpk[:, 64:112].bitcast(I32)
        xn = [sb("xn%d" % i, [128, DM], BF16) for i in range(2)]
        ss = sb("ss", [128, 40], F32)
        rstd = sb("rstd", [128, 40], F32)
        Ptt = sb("Ptt", [128, 4, 512], BF16)
        Pt = [Ptt[:, i, :] for i in range(4)]
        sqjunk = Ptt[:, 0:2, :].rearrange("p a b -> p (a b)")
        szt = [sb("sz%d" % i, [128, 512], BF16) for i in range(2)]
        mtb = [sb("mtb%d" % i, [128, 512], BF16) for i in range(2)]
        X = sb("X", [128, 10240], BF16)
        if do_l0:
            qkvB = X[:, 0:6144]
            qkst = [X[:, 6144 + 1024 * i:6144 + 1024 * (i + 1)].rearrange("p (b c) -> p b c", b=4) for i in range(2)]
            rt = [X[:, 8192 + 256 * i:8192 + 256 * (i + 1)].bitcast(F32).rearrange("p (a b c) -> p a b c", a=4, b=4) for i in range(4)]
            maskAb = X[:, 9216:9728]
            maskDb = X[:, 9728:10240]
            posf = sb("posf", [128, 48], F32)
            ang = qkvB[:, 0:768].bitcast(F32).rearrange("p (a b) -> p a b", a=48)
            ang2 = qkvB[:, 768:1536].bitcast(F32).rearrange("p (a b) -> p a b", a=48)
            angi = qkvB[:, 1536:2304].bitcast(I32).rearrange("p (a b) -> p a b", a=48)
            halfpi = sb("halfpi", [128, 1], F32)
            cosT = sb("cosT", [128, 48, 8], F32)
            sinT = sb("sinT", [128, 48, 8], F32)
        banks = [es.enter_context(nc.psum_tensor("bank%d" % i, [128, 512], F32)) for i in range(8)]
        sems = {e: es.enter_context(nc.semaphore("s_" + e)) for e in ENGS}
        chan_names = ["x0", "x1", "x2", "x3", "c", "cp", "w0", "w1", "wo", "mem", "fg", "o0", "o1"]
        chans = {c: es.enter_context(nc.semaphore("c_" + c)) for c in chan_names}
        block = es.enter_context(nc.Block())

        def KB(i):
            return [("bank", i)]
        HK = [("h", t) for t in range(NT)]
        HN = [("hnT", t) for t in range(NT)]

        xv = x_d.rearrange("(t p) d -> p t d", p=128)

        def load_x(chunks=(0, 1, 2, 3)):
            for i in chunks:
                A("sp", lambda e, i=i: e.dma_start(out=h[:, 4 * i:4 * i + 4, :], in_=xv[:, 4 * i:4 * i + 4, :]),
                  writes=HK[4 * i:4 * i + 4], chan="x%d" % i)
        cl = [(cpk, cpk_d, "sp"), (identb, ident_d, "pool")]
        if do_l0:
            cl += [(maskAb, maskA_d, "pool"), (maskDb, maskD_d, "pool")]
        for (dst, src, q) in cl:
            A(q, lambda e, dst=dst, src=src: e.dma_start(out=dst[:], in_=src), writes=["constsP" if q == "pool" else "consts"],
              chan=("cp" if q == "pool" else "c"))
        load_x((0, 1))
        load_x((2, 3))
        A("pool", lambda e: e.memset(ones64[:], 1.0), writes=["ones"])
        A("dve", lambda e: e.memset(ss[:], 0.0), writes=["ss"])
        if do_l0:
            A("dve", lambda e: e.memset(halfpi[:], PI / 2), writes=["halfpi"])

        wplan = []
        if do_l0:
            wplan += [(w0a_d[0], 384), (w0a_d[1], 384), (wkv_d[0], 512)] + [(w0a_d[i], 384) for i in range(2, 12)] + [(w0qm_d, 256), (w0z_d[1], 384), (w0z_d[0], 384)]
        if do_l1:
            wplan += [(wkv_d[1], 512)] + [(w1b_d[j], 512) for j in range(8)] + [(w1qm_d, 256), (w1z2_d, 256)]
        wstate = {"cur": 0, "issued": 0}

        def _wissue():
            k = wstate["issued"]
            if k >= len(wplan):
                return
            src_ap, ncols = wplan[k]
            i = k % 2
            wstate["issued"] += 1
            A("pool", lambda e: e.dma_start(out=wr[i][:, :, 0:ncols], in_=src_ap.rearrange("(kc p) n -> p kc n", p=128)),
              writes=[("wr", i)], chan="w%d" % i)

        def wload(src_ap, ncols, prefetch=True):
            k = wstate["cur"]
            assert wplan[k][1] == ncols, (k, ncols, wplan[k][1])
            while wstate["issued"] <= min(k + (1 if prefetch else 0), len(wplan) - 1):
                _wissue()
            wstate["cur"] += 1
            return wr[k % 2], ("wr", k % 2)

        def rms_squares(tiles, s0):
            for i, (src_tile, src_keys) in enumerate(tiles):
                A("act", lambda e, src_tile=src_tile, i=i: e.activation(out=sqjunk, in_=src_tile, func=AF.Square, accum_out=ss[:, s0 + i:s0 + i + 1]),
                  reads=list(src_keys) + ["ss"], writes=[("ss", s0 + i), ("P", 0), ("P", 1)])

        def rms_rstd(n, s0):
            A("dve", lambda e: e.tensor_scalar(out=rstd[:, s0:s0 + n], in0=ss[:, s0:s0 + n], scalar1=1.0 / DM, scalar2=EPS,
                                               op0=ALU.mult, op1=ALU.add), reads=[("ss", s0 + i) for i in range(n)] + ["ss"], writes=[("rs0", s0)])
            A("act", lambda e: e.activation(out=rstd[:, s0:s0 + n], in_=rstd[:, s0:s0 + n], func=AF.Sqrt), reads=[("rs0", s0)], writes=[("rs1", s0)])
            A("dve", lambda e: e.reciprocal(out=rstd[:, s0:s0 + n], in_=rstd[:, s0:s0 + n]), reads=[("rs1", s0)], writes=[("rs", s0)])

        def rms_stats(tiles, s0):
            rms_squares(tiles, s0)
            rms_rstd(len(tiles), s0)

        def rmsnorm_T(src_tile, src_keys, gvec, dstT, dst_col0, dst_keys, sidx, s0, i):
            xb = xn[i % 2]
            xk = ("xn", i % 2)
            A("act", lambda e: e.activation(out=xb[:], in_=src_tile, func=AF.Copy, scale=rstd[:, sidx:sidx + 1]),
              reads=list(src_keys) + [("rs", s0)], writes=[xk])
            tb_ = (2, 4)[i % 2]
            pb = banks[tb_][:].bitcast(BF16)
            for kc in range(8):
                A("pe", lambda e, kc=kc: e.transpose(out=pb[:, kc * 128:(kc + 1) * 128], in_=xb[:, kc * 128:(kc + 1) * 128], identity=identb[:]),
                  reads=[xk, "constsP"], writes=KB(tb_))
            A("dve", lambda e: e.tensor_tensor(out=dstT[:, :, dst_col0:dst_col0 + 128],
                                               in0=pb.rearrange("p (k t) -> p k t", k=8),
                                               in1=gvec.unsqueeze(2).to_broadcast([128, 8, 128]), op=ALU.mult),
              reads=KB(tb_) + ["consts"], writes=list(dst_keys))

        def layer_norm_cb(l, defer_last=None):
            def apply(b):
                for t in range(4 * b, 4 * b + 4):
                    rmsnorm_T(h[:, t, :], [HK[t]], ngs[:, l, :], hnT, t * 128, [HN[t]], t, 4 * b, t)

            def cb(b):
                rms_squares([(h[:, t, :], [HK[t]]) for t in range(4 * b, 4 * b + 4)], 4 * b)
                if b >= 1:
                    rms_rstd(4, 4 * (b - 1))
                if b >= 2:
                    apply(b - 2)
                if b == 3:
                    apply(1)
                    rms_rstd(4, 12)
                    if defer_last is not None:
                        defer_last.append(lambda: (apply(2), apply(3)))
                    else:
                        apply(2)
                        apply(3)
            return cb

        def layer_norm_phase(l, units=(), after_stats1=None):
            units = list(units)
            tiles_done = 0

            def stats(b):
                rms_stats([(h[:, t, :], [HK[t]]) for t in range(4 * b, 4 * b + 4)], 4 * b)

            stats(0)
            for b in range(4):
                if b + 1 < 4:
                    stats(b + 1)
                if b == 0 and after_stats1 is not None:
                    after_stats1()
                ready = 4 * b
                while units and tiles_done < ready:
                    kind, u = units.pop(0)
                    u()
                    if kind == "t":
                        tiles_done += 1
                while units and units[0][0] != "t":
                    units.pop(0)[1]()
                for t in range(4 * b, 4 * b + 4):
                    rmsnorm_T(h[:, t, :], [HK[t]], ngs[:, l, :], hnT, t * 128, [HN[t]], t, 4 * b, t)
            for kind, u in units:
                u()

        def mem_stats_part(l, tmp_ap, tmp_keys, extra_reads):
            A("sp", lambda e: e.dma_start(out=tmp_ap, in_=mem_d.rearrange("(t p) d -> p t d", p=128)),
              reads=list(extra_reads), writes=list(tmp_keys), chan="mem")
            rms_stats([(tmp_ap[:, t, :], tmp_keys) for t in range(2)], 16 + 2 * l)

        def mem_apply_part(l, tmp_ap, tmp_keys):
            for t in range(2):
                rmsnorm_T(tmp_ap[:, t, :], tmp_keys, mgs[:, l, :], memnT, t * 128, ["memnT"], 16 + 2 * l + t, 16 + 2 * l, t)

        def mem_norm_part(l, tmp_ap, tmp_keys, extra_reads):
            mem_stats_part(l, tmp_ap, tmp_keys, extra_reads)
            mem_apply_part(l, tmp_ap, tmp_keys)

        def mem_kv_part(l):
            wt, wk = wload(wkv_d[l], 512)
            for mc in range(2):
                for kc in range(8):
                    A("pe", lambda e, mc=mc, kc=kc: e.matmul(banks[0][:, mc * 256:(mc + 1) * 256], lhsT=wt[:, kc, mc * 128:(mc + 1) * 128],
                                                             rhs=memnT[:, kc, :], start=(kc == 0), stop=(kc == 7)),
                      reads=[wk, "memnT"], writes=KB(0))
            A("act", lambda e: e.activation(out=kmT[:].rearrange("p a b -> p (a b)"), in_=banks[0][:, :], func=AF.Copy),
              reads=KB(0), writes=["kmT"])
            for mb in range(2):
                for kc in range(8):
                    A("pe", lambda e, mb=mb, kc=kc: e.matmul(banks[1][:, mb * 256:(mb + 1) * 256], lhsT=memnT[:, kc, mb * 128:(mb + 1) * 128],
                                                             rhs=wt[:, kc, 256:512], start=(kc == 0), stop=(kc == 7)),
                      reads=[wk, "memnT"], writes=KB(1))
            A("dve", lambda e: e.tensor_copy(out=vm[:].rearrange("p a b -> p (a b)"), in_=banks[1][:, :]),
              reads=KB(1), writes=["vm"])

        def make_memattn(qmT, qm_keys, yT, ychunk0, rd_ap, rd_key, rd_first=()):
            out = []
            sbank = {(0, 0): 3, (1, 0): 4, (0, 1): 5, (1, 1): 6}
            for it in range(8):
                mc, qd = divmod(it, 4)
                qs = slice(qd * 512, (qd + 1) * 512)

                def S_fn(mc=mc, qs=qs):
                    for mb in range(2):
                        for hh in range(2):
                            hp = slice(hh * 64, hh * 64 + 64)
                            bk = sbank[(hh, mb)]
                            A("pe", lambda e, mb=mb, hp=hp, bk=bk: e.matmul(
                                banks[bk][:, :], lhsT=kmT[hp, mc, mb * 128:(mb + 1) * 128], rhs=qmT[hp, mc, qs], start=True, stop=True),
                              reads=["kmT"] + list(qm_keys[mc]), writes=KB(bk))
                    for mb in range(2):
                        for hh in range(2):
                            bk = sbank[(hh, mb)]
                            pi = hh * 2 + mb
                            A("act", lambda e, bk=bk, pi=pi: e.activation(out=Pt[pi][:], in_=banks[bk][:, :], func=AF.Exp, scale=0.125),
                              reads=KB(bk), writes=[("P", pi)])

                def PV_fn(mc=mc, qd=qd, qs=qs, it=it):
                    for hh in range(2):
                        ph = slice(hh * 64, hh * 64 + 64)
                        tp = (0, 64) if hh else None
                        for mb in range(2):
                            pi = hh * 2 + mb
                            A("pe", lambda e, mb=mb, hh=hh, ph=ph, pi=pi, tp=tp: e.matmul(
                                banks[7][ph, :], lhsT=vm[:, mb, mc * 128 + hh * 64:mc * 128 + hh * 64 + 64], rhs=Pt[pi][:],
                                start=(mb == 0), stop=(mb == 1), tile_position=tp),
                              reads=["vm", ("P", pi)], writes=KB(7))
                        for mb in range(2):
                            pi = hh * 2 + mb
                            A("pe", lambda e, mb=mb, ph=ph, pi=pi, tp=tp: e.matmul(
                                banks[2][ph, :], lhsT=ones64[:], rhs=Pt[pi][:], start=(mb == 0), stop=(mb == 1), tile_position=tp),
                              reads=["ones", ("P", pi)], writes=KB(2))
                    rdv = rd_ap[:, (it % 2) * 512:(it % 2) * 512 + 512]
                    rk = (rd_key, it % 2)
                    tb = mtb[it % 2]
                    tk = ("mtb", it % 2)
                    A("act", lambda e: e.activation(out=tb[:], in_=banks[7][:, :], func=AF.Copy, scale=0.5), reads=KB(7), writes=[tk])
                    A("act", lambda e: e.activation(out=rdv, in_=banks[2][:, :], func=AF.Copy), reads=KB(2), writes=[rk] + (list(rd_first) if it < 2 else []))
                    A("dve", lambda e: e.reciprocal(out=rdv, in_=rdv), reads=[rk], writes=[rk])
                    A("dve", lambda e: e.tensor_tensor(out=tb[:], in0=tb[:], in1=rdv, op=ALU.mult), reads=[tk, rk], writes=[tk])
                    A("pool", lambda e: e.tensor_tensor(out=yT[:, ychunk0 + mc, qs], in0=yT[:, ychunk0 + mc, qs], in1=tb[:], op=ALU.mult),
                      reads=[tk, ("y", ychunk0 + mc, qd)], writes=[("y", ychunk0 + mc, qd)])
                out.append((S_fn, PV_fn))
            return out

        def proj_unit(wt, wk, col0, qd, bk, evac):
            for kc in range(8):
                A("pe", lambda e, kc=kc: e.matmul(banks[bk][:, :], lhsT=wt[:, kc, col0:col0 + 128],
                                                 rhs=hnT[:, kc, qd * 512:(qd + 1) * 512], start=(kc == 0), stop=(kc == 7)),
                  reads=[wk] + HN[4 * qd:4 * qd + 4], writes=KB(bk))
            evac(qd, banks[bk], KB(bk))

        def proj_fm(wt, wk, col0, evac, bank_ids):
            for qd in range(4):
                bk = bank_ids[qd % len(bank_ids)]
                for kc in range(8):
                    A("pe", lambda e, kc=kc, qd=qd, bk=bk: e.matmul(banks[bk][:, :], lhsT=wt[:, kc, col0:col0 + 128],
                                                                    rhs=hnT[:, kc, qd * 512:(qd + 1) * 512], start=(kc == 0), stop=(kc == 7)),
                      reads=[wk] + HN[4 * qd:4 * qd + 4], writes=KB(bk))
                evac(qd, banks[bk], KB(bk))

        def load_wout(wo, wout_d, wo_keys, extra_reads=()):
            A("pool", lambda e: e.dma_start(out=wo, in_=wout_d.rearrange("(c p) n -> p c n", p=128)),
              reads=list(extra_reads), writes=list(wo_keys), chan="wo")

        def out_proj(yT, nchunk, wo, wo_keys, after_batch=None, after_tile=None):
            for t in range(NT):
                for hf in range(2):
                    bk = (0, 1, 7, 3)[(t * 2 + hf) % 4]
                    for c in range(nchunk):
                        A("pe", lambda e, t=t, hf=hf, c=c, bk=bk: e.matmul(banks[bk][:, :], lhsT=yT[:, c, t * 128:(t + 1) * 128],
                                                                           rhs=wo[:, c, hf * 512:(hf + 1) * 512], start=(c == 0), stop=(c == nchunk - 1)),
                          reads=list(wo_keys) + [("y", c, t // 4)], writes=KB(bk))
                    A("dve", lambda e, t=t, hf=hf, bk=bk: e.tensor_tensor(out=h[:, t, hf * 512:(hf + 1) * 512], in0=h[:, t, hf * 512:(hf + 1) * 512],
                                                                          in1=banks[bk][:, :], op=ALU.add),
                      reads=KB(bk) + [HK[t]], writes=[HK[t]])
                if after_batch is not None and t % 4 == 3:
                    after_batch(t // 4)
                if after_tile is not None:
                    after_tile(t)

        if do_l0:
            yT0 = U[:, 0:12288].rearrange("p (c t) -> p c t", c=6)
            qT = U[:, 12288:14336]
            kT = U[:, 14336:16384]
            Vt = U[:, 16384:18432].rearrange("p (b c) -> p b c", b=16)
            accn = U[:, 18432:22528].bitcast(F32)
            accd = U[:, 22528:26624].bitcast(F32)
            qmT0 = U[:, 12288:16384].rearrange("p (c t) -> p c t", c=2)
            memtmp0 = U[:, 18432:22528].bitcast(F32).rearrange("p (t d) -> p t d", t=2)

            stage(1)
            memtmp0 = U[:, 8192:12288].bitcast(F32).rearrange("p (t d) -> p t d", t=2)
            MK0 = [("y", 4 + i, qd) for i in range(2) for qd in range(4)]

            QB = qkvB
            sets = [
                dict(qT=U[:, 12288:14336], kT=U[:, 14336:16384], V=U[:, 16384:18432].rearrange("p (b c) -> p b c", b=16), kq="qT", kk="kT", kv="V", ix=0),
                dict(qT=QB[:, 0:2048], kT=QB[:, 2048:4096], V=QB[:, 4096:6144].rearrange("p (b c) -> p b c", b=16), kq="qT1", kk="kT1", kv="V1", ix=1),
            ]
            def GK(st0, which, b4):
                return (st0[which], b4)

            def GKALL(st0, which):
                return [(st0[which], b4) for b4 in range(4)]
            tile_ctr = {"n": 0}
            TB = (0, 1, 7)

            def make_inproj(k):
                c, g = divmod(k, 3)
                d = DIL[g]
                st_ = sets[k % 2]
                wt, wk = wload(w0a_d[k], 384)
                units = []
                for b4 in range(4):
                    stg = qkst[b4 % 2]
                    stk = ("qkst", b4 % 2)
                    for bi in range(4):
                        def tile_unit(b4=b4, bi=bi, stg=stg, stk=stk):
                            b = b4 * 4 + bi
                            t0 = tok_start(g, b)
                            if g == 0:
                                hn_keys = [HN[b]]
                            elif g == 1:
                                hn_keys = HN[4 * (b % 4):4 * (b % 4) + 4]
                            else:
                                hn_keys = HN
                            bk = TB[tile_ctr["n"] % 3]
                            tile_ctr["n"] += 1
                            for kc in range(8):
                                A("pe", lambda e, kc=kc, t0=t0, bk=bk: e.matmul(banks[bk][:, 0:384], lhsT=hnT[:, kc, sl(t0, 128, d)],
                                                                               rhs=wt[:, kc, 0:384], start=(kc == 0), stop=(kc == 7)),
                                  reads=[wk] + hn_keys, writes=KB(bk))
                            Vt = st_["V"]
                            if b % 2 == 0:
                                A("act", lambda e: e.activation(out=stg[:, bi, :], in_=banks[bk][:, 0:256], func=AF.Copy), reads=KB(bk), writes=[stk])
                                A("act", lambda e: e.activation(out=Vt[:, b, :], in_=banks[bk][:, 256:384], func=AF.Copy), reads=KB(bk), writes=[GK(st_, "kv", b // 4)])
                            else:
                                A("dve", lambda e: e.tensor_copy(out=stg[:, bi, :], in_=banks[bk][:, 0:256]), reads=KB(bk), writes=[stk])
                                A("dve", lambda e: e.tensor_copy(out=Vt[:, b, :], in_=banks[bk][:, 256:384]), reads=KB(bk), writes=[GK(st_, "kv", b // 4)])
                            if bi == 3:
                                sv = stg[:].rearrange("p b (h d) -> p b h d", h=4)
                                t1 = sv[:, :, :, 0:8]
                                t2 = sv[:, :, :, 8:16]
                                col = g * 16 + b4 * 4
                                cb = cosT[:, col:col + 4, :].unsqueeze(2).to_broadcast([128, 4, 4, 8])
                                sbb = sinT[:, col:col + 4, :].unsqueeze(2).to_broadcast([128, 4, 4, 8])
                                A("dve", lambda e: e.tensor_tensor(out=rt[0][:], in0=t1, in1=cb, op=ALU.mult), reads=[stk, "cosT"], writes=["rt0"])
                                A("pool", lambda e: e.tensor_tensor(out=rt[2][:], in0=t2, in1=cb, op=ALU.mult), reads=[stk, "cosT"], writes=["rt2"])
                                A("dve", lambda e: e.tensor_tensor(out=rt[1][:], in0=t2, in1=sbb, op=ALU.mult), reads=[stk, "sinT"], writes=["rt1"])
                                A("pool", lambda e: e.tensor_tensor(out=rt[3][:], in0=t1, in1=sbb, op=ALU.mult), reads=[stk, "sinT"], writes=["rt3"])
                                A("dve", lambda e: e.tensor_tensor(out=t1, in0=rt[0][:], in1=rt[1][:], op=ALU.subtract), reads=["rt0", "rt1", "rt3"], writes=[stk])
                                A("pool", lambda e: e.tensor_tensor(out=t2, in0=rt[2][:], in1=rt[3][:], op=ALU.add), reads=["rt2", "rt3", "rt1"], writes=[stk])
                        units.append(("t", tile_unit))

                    def tr_unit(b4=b4, stg=stg, stk=stk):
                        pb = banks[2][:].bitcast(BF16)
                        for bi in range(4):
                            A("pe", lambda e, bi=bi: e.transpose(out=pb[:, bi * 128:(bi + 1) * 128], in_=stg[:, bi, 0:128], identity=identb[:]),
                              reads=[stk, "constsP"], writes=KB(2))
                            A("pe", lambda e, bi=bi: e.transpose(out=pb[:, 512 + bi * 128:512 + (bi + 1) * 128], in_=stg[:, bi, 128:256], identity=identb[:]),
                              reads=[stk, "constsP"], writes=KB(2))
                        qT_, kT_ = st_["qT"], st_["kT"]
                        if b4 % 2 == 0:
                            A("act", lambda e: e.activation(out=qT_[:, b4 * 512:(b4 + 1) * 512], in_=pb[:, 0:512], func=AF.Copy), reads=KB(2), writes=[GK(st_, "kq", b4)])
                            A("act", lambda e: e.activation(out=kT_[:, b4 * 512:(b4 + 1) * 512], in_=pb[:, 512:1024], func=AF.Copy), reads=KB(2), writes=[GK(st_, "kk", b4)])
                        else:
                            A("dve", lambda e: e.tensor_copy(out=qT_[:, b4 * 512:(b4 + 1) * 512], in_=pb[:, 0:512]), reads=KB(2), writes=[GK(st_, "kq", b4)])
                            A("dve", lambda e: e.tensor_copy(out=kT_[:, b4 * 512:(b4 + 1) * 512], in_=pb[:, 512:1024]), reads=KB(2), writes=[GK(st_, "kk", b4)])
                    units.append(("r", tr_unit))
                tiles = [u for u in units if u[0] == "t"]
                trs = [u for u in units if u[0] == "r"]
                order = tiles[0:8] + [trs[0]] + tiles[8:12] + [trs[1]] + tiles[12:16] + [trs[2]]
                return order, trs[3][1]

            def make_attn(k):
                c, g = divmod(k, 3)
                d = DIL[g]
                nb = NT // d
                st_ = sets[k % 2]
                qT, kT, Vt = st_["qT"], st_["kT"], st_["V"]
                if g < 2:
                    iters = []
                    for r in range(d):
                        for nh in range(nb // 2):
                            iters.append([(r * nb + 2 * nh + s, (2 * nh + s) > 0) for s in range(2)])
                    mask = maskAb
                else:
                    iters = [[(4 * i + s, False) for s in range(4)] for i in range(4)]
                    mask = maskDb
                out = []
                for iti, qbs in enumerate(iters):
                    lo = 512
                    offs = []
                    for s, (b, hp_) in enumerate(qbs):
                        if g < 2:
                            o_prev, o_diag = s * 256, s * 256 + 128
                        else:
                            o_prev, o_diag = None, s * 128
                        offs.append((o_prev, o_diag))
                        lo = min(lo, o_prev if hp_ else o_diag)
                    par = iti % 2

                    def S_fn(qbs=qbs, offs=offs, lo=lo, par=par):
                        for s, (b, hp_) in enumerate(qbs):
                            o_prev, o_diag = offs[s]
                            for part in ((0, 1) if hp_ else (1,)):
                                for hh in range(2):
                                    hp = slice(hh * 64, hh * 64 + 64)
                                    bk = 3 + hh
                                    kb = b - 1 if part == 0 else b
                                    o = o_prev if part == 0 else o_diag
                                    A("pe", lambda e, hp=hp, bk=bk, b=b, kb=kb, o=o, hh=hh: e.matmul(banks[bk][:, o:o + 128], lhsT=kT[hp, kb * 128:(kb + 1) * 128],
                                                                                             rhs=qT[hp, b * 128:(b + 1) * 128], start=True, stop=True,
                                                                                             tile_position=(64 * hh, 0)),
                                      reads=[GK(st_, "kq", b // 4), GK(st_, "kk", kb // 4)], writes=KB(bk))
                        for hh in range(2):
                            bk = 3 + hh
                            pi = par * 2 + hh
                            A("act", lambda e, bk=bk, pi=pi: e.activation(out=Pt[pi][:, lo:512], in_=banks[bk][:, lo:512], func=AF.Exp, scale=0.125),
                              reads=KB(bk), writes=[("P", pi)])
                            A("dve" if hh == 0 else "pool", lambda e, pi=pi: e.tensor_tensor(out=Pt[pi][:, lo:512], in0=Pt[pi][:, lo:512],
                                                                                          in1=mask[:, lo:512], op=ALU.mult),
                              reads=[("P", pi), "constsP"], writes=[("P", pi)])

                    def PV_fn(qbs=qbs, offs=offs, par=par, iti=iti):
                        onb, odb = 5, 6
                        for (obank, lv) in ((onb, True), (odb, False)):
                            for s, (b, hp_) in enumerate(qbs):
                                o_prev, o_diag = offs[s]
                                oc = s * 128
                                for part in ((0, 1) if hp_ else (1,)):
                                    for hh in range(2):
                                        ph = slice(hh * 64, hh * 64 + 64)
                                        tp = (0, 64 * hh)
                                        pi = par * 2 + hh
                                        kb = b - 1 if part == 0 else b
                                        o = o_prev if part == 0 else o_diag
                                        st_flag = (part == 0) or (not hp_)
                                        sp_flag = (part == 1)
                                        A("pe", lambda e, ph=ph, tp=tp, pi=pi, kb=kb, o=o, oc=oc, obank=obank, lv=lv, hh=hh, st_flag=st_flag, sp_flag=sp_flag: e.matmul(
                                            banks[obank][ph, oc:oc + 128], lhsT=(Vt[:, kb, hh * 64:hh * 64 + 64] if lv else ones64[:]),
                                            rhs=Pt[pi][:, o:o + 128], start=st_flag, stop=sp_flag, tile_position=tp),
                                          reads=[GK(st_, "kv", kb // 4), "ones", ("P", pi)], writes=KB(obank))
                        nq = len(qbs) * 128
                        if g == 0:
                            dn = accn[:, iti * 256:iti * 256 + 256]
                            dd = accd[:, iti * 256:iti * 256 + 256]
                        elif g == 1:
                            r, nh = divmod(iti, 2)
                            dn = accn[:, sl(1024 * nh + r, 256, 4)]
                            dd = accd[:, sl(1024 * nh + r, 256, 4)]
                        else:
                            dn = accn.rearrange("p (i r) -> p r i", r=16)[:, 4 * iti:4 * iti + 4, :]
                            dd = accd.rearrange("p (i r) -> p r i", r=16)[:, 4 * iti:4 * iti + 4, :]
                        srcn = banks[onb][:, 0:nq]
                        srcd = banks[odb][:, 0:nq]
                        if g == 2:
                            srcn = srcn.rearrange("p (r i) -> p r i", r=4)
                            srcd = srcd.rearrange("p (r i) -> p r i", r=4)
                        if g == 0:
                            A("act", lambda e: e.activation(out=dn, in_=srcn, func=AF.Copy), reads=KB(onb), writes=[("accn", iti // 2)])
                            A("dve", lambda e: e.tensor_copy(out=dd, in_=srcd), reads=KB(odb), writes=[("accd", iti // 2)])
                        else:
                            AN = [("accn", q_) for q_ in range(4)]
                            AD = [("accd", q_) for q_ in range(4)]
                            A("dve", lambda e: e.tensor_tensor(out=dn, in0=dn, in1=srcn, op=ALU.add), reads=KB(onb) + AN, writes=AN)
                            A("dve", lambda e: e.tensor_tensor(out=dd, in0=dd, in1=srcd, op=ALU.add), reads=KB(odb) + AD, writes=AD)
                    out.append((S_fn, PV_fn))
                return out

            NK = 12
            S.label = "norm0"
            def rope_tables():
                A("dve", lambda e: e.tensor_copy(out=posf[:], in_=pos_i[:]), reads=["consts"], writes=["posf"])
                A("dve", lambda e: e.tensor_tensor(out=ang[:], in0=posf[:].unsqueeze(2).to_broadcast([128, 48, 8]),
                                                   in1=invfs[:].unsqueeze(1).to_broadcast([128, 48, 8]), op=ALU.mult),
                  reads=["posf", "consts"], writes=["ang"])
                C1 = 6.28125
                C2 = 2.0 * np.pi - C1
                A("dve", lambda e: e.tensor_scalar(out=ang2[:], in0=ang[:], scalar1=1.0 / (2 * PI), scalar2=None, op0=ALU.mult),
                  reads=["ang"], writes=["ang2"])
                A("dve", lambda e: e.tensor_copy(out=angi[:], in_=ang2[:]), reads=["ang2"], writes=["angi"])
                A("dve", lambda e: e.tensor_copy(out=ang2[:], in_=angi[:]), reads=["angi", "ang2"], writes=["angf"])
                A("dve", lambda e: e.scalar_tensor_tensor(out=ang[:], in0=ang2[:], scalar=-C1, in1=ang[:], op0=ALU.mult, op1=ALU.add),
                  reads=["angf", "ang"], writes=["r1"])
                A("dve", lambda e: e.scalar_tensor_tensor(out=ang[:], in0=ang2[:], scalar=-C2, in1=ang[:], op0=ALU.mult, op1=ALU.add),
                  reads=["angf", "r1"], writes=["frac"])
                A("act", lambda e: e.activation(out=sinT[:], in_=ang[:], func=AF.Sin, scale=0.5), reads=["frac"], writes=["sh"])
                A("act", lambda e: e.activation(out=cosT[:], in_=ang[:], func=AF.Sin, scale=-0.5, bias=halfpi[:, 0:1]), reads=["frac", "halfpi"], writes=["ch"])
                A("dve", lambda e: e.tensor_tensor(out=ang2[:], in0=sinT[:], in1=sinT[:], op=ALU.mult), reads=["sh", "angf", "frac"], writes=["s2"])
                A("dve", lambda e: e.scalar_tensor_tensor(out=sinT[:], in0=sinT[:], scalar=2.0, in1=cosT[:], op0=ALU.mult, op1=ALU.mult),
                  reads=["sh", "ch", "s2"], writes=["sinT"])
                A("dve", lambda e: e.tensor_scalar(out=cosT[:], in0=ang2[:], scalar1=-2.0, scalar2=1.0, op0=ALU.mult, op1=ALU.add),
                  reads=["s2", "sinT"], writes=["cosT"])

            units0, carry = make_inproj(0)
            pending_norm = []
            layer_norm_phase(0, units=units0, after_stats1=rope_tables)
            mem_stats_part(0, memtmp0, MK0, [])
            for k in range(NK):
                c, g = divmod(k, 3)
                S.label = "attn%d" % k
                its = make_attn(k)
                nxt, nxt_carry = make_inproj(k + 1) if k + 1 < NK else ([], None)
                n_it = len(its)
                if k == NK - 1:
                    wq_t, wq_k = wload(w0qm_d, 256)
                    qn = 0
                    for mc in range(2):
                        for qd in range(4):
                            def qm_unit(mc=mc, qd=qd, bk=(0, 1)[qn % 2]):
                                def ev_qm(qd_, bank, bkey):
                                    A("act", lambda e: e.activation(out=qmT0[:, mc, qd_ * 512:(qd_ + 1) * 512], in_=bank[:, :], func=AF.Copy),
                                      reads=list(bkey), writes=GKALL(sets[0], ("kq", "kk")[mc]))
                                proj_unit(wq_t, wq_k, mc * 128, qd, bk, ev_qm)
                            nxt.append(("t", qm_unit))
                            qn += 1
                    qm_done = True
                per = [[] for _ in range(n_it)]
                tiles_per = max(1, sum(1 for kd, _ in nxt if kd == "t") // n_it)
                cnt = 0
                slot = 0
                for (kind, u) in nxt:
                    per[min(slot, n_it - 1)].append(u)
                    if kind == "t":
                        cnt += 1
                        if cnt % tiles_per == 0:
                            slot += 1
                def make_norm(c_):
                    fns = []
                    for q_ in range(4):
                        def fn(q_=q_):
                            cs = slice(q_ * 512, (q_ + 1) * 512)
                            A("act", lambda e: e.activation(out=accd[:, cs], in_=accd[:, cs], func=AF.Ln), reads=[("accd", q_)], writes=[("accd", q_)])
                            A("act", lambda e: e.activation(out=accd[:, cs], in_=accd[:, cs], func=AF.Exp, scale=-1.0), reads=[("accd", q_)], writes=[("accd", q_)])
                            A("dve", lambda e: e.scalar_tensor_tensor(out=yT0[:, c_, cs], in0=accn[:, cs], scalar=0.5, in1=accd[:, cs], op0=ALU.mult, op1=ALU.mult),
                              reads=[("accn", q_), ("accd", q_)], writes=[("y", c_, q_)])
                        fns.append(fn)
                    return fns

                its[0][0]()
                for i in range(n_it):
                    if i + 1 < n_it:
                        its[i + 1][0]()
                    if i == 0 and carry is not None:
                        carry()
                    for u in per[i]:
                        u()
                    if g == 0 and pending_norm and i % 2 == 0:
                        pending_norm.pop(0)()
                    its[i][1]()
                carry = nxt_carry
                if k == 0:
                    S.label = "mem0"
                    mem_apply_part(0, memtmp0, MK0)
                    mem_kv_part(0)
                if g == 2:
                    pending_norm = make_norm(c)
                    if c == 3:
                        for fn in pending_norm:
                            fn()
                        pending_norm = []
            S.label = "qm0"
            ACCK = [("accn", q_) for q_ in range(4)] + [("accd", q_) for q_ in range(4)]
            wo0 = U[:, 18432:24576].rearrange("p (c n) -> p c n", c=6)
            assert qm_done
            if do_l1:
                memtmpX = X[:, 0:4096].bitcast(F32).rearrange("p (t d) -> p t d", t=2)
                mem_stats_part(1, memtmpX, GKALL(sets[1], "kq") + GKALL(sets[1], "kk"), [])
            S.label = "z0"
            zstate = {"zi": 0, "n": 0}
            zunits = []
            for zb in (1, 0):
                def lazy_w(zb=zb, cache={}):
                    if "w" not in cache:
                        cache["w"] = wload(w0z_d[zb], 384)
                    return cache["w"]
                for zc3 in ((1, 2, 0) if zb == 1 else (0, 1, 2)):
                    zc = zb * 3 + zc3
                    for qd in range(4):
                        def zunit(zc=zc, zc3=zc3, qd=qd, lazy_w=lazy_w):
                            wt, wk = lazy_w()

                            def ev_z(qd, bank, bkey):
                                szb = szt[zstate["zi"] % 2]
                                szk = ("sz", zstate["zi"] % 2)
                                zstate["zi"] += 1
                                qs_ = slice(qd * 512, (qd + 1) * 512)
                                A("act", lambda e: e.activation(out=szb[:], in_=bank[:, :], func=AF.Tanh, scale=0.5), reads=list(bkey), writes=[szk])
                                if zc >= 4:
                                    A("dve", lambda e: e.scalar_tensor_tensor(out=yT0[:, zc, qs_], in0=szb[:], scalar=1.0, in1=bank[:, :], op0=ALU.add, op1=ALU.mult),
                                      reads=list(bkey) + [szk], writes=[("y", zc, qd)])
                                    return
                                A("dve", lambda e: e.scalar_tensor_tensor(out=szb[:], in0=szb[:], scalar=1.0, in1=bank[:, :], op0=ALU.add, op1=ALU.mult),
                                  reads=list(bkey) + [szk], writes=[szk])
                                A("pool", lambda e: e.tensor_tensor(out=yT0[:, zc, qs_], in0=yT0[:, zc, qs_], in1=szb[:], op=ALU.mult),
                                  reads=[szk, ("y", zc, qd)], writes=[("y", zc, qd)])
                            bk = (0, 1)[zstate["n"] % 2]
                            zstate["n"] += 1
                            proj_unit(wt, wk, zc3 * 128, qd, bk, ev_z)
                        zunits.append(zunit)
            for u in zunits[:8]:
                u()
            load_wout(wo0, wout0_d, ACCK)
            rest = zunits[8:]
            rd0 = U[:, 16384:18432].bitcast(F32)
            mits = make_memattn(qmT0, [GKALL(sets[0], "kq"), GKALL(sets[0], "kk")], yT0, 4, rd0, "rdA", rd_first=GKALL(sets[0], "kv"))
            S.label = "memattn0"
            for i, (S_fn, PV_fn) in enumerate(mits):
                S_fn()
                for u in rest[2 * i:2 * i + 2]:
                    u()
                PV_fn()
            S.label = "outproj0"
            l1_deferred = []
            out_proj(yT0, 6, wo0, ACCK, after_batch=(layer_norm_cb(1, defer_last=l1_deferred) if do_l1 else None))
            if do_l1:
                S.label = "mem1n"
                mem_apply_part(1, memtmpX, GKALL(sets[1], "kq") + GKALL(sets[1], "kk"))
                S.label = "mem1kv"
                mem_kv_part(1)

        if do_l1:
            yT1 = U[:, 0:20480].rearrange("p (c t) -> p c t", c=10)
            cgs = U[:, 20480:21504]
            a_sb = U[:, 21504:23560].bitcast(F32)
            cv = U[:, 23560:25608].bitcast(F32)
            memtmp1 = U[:, 20480:24576].bitcast(F32).rearrange("p (t d) -> p t d", t=2)
            qmT1 = U[:, 20480:24576].rearrange("p (c t) -> p c t", c=2)
            rd1 = U[:, 24576:26624].bitcast(F32)
            L1K = ["cg", "a", "cv"]

            S.label = 'L1mem'
            if not do_l0:
                mem_norm_part(1, memtmp1, L1K, [])
                mem_kv_part(1)
            S.label = 'L1norm'
            if not do_l0:
                layer_norm_phase(1)
            wo1 = X[:, :].rearrange("p (c n) -> p c n", c=10)
            load_wout(wo1, wout1_d, ["X"] + ((GKALL(sets[1], "kq") + GKALL(sets[1], "kk")) if do_l0 else []), HK)

            zi = 0
            for j in range(8):
                S.label = 'L1conv%d' % j
                wt, wk = wload(w1b_d[j], 512)
                for half in range(2):
                    tk0 = half * 1024
                    hnk = HN[8 * half:8 * half + 8]

                    def proj_half(col0, evac, bank_ids, wt=wt, wk=wk, tk0=tk0, hnk=hnk):
                        for q2 in range(2):
                            bk = bank_ids[q2]
                            for kc in range(8):
                                A("pe", lambda e, kc=kc, q2=q2, bk=bk, wt=wt, tk0=tk0, col0=col0: e.matmul(banks[bk][:, :], lhsT=wt[:, kc, col0:col0 + 128],
                                                                                rhs=hnT[:, kc, tk0 + q2 * 512:tk0 + (q2 + 1) * 512], start=(kc == 0), stop=(kc == 7)),
                                  reads=[wk] + hnk, writes=KB(bk))
                            evac(q2, banks[bk], KB(bk))

                    if half == 0:
                        A("dve", lambda e: e.memset(a_sb[:, 0:2], 0.0), writes=["a"])
                    else:
                        A("dve", lambda e: e.tensor_copy(out=a_sb[:, 0:2], in_=a_sb[:, 1024:1026]), reads=["a", "cv"], writes=["a"])
                    proj_half(128, lambda q2, bank, bkey: A("act", lambda e: e.activation(out=cgs[:, q2 * 512:(q2 + 1) * 512], in_=bank[:, :], func=AF.Copy),
                                                            reads=list(bkey), writes=["cg"]), (0, 1))
                    proj_half(256, lambda q2, bank, bkey: A("dve", lambda e: e.tensor_tensor(out=a_sb[:, 2 + q2 * 512:2 + (q2 + 1) * 512], in0=bank[:, :],
                                                                                             in1=cgs[:, q2 * 512:(q2 + 1) * 512], op=ALU.mult),
                                                            reads=list(bkey) + ["cg"], writes=["a"]), (3, 4))
                    A("act", lambda e, j=j: e.activation(out=cv[:, :], in_=a_sb[:, 2:1026], func=AF.Copy, scale=cws[:, j, 2:3]), reads=["a", "consts"], writes=["cv"])
                    A("dve", lambda e, j=j: e.scalar_tensor_tensor(out=cv[:, :], in0=a_sb[:, 1:1025], scalar=cws[:, j, 1:2], in1=cv[:, :], op0=ALU.mult, op1=ALU.add),
                      reads=["a", "cv", "consts"], writes=["cv"])
                    A("dve", lambda e, j=j: e.scalar_tensor_tensor(out=cv[:, :], in0=a_sb[:, 0:1024], scalar=cws[:, j, 0:1], in1=cv[:, :], op0=ALU.mult, op1=ALU.add),
                      reads=["a", "cv", "consts"], writes=["cv"])
                    proj_half(0, lambda q2, bank, bkey: A("dve", lambda e: e.tensor_tensor(out=cv[:, q2 * 512:(q2 + 1) * 512], in0=bank[:, :],
                                                                                           in1=cv[:, q2 * 512:(q2 + 1) * 512], op=ALU.mult),
                                                          reads=list(bkey) + ["cv"], writes=["cv"]), (5, 6))

                    def ev_z1(q2, bank, bkey, j=j, tk0=tk0, half=half):
                        nonlocal zi
                        szb = szt[zi % 2]
                        szk = ("sz", zi % 2)
                        zi += 1
                        A("act", lambda e: e.activation(out=szb[:], in_=bank[:, :], func=AF.Silu), reads=list(bkey), writes=[szk])
                        A("pool", lambda e: e.tensor_tensor(out=yT1[:, j, tk0 + q2 * 512:tk0 + (q2 + 1) * 512], in0=cv[:, q2 * 512:(q2 + 1) * 512],
                                                            in1=szb[:], op=ALU.mult),
                          reads=[szk, "cv"], writes=[("y", j, half * 2 + q2)])
                    proj_half(384, ev_z1, (7, 2))
                    if j == 0 and half == 0 and do_l0:
                        for fn in l1_deferred:
                            fn()

            S.label = 'L1qm'
            wt, wk = wload(w1qm_d, 256)
            for mc in range(2):
                def ev_qm1(qd, bank, bkey, mc=mc):
                    A("act", lambda e: e.activation(out=qmT1[:, mc, qd * 512:(qd + 1) * 512], in_=bank[:, :], func=AF.Copy),
                      reads=list(bkey), writes=L1K)
                proj_fm(wt, wk, mc * 128, ev_qm1, (0, 1))
            S.label = 'L1memattn'
            wz2 = wload(w1z2_d, 256)
            mits1 = make_memattn(qmT1, [[L1K[0]], [L1K[0]]], yT1, 8, rd1, "rdB", rd_first=["cv"])
            for i, (S_fn, PV_fn) in enumerate(mits1):
                S_fn()
                zc, qd = divmod(i, 4)

                def ev_z2(qd, bank, bkey, zc=zc, i=i):
                    szb = szt[i % 2]
                    szk = ("sz", i % 2)
                    A("act", lambda e: e.activation(out=szb[:], in_=bank[:, :], func=AF.Tanh, scale=0.5), reads=list(bkey), writes=[szk])
                    A("dve", lambda e: e.scalar_tensor_tensor(out=yT1[:, 8 + zc, qd * 512:(qd + 1) * 512], in0=szb[:], scalar=1.0, in1=bank[:, :],
                                                              op0=ALU.add, op1=ALU.mult),
                      reads=list(bkey) + [szk], writes=[("y", 8 + zc, qd)])
                proj_unit(wz2[0], wz2[1], zc * 128, qd, (0, 1)[i % 2], ev_z2)
                PV_fn()
            S.label = 'L1outproj'
            ov = out_d.rearrange("(t p) d -> p t d", p=128)
            fgs = wr[0][:].rearrange("p a b -> p (a b)")[:, 0:2048].bitcast(F32)
            A("sp", lambda e: e.dma_start(out=fgs, in_=fg_d), writes=[("wr", 0)], chan="fg")
            ost = [U[:, 20480:22528].bitcast(F32), U[:, 22528:24576].bitcast(F32)]
            ykeys = [[("ost", 0)], [("ost", 1)]]

            FG = [(0, 4), (4, 8), (8, 12), (12, 14), (14, 15), (15, 16)]

            def final_apply(gi):
                t0_, t1_ = FG[gi]
                for t in range(t0_, t1_):
                    sidx = 20 + t
                    o = ost[t % 2]
                    A("dve", lambda e, t=t, o=o, sidx=sidx: e.scalar_tensor_tensor(out=o, in0=h[:, t, :], scalar=rstd[:, sidx:sidx + 1], in1=fgs,
                                                                                   op0=ALU.mult, op1=ALU.mult),
                      reads=[HK[t], ("rs", 20 + t0_), ("wr", 0)], writes=ykeys[t % 2] + (L1K if t < 2 else []))
                    A("sp", lambda e, t=t, o=o: e.dma_start(out=ov[:, t, :], in_=o), reads=ykeys[t % 2], chan="o%d" % (t % 2))

            def final_tile_cb(t):
                for gi, (t0_, t1_) in enumerate(FG):
                    if t == t1_ - 1:
                        rms_stats([(h[:, tt, :], [HK[tt]]) for tt in range(t0_, t1_)], 20 + t0_)
                        if gi > 0:
                            final_apply(gi - 1)
                        if gi == len(FG) - 1:
                            final_apply(gi)
            out_proj(yT1, 10, wo1, ["X"], after_tile=final_tile_cb)

        frozen["f"] = False
        if do_l1:
            fw = {"sp": [("o0", 8), ("o1", 8)]}
        else:
            ov = out_d.rearrange("(t p) d -> p t d", p=128)
            for i in range(4):
                A("sp", lambda e, i=i: e.dma_start(out=ov[:, 4 * i:4 * i + 4, :], in_=h[:, 4 * i:4 * i + 4, :]), reads=HK[4 * i:4 * i + 4], chan="o%d" % (i % 2))
            fw = {"sp": [("o0", 2), ("o1", 2)]}
        S.emit_all(block, sems, chans, fw)
        if os.environ.get('KLABELS'):
            import json
            json.dump(S.labels, open(os.environ['KLABELS'], 'w'))
    return nc


def _consts():
    ident = np.eye(128, dtype=np.float32)
    k = np.arange(128)[:, None]
    q = np.arange(128)[None, :]
    diag = (q >= k).astype(np.float32)
    prev = (q <= k).astype(np.float32)
    maskA = np.concatenate([prev, diag, prev, diag], axis=1)
    maskD = np.concatenate([diag, diag, diag, diag], axis=1)
    half = 8
    invf = (np.float32(ROPE_THETA) ** (-np.arange(half, dtype=np.float32) * np.float32(2.0 / 16))).astype(np.float32)
    invf = np.broadcast_to(invf[None, :], (128, half)).copy()
    return ident, maskA, maskD, invf


def _vec_layout(v):
    L = v.shape[0]
    return np.ascontiguousarray(v.reshape(L, 8, 128).transpose(2, 0, 1))


def _prep_shared(norm_g, mem_norm_g, w_mem_kv, attn_w_in, attn_w_out, conv_w_in, conv_w, conv_w_out, final_g):
    ident, maskA, maskD, invf = _consts()
    d = {"ident": ident, "maskA": maskA, "maskD": maskD}
    ng = _vec_layout(np.asarray(norm_g, np.float32)).reshape(128, 16)
    mg = _vec_layout(np.asarray(mem_norm_g, np.float32)).reshape(128, 16)
    d["wkv"] = np.ascontiguousarray(w_mem_kv, dtype=np.float32)
    w0 = np.asarray(attn_w_in[0], np.float32)
    blocks = []
    for c in range(4):
        for g in range(3):
            o = g * 512 + c * 128
            blocks.append(np.concatenate([w0[:, o:o + 128], w0[:, 1536 + o:1536 + o + 128], w0[:, 3072 + o:3072 + o + 128]], axis=1))
    d["w0a"] = np.ascontiguousarray(np.stack(blocks))
    d["w0qm"] = np.ascontiguousarray(w0[:, 4608:4864])
    d["w0z"] = np.ascontiguousarray(np.stack([w0[:, 4864:4864 + 384], w0[:, 4864 + 384:5632]]))
    d["wout0"] = np.ascontiguousarray(attn_w_out[0], dtype=np.float32)
    w1 = np.asarray(conv_w_in[0], np.float32)
    b1 = []
    for j in range(8):
        o = j * 128
        b1.append(np.concatenate([w1[:, o:o + 128], w1[:, 1024 + o:1024 + o + 128], w1[:, 2048 + o:2048 + o + 128],
                                  w1[:, 3328 + o:3328 + o + 128]], axis=1))
    d["w1b"] = np.ascontiguousarray(np.stack(b1))
    d["w1qm"] = np.ascontiguousarray(w1[:, 3072:3328])
    d["w1z2"] = np.ascontiguousarray(w1[:, 3328 + 1024:3328 + 1280])
    d["wout1"] = np.ascontiguousarray(conv_w_out[0], dtype=np.float32)
    cw = np.asarray(conv_w[0], np.float32)
    cwl = np.ascontiguousarray(cw.reshape(3, 8, 128).transpose(2, 1, 0)).reshape(128, 24)
    d["_cpk_head"] = np.ascontiguousarray(np.concatenate([ng, mg, invf, cwl], axis=1))
    d["fg"] = np.ascontiguousarray(np.broadcast_to(np.asarray(final_g, np.float32)[None, :], (128, DM)))
    return d


def _pos_layout(pos_row):
    out = np.empty((128, 48), np.int32)
    p = np.arange(128)
    for g in range(3):
        for b in range(16):
            out[:, g * 16 + b] = pos_row[tok_start(g, b) + p * DIL[g]]
    return out


L0_KEYS = ["x", "mem", "ident", "cpk", "wkv", "maskA", "maskD", "w0a", "w0z", "w0qm", "wout0"]
L1_KEYS = ["x", "mem", "ident", "cpk", "wkv", "fg", "w1b", "w1z2", "w1qm", "wout1"]
FULL_KEYS = L0_KEYS + [k for k in L1_KEYS if k not in L0_KEYS]

_CACHE = {}


def _get_nc(mode):
    if mode not in _CACHE:
        st = os.environ.get("KSTOP")
        _CACHE[mode] = build(mode, stop=int(st) if st else None)
    return _CACHE[mode]


def kernel(x, mem, positions, norm_g, mem_norm_g, w_mem_kv, attn_w_in, attn_w_out,
           conv_w_in, conv_w, conv_w_out, final_g, _mode="full"):
    x = np.asarray(x, np.float32)
    mem = np.asarray(mem, np.float32)
    positions = np.asarray(positions, np.int32)
    B = x.shape[0]
    shared = _prep_shared(norm_g, mem_norm_g, w_mem_kv, attn_w_in, attn_w_out, conv_w_in, conv_w, conv_w_out, final_g)

    def run(mode, xs, keys):
        nc = _get_nc(mode)
        in_maps = []
        for b in range(B):
            m = dict(shared)
            m["x"] = np.ascontiguousarray(xs[b])
            m["mem"] = np.ascontiguousarray(mem[b])
            m["cpk"] = np.ascontiguousarray(np.concatenate([shared["_cpk_head"], _pos_layout(positions[b]).view(np.float32)], axis=1))
            in_maps.append({k: m[k] for k in keys})
        res = run_bass_kernel_spmd(nc, in_maps, core_ids=list(range(B)))
        return np.stack([r["out"] for r in res.results], axis=0)

    if _mode == "full":
        return run("full", x, FULL_KEYS)
    if _mode == "l0":
        return run("l0", x, L0_KEYS)
    if _mode == "unfused":
        h1 = run("l0", x, L0_KEYS)
        return run("l1", h1, L1_KEYS)
    raise ValueError(_mode)
```
